# Optimizing a Trainium2 kernel written in Bass

```python
import jax, jax.numpy as jnp
from jax import lax
import numpy as np

D_MODEL = 1024
BATCH = 4
SEQ = 8192
DEPTH = 1
DEC_BATCH = 32
DEC_SEQ = 64
PAST_LEN = 2048

CHUNK = 64
D_PLE = 256
D_A = 1024
CONV_W = 3
H_B = 8
K_B = 128
V_B = 128
D_B = H_B * V_B
SPLIT_SIZES = (D_A, D_A, D_A, D_A, H_B * K_B, H_B * K_B, D_B, D_B, D_MODEL, D_MODEL)
N_IN = 4 * D_A + 2 * H_B * K_B + 2 * D_B + 2 * D_MODEL
EPS = 1e-6
QK_SCALE = K_B ** -0.5

kernel_name = 'hybrid_shortconv_hgrn2_stream_step'


def rms_norm(x, g):
    xf = x.astype(jnp.float32)
    y = xf * lax.rsqrt(jnp.mean(xf * xf, axis=-1, keepdims=True) + EPS) * g.astype(jnp.float32)
    return y.astype(x.dtype)


def split_cols(z):
    out, off = [], 0
    for s in SPLIT_SIZES:
        out.append(z[..., off:off + s])
        off += s
    return out


def causal_conv3(u_ext, w):
    T = u_ext.shape[1] - (CONV_W - 1)
    out = w[0] * u_ext[:, 0:T]
    for j in range(1, CONV_W):
        out = out + w[j] * u_ext[:, j:j + T]
    return out


def hgrn2_block(S, q, k, v, logf):
    L = q.shape[1]
    b = jnp.cumsum(logf, axis=1)
    ref = b[:, (L - 1) // 2][:, None]
    qe = q * jnp.exp(b - ref)
    ke = k * jnp.exp(ref - b)
    causal = jnp.tril(jnp.ones((L, L), dtype=bool))
    att = jnp.where(causal, jnp.einsum('bthk,bshk->bhts', qe, ke), 0.0)
    o = jnp.einsum('bhts,bshv->bthv', att, v) + jnp.einsum('bthk,bhkv->bthv', q * jnp.exp(b), S)
    b_last = b[:, -1]
    S_new = jnp.exp(b_last)[..., None] * S + jnp.einsum('bshk,bshv->bhkv', k * jnp.exp(b_last[:, None] - b), v)
    return S_new, o


def hgrn2_scan(S0, q, k, v, logf):
    B, T = q.shape[0], q.shape[1]
    if T <= CHUNK:
        return hgrn2_block(S0, q, k, v, logf)
    n = T // CHUNK

    def to_chunks(a):
        return jnp.moveaxis(a.reshape((B, n, CHUNK) + a.shape[2:]), 1, 0)

    def step(S, xs):
        return hgrn2_block(S, xs[0], xs[1], xs[2], xs[3])

    S_fin, o = lax.scan(step, S0, (to_chunks(q), to_chunks(k), to_chunks(v), to_chunks(logf)))
    o = jnp.moveaxis(o, 0, 1).reshape(B, T, H_B, V_B)
    return S_fin, o


def encoder_layer(x, p, conv_hist, S0, lb, w_in, conv_w, g_pre, g_onorm, w_a_out, w_b_out, w_o,
                  g_post, g_ple, w_ple_gate, w_ple_proj):
    B, T = x.shape[0], x.shape[1]
    h = rms_norm(x, g_pre)
    proj = jnp.einsum('btd,dn->btn', h, w_in)
    vA, bA, cA, zA, q, f, iv, zB, gA, gB = split_cols(proj)
    u = cA * vA
    u_ext = jnp.concatenate([conv_hist.astype(u.dtype), u], axis=1)
    conv = causal_conv3(u_ext, conv_w)
    yA = jnp.einsum('btc,cd->btd', jax.nn.silu(zA) * bA * conv, w_a_out)
    new_conv = u_ext[:, -(CONV_W - 1):]
    fg = lb + (1.0 - lb) * jax.nn.sigmoid(f.astype(jnp.float32))
    logf = jnp.log(fg).reshape(B, T, H_B, K_B)
    kk = (1.0 - fg).reshape(B, T, H_B, K_B)
    qq = (jax.nn.silu(q.astype(jnp.float32)) * QK_SCALE).reshape(B, T, H_B, K_B)
    vv = iv.astype(jnp.float32).reshape(B, T, H_B, V_B)
    S_new, o = hgrn2_scan(S0.astype(jnp.float32), qq, kk, vv, logf)
    o = o * lax.rsqrt(jnp.mean(o * o, axis=-1, keepdims=True) + EPS)
    o = (o.reshape(B, T, D_B) * g_onorm.astype(jnp.float32)).astype(x.dtype)
    yB = jnp.einsum('btc,cd->btd', o * jax.nn.silu(zB), w_b_out)
    merged = jax.nn.sigmoid(gA) * yA + jax.nn.sigmoid(gB) * yB
    out = jnp.einsum('btd,de->bte', merged, w_o)
    x = x + rms_norm(out, g_post)
    gate = jax.nn.sigmoid(jnp.einsum('btd,de->bte', rms_norm(x, g_ple), w_ple_gate))
    x = x + jnp.einsum('btp,pd->btd', p.astype(x.dtype), w_ple_proj) * gate
    return x, new_conv, S_new


def setup_inputs(seed: int = 0) -> dict:
    key = jax.random.key(seed)
    ks = jax.random.split(key, 20)
    nrm = jax.random.normal
    f32 = jnp.float32
    return {
        'x_prompt': nrm(ks[0], (BATCH, SEQ, D_MODEL), f32),
        'x_sample': nrm(ks[1], (DEC_BATCH, DEC_SEQ, D_MODEL), f32),
        'state_conv': nrm(ks[2], (DEPTH, DEC_BATCH, CONV_W - 1, D_A), f32),
        'state_hgrn': 0.5 * nrm(ks[3], (DEPTH, DEC_BATCH, H_B, K_B, V_B), f32),
        'p_prompt': nrm(ks[4], (DEPTH, BATCH, SEQ, D_PLE), f32),
        'p_sample': nrm(ks[5], (DEPTH, DEC_BATCH, DEC_SEQ, D_PLE), f32),
        'w_in': nrm(ks[6], (DEPTH, D_MODEL, N_IN), f32) * D_MODEL ** -0.5,
        'conv_w': nrm(ks[7], (DEPTH, CONV_W, D_A), f32) * CONV_W ** -0.5,
        'lb_raw': 0.1 * nrm(ks[8], (DEPTH + 1, H_B * K_B), f32),
        'g_pre': 1.0 + 0.05 * nrm(ks[9], (DEPTH, D_MODEL), f32),
        'g_onorm': 1.0 + 0.05 * nrm(ks[10], (DEPTH, D_B), f32),
        'w_a_out': nrm(ks[11], (DEPTH, D_A, D_MODEL), f32) * D_A ** -0.5,
        'w_b_out': nrm(ks[12], (DEPTH, D_B, D_MODEL), f32) * D_B ** -0.5,
        'w_o': nrm(ks[13], (DEPTH, D_MODEL, D_MODEL), f32) * D_MODEL ** -0.5,
        'g_post': 1.0 + 0.05 * nrm(ks[14], (DEPTH, D_MODEL), f32),
        'g_ple': 1.0 + 0.05 * nrm(ks[15], (DEPTH, D_MODEL), f32),
        'w_ple_gate': nrm(ks[16], (DEPTH, D_MODEL, D_MODEL), f32) * D_MODEL ** -0.5,
        'w_ple_proj': nrm(ks[17], (DEPTH, D_PLE, D_MODEL), f32) * D_PLE ** -0.5,
    }


def reference(x_prompt, x_sample, state_conv, state_hgrn, p_prompt, p_sample, w_in, conv_w, lb_raw,
              g_pre, g_onorm, w_a_out, w_b_out, w_o, g_post, g_ple, w_ple_gate, w_ple_proj):
    lbs = jnp.cumsum(jax.nn.softmax(lb_raw.astype(jnp.float32), axis=0), axis=0)
    xp, xs = x_prompt, x_sample
    conv_p, hgrn_p, conv_s, hgrn_s = [], [], [], []
    for i in range(DEPTH):
        layer_w = (w_in[i], conv_w[i], g_pre[i], g_onorm[i], w_a_out[i], w_b_out[i], w_o[i],
                   g_post[i], g_ple[i], w_ple_gate[i], w_ple_proj[i])
        zero_conv = jnp.zeros((xp.shape[0], CONV_W - 1, D_A), xp.dtype)
        zero_S = jnp.zeros((xp.shape[0], H_B, K_B, V_B), jnp.float32)
        xp, cp, sp = encoder_layer(xp, p_prompt[i], zero_conv, zero_S, lbs[i], *layer_w)
        xs, cs, ss = encoder_layer(xs, p_sample[i], state_conv[i], state_hgrn[i], lbs[i], *layer_w)
        conv_p.append(cp)
        hgrn_p.append(sp)
        conv_s.append(cs)
        hgrn_s.append(ss)
    new_conv_prompt = jnp.stack(conv_p)
    new_hgrn_prompt = jnp.stack(hgrn_p)
    new_conv_sample = jnp.stack(conv_s)
    new_hgrn_sample = jnp.stack(hgrn_s)
    return (xp, xs, new_conv_prompt, new_hgrn_prompt, new_conv_sample, new_hgrn_sample)
```

```python
import numpy as np
from contextlib import ExitStack
import concourse.bass as bass
import concourse.mybir as mybir
from concourse.bass_utils import run_bass_kernel_spmd

F32, BF16 = mybir.dt.float32, mybir.dt.bfloat16
AF = mybir.ActivationFunctionType
ALU = mybir.AluOpType

D = 1024
DPLE = 256
NIN = 10240
H = 8
EPS = 1e-6
QK_SCALE = 128 ** -0.5
SEQ = 8192
NSEQ_S = 4
R_SLOTS = 6
NTMP = 12
N_CORES = 8

OFF = dict(vA=0, bA=1024, cA=2048, zA=3072, q=4096, f=5120, iv=6144, zB=7168, gA=8192, gB=9216)


class Buf:
    __slots__ = ("name", "w", "r")

    def __init__(self, name):
        self.name = name
        self.w = None
        self.r = {}


class Chan:
    def __init__(self, sem, key):
        self.sem = sem
        self.key = key
        self.val = 0


class Eng:
    def __init__(self, eng, sem, key, is_pe=False):
        self.eng = eng
        self.sem = sem
        self.key = key
        self.cnt = 0
        self.seen = {}
        self.is_pe = is_pe

    def wait_for(self, deps):
        need = {}
        for d in deps:
            if d is None:
                continue
            key, sem, val = d
            if self.is_pe and key == self.key:
                continue
            if val > self.seen.get(key, 0):
                if key not in need or need[key][1] < val:
                    need[key] = (sem, val)
        for key, (sem, val) in need.items():
            self.eng.wait_ge(sem, val)
            self.seen[key] = val


class T:
    def __init__(self, ap, buf):
        self.ap = ap
        self.buf = buf


class K:
    def __init__(self, nc, es):
        self.nc = nc
        self.es = es
        self.nsem = 0
        self.PE = Eng(nc.tensor, self.sem("pe"), "pe", True)
        self.ACT = Eng(nc.scalar, self.sem("act"), "act")
        self.DVE = Eng(nc.vector, self.sem("dve"), "dve")
        self.POOL = Eng(nc.gpsimd, self.sem("pool"), "pool")
        self.SP = Eng(nc.sync, None, "sp")
        self.tmp_i = 0
        self.bank_i = 0

    def sem(self, name):
        self.nsem += 1
        return self.es.enter_context(self.nc.semaphore(name))

    def chan(self, name):
        return Chan(self.sem(name), name)

    def sb(self, name, shape, dt):
        t = self.es.enter_context(self.nc.sbuf_tensor(name, shape, dt))
        return T(t[:], Buf(name))

    def ps(self, name, shape, dt):
        t = self.es.enter_context(self.nc.psum_tensor(name, shape, dt))
        return T(t[:], Buf(name))

    def _deps(self, reads, writes):
        deps = []
        for b in reads:
            deps.append(b.w)
        for b in writes:
            deps.append(b.w)
            deps.extend(b.r.values())
        return deps

    def _mark(self, stamp, reads, writes):
        for b in writes:
            b.w = stamp
            b.r = {}
        for b in reads:
            old = b.r.get(stamp[0])
            if old is None or old[2] < stamp[2]:
                b.r[stamp[0]] = stamp

    def op(self, E, fn, reads=(), writes=()):
        E.wait_for(self._deps(reads, writes))
        ins = fn(E.eng)
        E.cnt += 1
        ins.then_inc(E.sem, 1)
        self._mark((E.key, E.sem, E.cnt), reads, writes)

    def pe_group(self, fns, reads=(), writes=()):
        E = self.PE
        E.wait_for(self._deps(reads, writes))
        ins = None
        for fn in fns:
            ins = fn(E.eng)
        E.cnt += 1
        ins.then_inc(E.sem, 1)
        self._mark((E.key, E.sem, E.cnt), reads, writes)

    def dma(self, Q, fn, chan, reads=(), writes=()):
        Q.wait_for(self._deps(reads, writes))
        ins = fn(Q.eng)
        chan.val += 16
        ins.then_inc(chan.sem, 16)
        self._mark((chan.key, chan.sem, chan.val), reads, writes)


class StopBuild(Exception):
    pass


import os
_STOP = int(os.environ.get("KSTOP", "-1"))
_VAR = os.environ.get("KVAR", "")


_stage_ctr = [0]


def stage(n):
    c = _stage_ctr[0]
    _stage_ctr[0] += 1
    if _STOP == c:
        raise StopBuild()


def build(NPT, EXCH=True):
    _stage_ctr[0] = 0
    nc = bass.Bass("TRN2", target_bir_lowering=False)
    es = ExitStack()
    k = K(nc, es)
    PE, ACT, DVE, POOL, SP = k.PE, k.ACT, k.DVE, k.POOL, k.SP
    NTOK = NPT * 512

    def din(name, shape):
        return nc.dram_tensor(name, shape, F32, kind="ExternalInput").ap()

    def dout(name, shape):
        return nc.dram_tensor(name, shape, F32, kind="ExternalOutput").ap()

    xp = din("xp", [NTOK, D])
    pp = din("pp", [NTOK, DPLE])
    xs = din("xs", [NSEQ_S * 64, D])
    ps_ = din("ps", [NSEQ_S * 64, DPLE])
    sconv = din("sconv", [NSEQ_S * 2, D])
    shgrn = din("shgrn", [NSEQ_S, H, 128, 128])
    w_in = din("w_in", [D, NIN])
    conv_w = din("conv_w", [3, D])
    lb_raw = din("lb_raw", [2, D])
    g_pre = din("g_pre", [D])
    g_onorm = din("g_onorm", [D])
    w_a_out = din("w_a_out", [D, D])
    w_b_out = din("w_b_out", [D, D])
    w_o = din("w_o", [D, D])
    g_post = din("g_post", [D])
    g_ple = din("g_ple", [D])
    w_ple_gate = din("w_ple_gate", [D, D])
    w_ple_proj = din("w_ple_proj", [DPLE, D])

    xhalo = din("xhalo", [128, D])
    selm = din("selm", [128, 1])
    yp = dout("yp", [NTOK, D])
    ys = dout("ys", [NSEQ_S * 64, D])
    conv_p = dout("conv_p", [2, D])
    hgrn_p = dout("hgrn_p", [H, 128, 128])
    conv_s = dout("conv_s", [NSEQ_S * 2, D])
    hgrn_s = dout("hgrn_s", [NSEQ_S, H, 128, 128])

    def win(sec, half):
        return ("w_in", w_in, OFF[sec] * 1 + half * 512)

    stream = []
    stream += [("q%d" % i, w_in, OFF["q"] + i * 512) for i in range(2)]
    stream += [("f%d" % i, w_in, OFF["f"] + i * 512) for i in range(2)]
    stream += [("iv%d" % i, w_in, OFF["iv"] + i * 512) for i in range(2)]
    for s in range(2):
        for sec in ("vA", "cA", "zA", "bA"):
            stream.append(("%s%d" % (sec, s), w_in, OFF[sec] + s * 512))
    stream += [("zB%d" % i, w_in, OFF["zB"] + i * 512) for i in range(2)]
    for s in range(2):
        stream.append(("gA%d" % s, w_in, OFF["gA"] + s * 512))
        stream.append(("wao%d" % s, w_a_out, s * 512))
        stream.append(("gB%d" % s, w_in, OFF["gB"] + s * 512))
        stream.append(("wbo%d" % s, w_b_out, s * 512))
    stream += [("wo%d" % i, w_o, i * 512) for i in range(2)]
    stream += [("wg%d" % i, w_ple_gate, i * 512) for i in range(2)]
    stream.append(("ple", w_ple_proj, 0))
    NG = len(stream)
    wsc = nc.dram_tensor("wsc", [NG, 128, 4096], BF16, kind="Internal").ap()
    wsc_buf = [Buf("wsc%d" % g) for g in range(NG)]

    ident_bf = k.sb("ident_bf", [128, 128], BF16)
    identf = k.sb("identf", [128, 128], F32)
    ones_bf = k.sb("ones_bf", [128, 128], BF16)
    mask2 = k.sb("mask2", [128, 128], F32)
    rmask = k.sb("rmask", [128, 512], F32)
    gpre32 = k.sb("gpre32", [128, 8], F32)
    gon = k.sb("gon", [128, 8], F32)
    gple32 = k.sb("gple32", [128, 8], F32)
    lbt = k.sb("lbt", [128, 8], F32)
    omlt = k.sb("omlt", [128, 8], F32)
    nomlt = k.sb("nomlt", [128, 8], F32)
    lbr = k.sb("lbr", [128, 16], F32)
    cw = k.sb("cw", [128, 24], F32)
    gpost32 = k.sb("gpost32", [128, 1024], F32)
    dcy = k.sb("dcy", [128, 64], F32)
    uh = k.sb("uh", [128, 16], F32)
    uhs = k.sb("uhs", [128, 64], F32)
    uo = k.sb("uo", [128, 64], F32)
    smalls = [k.sb("small%d" % i, [128, 8], F32) for i in range(8)]
    sqb = k.sb("sqb", [128, 1024], BF16)
    junk = sqb

    wring = [k.sb("wring%d" % i, [128, 4096], BF16) for i in range(R_SLOTS)]
    wchan = [k.chan("wch%d" % i) for i in range(R_SLOTS)]
    hT = k.sb("hT", [128, 4096], BF16)
    bigA = k.sb("bigA", [128, 4096], BF16)
    bigB = k.sb("bigB", [128, 4096], BF16)
    bigC = k.sb("bigC", [128, 4096], BF16)
    bigD = k.sb("bigD", [128, 4096], BF16)
    bigE = k.sb("bigE", [128, 4096], BF16)
    kv = k.es.enter_context(nc.sbuf_tensor("kv", [128, 8192], BF16))
    kd_tm = T(kv[:, 0:4096], Buf("kd_tm"))
    v_tm = T(kv[:, 4096:8192], Buf("v_tm"))
    sigf_ap = kv[:].bitcast(F32)
    oT = k.sb("oT", [128, 4096], F32)
    tmps = [k.sb("tmp%d" % i, [128, 520], F32) for i in range(NTMP)]
    xin = [k.sb("xin%d" % i, [128, 1024], F32) for i in range(2)]
    xin_ch = [k.chan("xin_ch%d" % i) for i in range(2)]
    sc_in = T(xin[0].ap[0:8, :], xin[0].buf)
    cv_out = T(xin[1].ap[0:8, :], xin[1].buf)
    hn = [k.sb("hn%d" % i, [128, 1024], BF16) for i in range(4)]
    Sf = [k.sb("Sf%d" % i, [128, 1024], F32) for i in range(2)]
    Sb = [k.sb("Sb%d" % i, [128, 1024], BF16) for i in range(2)]
    s_ld_ch = [k.chan("sld%d" % i) for i in range(2)]
    s_st_ch = [k.chan("sst%d" % i) for i in range(2)]
    xr = xin
    xr_ch = xin_ch
    n2 = [k.sb("n2_%d" % i, [128, 1024], BF16) for i in range(2)]
    n2T = [k.sb("n2T%d" % i, [128, 1024], BF16) for i in range(2)]
    pt = [k.sb("pt%d" % i, [128, 256], F32) for i in range(2)]
    pt_ch = [k.chan("pt_ch%d" % i) for i in range(2)]
    ptb = [k.sb("ptb%d" % i, [128, 256], BF16) for i in range(2)]
    pT = [k.sb("pT%d" % i, [128, 256], BF16) for i in range(2)]
    y_ch = [k.chan("y_ch%d" % i) for i in range(4)]
    cst = k.chan("cst")
    misc_ch = k.chan("misc")
    conv_ch = [k.chan("cv%d" % g) for g in range(NG)]

    banks = [k.ps("bank%d" % i, [128, 512], F32) for i in range(8)]
    PO = banks[6:8]
    GB = banks[0:6]

    def bank():
        b = GB[k.bank_i % len(GB)]
        k.bank_i += 1
        return b

    def tmp():
        t = tmps[k.tmp_i % NTMP]
        k.tmp_i += 1
        return t

    small_i = [0]

    def small():
        t = smalls[small_i[0] % len(smalls)]
        small_i[0] += 1
        return t

    ws = {"issued": 0, "consumed": 0}
    gid = {nm: g for g, (nm, _, _) in enumerate(stream)}
    full_list = [nm for (nm, _, _) in stream]
    seq = []
    if EXCH:
        seq += ["f0", "f1", "iv0", "iv1"]
        seq += ["vA0", "cA0", "vA1", "cA1"]
    for _t in range(NPT + 1):
        pre_f = (_t >= 1) and (_t < NPT)
        nxt_pre = (_t + 1 < NPT)
        for nm in full_list:
            if nm in ("f0", "f1") and pre_f:
                continue
            seq.append(nm)
            if nm == "wo1" and nxt_pre:
                seq += ["f0", "f1"]

    def wissue_upto(limit):
        while ws["issued"] < min(limit, len(seq)):
            i = ws["issued"]
            g = gid[seq[i]]
            slot = i % R_SLOTS
            name = stream[g][0]
            if name == "ple":
                k.dma(SP, lambda e, slot=slot, g=g: e.dma_start(out=wring[slot].ap[:, 0:2048], in_=wsc[g][:, 0:2048]),
                      wchan[slot], reads=(wsc_buf[g],), writes=(wring[slot].buf,))
            else:
                k.dma(SP, lambda e, slot=slot, g=g: e.dma_start(out=wring[slot].ap, in_=wsc[g]),
                      wchan[slot], reads=(wsc_buf[g],), writes=(wring[slot].buf,))
            ws["issued"] += 1

    def wfetch(expect):
        i = ws["consumed"]
        assert seq[i] == expect, (seq[i], expect)
        ws["consumed"] += 1
        wissue_upto(i + R_SLOTS - 3)
        slot = wring[i % R_SLOTS]
        if expect == "ple":
            v = slot.ap[:, 0:2048].rearrange("p (kc c) -> p kc c", c=1024)
        else:
            v = slot.ap.rearrange("p (kc c) -> p kc c", c=512)
        return T(v, slot.buf)

    def fm(vec):
        return vec.rearrange("(c p) -> p c", p=128)

    raw_g = k.sb("raw_g", [128, 40], F32)
    cdmas = []

    def cdma(out_ap, in_ap):
        ins = nc.sync.dma_start(out=out_ap, in_=in_ap, allow_slow_non_contiguous=True)
        cst.val += 16
        ins.then_inc(cst.sem, 16)

    cdma(raw_g.ap[:, 0:8], fm(g_pre))
    cdma(raw_g.ap[:, 8:16], fm(g_onorm))
    cdma(raw_g.ap[:, 16:24], fm(g_ple))
    cdma(lbr.ap[:, 0:8], fm(lb_raw[0]))
    cdma(lbr.ap[:, 8:16], fm(lb_raw[1]))
    for t_ in range(3):
        cdma(cw.ap.rearrange("p (j t) -> p j t", t=3)[:, :, t_], fm(conv_w[t_]))
    cdma(gpost32.ap, g_post.partition_broadcast(128))
    cstamp = (cst.key, cst.sem, cst.val)
    for t_ in (raw_g, lbr, cw, gpost32):
        t_.buf.w = cstamp

    k.op(DVE, lambda e: e.memset(identf.ap, 1.0), writes=(identf.buf,))
    k.op(POOL, lambda e: e.affine_select(out=identf.ap, in_=identf.ap, pattern=[[-1, 128]],
                                         compare_op=ALU.is_equal, fill=0.0, base=0, channel_multiplier=1),
         reads=(identf.buf,), writes=(identf.buf,))
    k.op(DVE, lambda e: e.tensor_copy(out=ident_bf.ap, in_=identf.ap), reads=(identf.buf,), writes=(ident_bf.buf,))
    k.op(DVE, lambda e: e.memset(ones_bf.ap, 1.0), writes=(ones_bf.buf,))
    k.op(DVE, lambda e: e.memset(mask2.ap, 1.0), writes=(mask2.buf,))
    k.op(POOL, lambda e: e.affine_select(out=mask2.ap, in_=mask2.ap, pattern=[[1, 128]],
                                         compare_op=ALU.is_ge, fill=0.0, base=0, channel_multiplier=-1),
         reads=(mask2.buf,), writes=(mask2.buf,))
    k.op(DVE, lambda e: e.memset(mask2.ap[0:64, 64:128], 0.0), reads=(mask2.buf,), writes=(mask2.buf,))
    k.op(DVE, lambda e: e.memset(rmask.ap, 1.0), writes=(rmask.buf,))
    k.op(DVE, lambda e: e.memset(rmask.ap.rearrange("p (c t) -> p c t", t=64)[:, :, 0:1], 0.0),
         reads=(rmask.buf,), writes=(rmask.buf,))
    k.op(DVE, lambda e: e.tensor_scalar(out=gpre32.ap, in0=raw_g.ap[:, 0:8], scalar1=32.0, scalar2=None, op0=ALU.mult),
         reads=(raw_g.buf,), writes=(gpre32.buf,))
    k.op(DVE, lambda e: e.tensor_scalar(out=gon.ap, in0=raw_g.ap[:, 8:16], scalar1=float(128 ** 0.5), scalar2=None,
                                         op0=ALU.mult), reads=(raw_g.buf,), writes=(gon.buf,))
    k.op(DVE, lambda e: e.tensor_scalar(out=gple32.ap, in0=raw_g.ap[:, 16:24], scalar1=32.0, scalar2=None,
                                         op0=ALU.mult), reads=(raw_g.buf,), writes=(gple32.buf,))
    k.op(DVE, lambda e: e.tensor_scalar(out=gpost32.ap, in0=gpost32.ap, scalar1=32.0, scalar2=None, op0=ALU.mult),
         reads=(gpost32.buf,), writes=(gpost32.buf,))
    k.op(DVE, lambda e: e.tensor_tensor(out=lbr.ap[:, 0:8], in0=lbr.ap[:, 0:8], in1=lbr.ap[:, 8:16], op=ALU.subtract),
         reads=(lbr.buf,), writes=(lbr.buf,))
    k.op(ACT, lambda e: e.activation(out=lbt.ap, in_=lbr.ap[:, 0:8], func=AF.Sigmoid), reads=(lbr.buf,), writes=(lbt.buf,))
    k.op(DVE, lambda e: e.tensor_scalar(out=omlt.ap, in0=lbt.ap, scalar1=-1.0, scalar2=1.0, op0=ALU.mult, op1=ALU.add),
         reads=(lbt.buf,), writes=(omlt.buf,))
    k.op(DVE, lambda e: e.tensor_scalar(out=nomlt.ap, in0=omlt.ap, scalar1=-1.0, scalar2=None, op0=ALU.mult),
         reads=(omlt.buf,), writes=(nomlt.buf,))
    k.op(DVE, lambda e: e.memset(uh.ap, 0.0), writes=(uh.buf,))
    k.op(DVE, lambda e: e.memset(Sf[0].ap, 0.0), writes=(Sf[0].buf,))
    k.op(DVE, lambda e: e.memset(Sb[0].ap, 0.0), writes=(Sb[0].buf,))

    conv_order = [g for g, (nm, _, _) in enumerate(stream) if nm[:1] == "f" or nm[:2] == "iv"]
    conv_order += [g for g, (nm, _, _) in enumerate(stream) if nm[:2] in ("vA", "cA")]
    conv_order += [g for g in range(len(stream)) if g not in conv_order]
    conv_state = {"i": 0}

    def convert_next(n, gate_bufs=()):
        for _ in range(n):
            if conv_state["i"] >= len(conv_order):
                return
            g = conv_order[conv_state["i"]]
            conv_state["i"] += 1
            name, W, c0 = stream[g]
            if name == "ple":
                src = W.rearrange("(kc p) c -> p kc c", p=128)
                dst = wsc[g][:, 0:2048].rearrange("p (kc c) -> p kc c", c=1024)
            else:
                src = W[:, c0:c0 + 512].rearrange("(kc p) c -> p kc c", p=128)
                dst = wsc[g].rearrange("p (kc c) -> p kc c", c=512)
            k.dma(POOL, lambda e, dst=dst, src=src: e.dma_start(out=dst, in_=src), conv_ch[g],
                  reads=tuple(gate_bufs), writes=(wsc_buf[g],))

    convert_next(len(conv_order))

    def rstd_from_ssq(ssq_t, ncols, n_eps):
        l_ = small()
        r_ = small()
        k.op(ACT, lambda e: e.activation(out=l_.ap[:, 0:ncols], in_=ssq_t.ap[:, 0:ncols], func=AF.Ln, bias=float(n_eps), scale=1.0),
             reads=(ssq_t.buf,), writes=(l_.buf,))
        k.op(ACT, lambda e: e.activation(out=r_.ap[:, 0:ncols], in_=l_.ap[:, 0:ncols], func=AF.Exp, scale=-0.5),
             reads=(l_.buf,), writes=(r_.buf,))
        return r_

    def proj_fm(slot, cc, NT, rhs_t, rhs_view):
        b = bank()
        k.pe_group([(lambda e, kc=kc: e.matmul(b.ap[:, 0:NT], lhsT=slot.ap[:, kc, cc * 128:(cc + 1) * 128],
                                               rhs=rhs_view[:, kc, :], start=(kc == 0), stop=(kc == 7)))
                    for kc in range(8)],
                   reads=(slot.buf, rhs_t.buf), writes=(b.buf,))
        return b

    def s1_stats(x_d, row0, NB):
        for b in range(NB):
            xb = xin[b % 2]
            k.dma(SP, lambda e: e.dma_start(out=xb.ap, in_=x_d[row0 + b * 128: row0 + (b + 1) * 128, :]),
                  xin_ch[b % 2], writes=(xb.buf,))
            ssq = small()
            k.op(ACT, lambda e: e.activation(out=junk.ap, in_=xb.ap, func=AF.Square, accum_out=ssq.ap[:, 0:1]),
                 reads=(xb.buf,), writes=(junk.buf, ssq.buf))
            r_ = rstd_from_ssq(ssq, 1, D * EPS)
            hb = hn[b]
            k.op(DVE, lambda e: e.tensor_scalar(out=hb.ap, in0=xb.ap, scalar1=r_.ap[:, 0:1], scalar2=None, op0=ALU.mult),
                 reads=(xb.buf, r_.buf), writes=(hb.buf,))

    def s1_xpose(NB, NT):
        hT3 = hT.ap[:, 0:8 * NT].rearrange("p (k t) -> p k t", t=NT)
        for b in range(NB):
            hb = hn[b]
            bk = bank()
            pv = bk.ap.bitcast(BF16)
            k.pe_group([(lambda e, kc=kc: e.transpose(out=pv[:, kc * 128:(kc + 1) * 128],
                                                      in_=hb.ap[:, kc * 128:(kc + 1) * 128], identity=ident_bf.ap))
                        for kc in range(8)], reads=(hb.buf, ident_bf.buf), writes=(bk.buf,))
            k.op(DVE, lambda e: e.tensor_tensor(out=hT3[:, :, b * 128:(b + 1) * 128],
                                                in0=pv.rearrange("p (k t) -> p k t", t=128),
                                                in1=gpre32.ap.unsqueeze(2).to_broadcast([128, 8, 128]), op=ALU.mult),
                 reads=(bk.buf, gpre32.buf), writes=(hT.buf,))

    def s1(x_d, row0, NB, NT):
        s1_stats(x_d, row0, NB)
        s1_xpose(NB, NT)

    def s2_f(NT):
        hT3 = hT.ap[:, 0:8 * NT].rearrange("p (k t) -> p k t", t=NT)
        sigf3 = sigf_ap[:, 0:8 * NT].rearrange("p (k t) -> p k t", t=NT)
        for g in range(2):
            slot = wfetch("f%d" % g)
            for cc in range(4):
                h = g * 4 + cc
                b_ = proj_fm(slot, cc, NT, hT, hT3)
                k.op(ACT, lambda e: e.activation(out=sigf3[:, h, :], in_=b_.ap[:, 0:NT], func=AF.Sigmoid),
                     reads=(b_.buf,), writes=(kd_tm.buf, v_tm.buf))

    pre = {"stats": False, "xposed": False, "f": False, "recv": None}

    state = {"S": 0, "B": 0}

    def tile(x_d, p_d, y_d, row0, NT, sample, nxt=None):
        NB = NT // 128
        NCH = NT // 64
        nseq = NSEQ_S if sample else 1
        L = NT // nseq

        def v3(ap, n=NT):
            return ap[:, 0:8 * n].rearrange("p (k t) -> p k t", t=n)

        hT3 = v3(hT.ap)
        qe3, ke3, qb3, kdT3 = v3(bigA.ap), v3(bigB.ap), v3(bigC.ap), v3(bigD.ap)
        gated3, og3, merged3 = v3(bigE.ap), ke3, qb3
        attm4 = bigD.ap[:, 0:NB * 1024].rearrange("p (b h t) -> p b h t", h=8, t=128)
        kd3 = kd_tm.ap[:, 0:NB * 1024].rearrange("p (b c) -> p b c", c=1024)
        vt3 = v_tm.ap[:, 0:NB * 1024].rearrange("p (b c) -> p b c", c=1024)
        sigf3 = v3(sigf_ap)
        sq3 = v3(oT.ap)
        oT3 = v3(oT.ap)
        outs3 = oT.ap[:, 0:NB * 1024].rearrange("p (b c) -> p b c", c=1024)
        dcy3 = dcy.ap.rearrange("p (h c) -> p h c", c=8)

        if not pre["stats"]:
            s1_stats(x_d, row0, NB)
        if not pre["xposed"]:
            s1_xpose(NB, NT)
        pre["stats"] = pre["xposed"] = False
        stage(1)
        for g in range(2):
            slot = wfetch("q%d" % g)
            for cc in range(4):
                h = g * 4 + cc
                b_ = proj_fm(slot, cc, NT, hT, hT3)
                sg = tmp()
                k.op(ACT, lambda e: e.activation(out=sg.ap[:, 0:NT], in_=b_.ap[:, 0:NT], func=AF.Sigmoid),
                     reads=(b_.buf,), writes=(sg.buf,))
                k.op(DVE, lambda e: e.scalar_tensor_tensor(out=sq3[:, h, :], in0=b_.ap[:, 0:NT], scalar=float(QK_SCALE),
                                                           in1=sg.ap[:, 0:NT], op0=ALU.mult, op1=ALU.mult),
                     reads=(b_.buf, sg.buf), writes=(oT.buf,))
        if not pre["f"]:
            s2_f(NT)
        pre["f"] = False

        ivs = [wfetch("iv0"), wfetch("iv1")]
        iv_items = [(g, blk) for g in range(2) for blk in range(NB)]

        def iv_group(g, blk):
            b_ = bank()
            k.pe_group([(lambda e, kc=kc: e.matmul(b_.ap[:, 0:512], lhsT=hT3[:, kc, blk * 128:(blk + 1) * 128],
                                                   rhs=ivs[g].ap[:, kc, :], start=(kc == 0), stop=(kc == 7)))
                        for kc in range(8)], reads=(ivs[g].buf, hT.buf), writes=(b_.buf,))
            k.op(ACT, lambda e: e.copy(out=hn[blk].ap[:, g * 512:(g + 1) * 512], in_=b_.ap[:, 0:512]),
                 reads=(b_.buf,), writes=(hn[blk].buf,))

        stage(2)
        def c3(ap):
            return ap[:, 0:NT].rearrange("p (c t) -> p c t", t=64)

        for h in range(8):
            for it_ in iv_items[h * len(iv_items) // 8:(h + 1) * len(iv_items) // 8]:
                iv_group(*it_)
            Lg, B_, BM, BL, E1, E3, KK, E2, E4 = [tmp() for _ in range(9)]
            k.op(ACT, lambda e: e.activation(out=Lg.ap[:, 0:NT], in_=sigf3[:, h, :], func=AF.Ln,
                                             bias=lbt.ap[:, h:h + 1], scale=omlt.ap[:, h:h + 1]),
                 reads=(kd_tm.buf, v_tm.buf, lbt.buf, omlt.buf), writes=(Lg.buf,))
            k.op(ACT, lambda e: e.activation(out=KK.ap[:, 0:NT], in_=sigf3[:, h, :], func=AF.Identity,
                                             bias=omlt.ap[:, h:h + 1], scale=nomlt.ap[:, h:h + 1]),
                 reads=(kd_tm.buf, v_tm.buf, nomlt.buf, omlt.buf), writes=(KK.buf,))
            k.op(DVE, lambda e: e.tensor_tensor_scan(out=B_.ap[:, 0:NT], data0=rmask.ap[:, 0:NT], data1=Lg.ap[:, 0:NT],
                                                     initial=0.0, op0=ALU.mult, op1=ALU.add),
                 reads=(rmask.buf, Lg.buf), writes=(B_.buf,))
            k.op(DVE, lambda e: e.tensor_tensor(out=c3(BM.ap), in0=c3(B_.ap),
                                                in1=c3(B_.ap)[:, :, 31:32].to_broadcast([128, NCH, 64]), op=ALU.subtract),
                 reads=(B_.buf,), writes=(BM.buf,))
            k.op(DVE, lambda e: e.tensor_tensor(out=c3(BL.ap), in0=c3(B_.ap)[:, :, 63:64].to_broadcast([128, NCH, 64]),
                                                in1=c3(B_.ap), op=ALU.subtract),
                 reads=(B_.buf,), writes=(BL.buf,))
            k.op(ACT, lambda e: e.activation(out=E1.ap[:, 0:NT], in_=BM.ap[:, 0:NT], func=AF.Exp), reads=(BM.buf,), writes=(E1.buf,))
            k.op(ACT, lambda e: e.activation(out=E2.ap[:, 0:NT], in_=BM.ap[:, 0:NT], func=AF.Exp, scale=-1.0),
                 reads=(BM.buf,), writes=(E2.buf,))
            k.op(ACT, lambda e: e.activation(out=E3.ap[:, 0:NT], in_=B_.ap[:, 0:NT], func=AF.Exp), reads=(B_.buf,), writes=(E3.buf,))
            k.op(ACT, lambda e: e.activation(out=E4.ap[:, 0:NT], in_=BL.ap[:, 0:NT], func=AF.Exp), reads=(BL.buf,), writes=(E4.buf,))
            k.op(ACT, lambda e: e.activation(out=dcy3[:, h, 0:NCH], in_=c3(B_.ap)[:, :, 63], func=AF.Exp),
                 reads=(B_.buf,), writes=(dcy.buf,))
            k.op(DVE, lambda e: e.tensor_tensor(out=qe3[:, h, :], in0=sq3[:, h, :], in1=E1.ap[:, 0:NT], op=ALU.mult),
                 reads=(oT.buf, E1.buf), writes=(bigA.buf,))
            k.op(DVE, lambda e: e.tensor_tensor(out=qb3[:, h, :], in0=sq3[:, h, :], in1=E3.ap[:, 0:NT], op=ALU.mult),
                 reads=(oT.buf, E3.buf), writes=(bigC.buf,))
            k.op(POOL, lambda e: e.tensor_tensor(out=ke3[:, h, :], in0=KK.ap[:, 0:NT], in1=E2.ap[:, 0:NT], op=ALU.mult),
                 reads=(KK.buf, E2.buf), writes=(bigB.buf,))
            k.op(POOL, lambda e: e.tensor_tensor(out=kdT3[:, h, :], in0=KK.ap[:, 0:NT], in1=E4.ap[:, 0:NT], op=ALU.mult),
                 reads=(KK.buf, E4.buf), writes=(bigD.buf,))

        stage(4)
        for blk in range(NB):
            bk = bank()
            pv = bk.ap.bitcast(BF16)
            k.pe_group([(lambda e, h=h: e.transpose(out=pv[:, h * 128:(h + 1) * 128],
                                                    in_=kdT3[:, h, blk * 128:(blk + 1) * 128], identity=ident_bf.ap))
                        for h in range(8)], reads=(bigD.buf, ident_bf.buf), writes=(bk.buf,))
            k.op(DVE, lambda e: e.tensor_copy(out=kd3[:, blk, :], in_=pv), reads=(bk.buf,), writes=(kd_tm.buf,))

        stage(5)
        for p in range(NB):
            for hg in range(2):
                b_ = bank()
                k.pe_group([(lambda e, hh=hh: e.matmul(b_.ap[:, hh * 128:(hh + 1) * 128],
                                                       lhsT=ke3[:, hg * 4 + hh, p * 128:(p + 1) * 128],
                                                       rhs=qe3[:, hg * 4 + hh, p * 128:(p + 1) * 128], start=True, stop=True))
                            for hh in range(4)], reads=(bigA.buf, bigB.buf), writes=(b_.buf,))
                k.op(DVE, lambda e: e.tensor_tensor(out=attm4[:, p, hg * 4:(hg + 1) * 4, :],
                                                    in0=b_.ap.rearrange("p (h t) -> p h t", t=128),
                                                    in1=mask2.ap.unsqueeze(1).to_broadcast([128, 4, 128]), op=ALU.mult),
                     reads=(b_.buf, mask2.buf), writes=(bigD.buf,))

        aw = {}

        abk = {}

        def brA_pe(j):
            s_, jj = j // 4, j % 4
            if jj == 0:
                aw["s"] = (wfetch("vA%d" % s_), wfetch("cA%d" % s_), wfetch("zA%d" % s_), wfetch("bA%d" % s_))
            sv, sc_, sz, sb_ = aw["s"]
            abk[j] = (proj_fm(sv, jj, NT, hT, hT3), proj_fm(sc_, jj, NT, hT, hT3),
                      proj_fm(sz, jj, NT, hT, hT3), proj_fm(sb_, jj, NT, hT, hT3))

        atm = {}

        def brA_early(j):
            bv, bc, bz, bb = abk.pop(j)
            vAs, sgz, u, t1, t2 = tmp(), tmp(), tmp(), tmp(), tmp()
            atm[j] = (sgz, u, t1, t2)
            u3 = u.ap[:, 0:nseq * (L + 2)].rearrange("p (s l) -> p s l", l=L + 2)

            def s3(ap):
                return ap[:, 0:NT].rearrange("p (s l) -> p s l", l=L)

            k.op(ACT, lambda e: e.copy(out=vAs.ap[:, 0:NT], in_=bv.ap[:, 0:NT]), reads=(bv.buf,), writes=(vAs.buf,))
            k.op(ACT, lambda e: e.activation(out=sgz.ap[:, 0:NT], in_=bz.ap[:, 0:NT], func=AF.Sigmoid),
                 reads=(bz.buf,), writes=(sgz.buf,))
            k.op(DVE, lambda e: e.tensor_tensor(out=u3[:, :, 2:2 + L], in0=s3(vAs.ap), in1=s3(bc.ap), op=ALU.mult),
                 reads=(vAs.buf, bc.buf), writes=(u.buf,))
            k.op(DVE, lambda e: e.tensor_tensor(out=sgz.ap[:, 0:NT], in0=sgz.ap[:, 0:NT], in1=bz.ap[:, 0:NT], op=ALU.mult),
                 reads=(sgz.buf, bz.buf), writes=(sgz.buf,))
            k.op(DVE, lambda e: e.tensor_tensor(out=sgz.ap[:, 0:NT], in0=sgz.ap[:, 0:NT], in1=bb.ap[:, 0:NT], op=ALU.mult),
                 reads=(sgz.buf, bb.buf), writes=(sgz.buf,))

        def brA_late(j):
            sgz, u, t1, t2 = atm.pop(j)
            u3 = u.ap[:, 0:nseq * (L + 2)].rearrange("p (s l) -> p s l", l=L + 2)

            def s3(ap):
                return ap[:, 0:NT].rearrange("p (s l) -> p s l", l=L)

            if sample:
                hist_src = uhs.ap.rearrange("p (j s r) -> p j s r", s=NSEQ_S, r=2)[:, j, :, :]
                hist_buf = uhs.buf
            else:
                hist_src = uh.ap.rearrange("p (j s r) -> p j s r", s=1, r=2)[:, j, :, :]
                hist_buf = uh.buf
            k.op(POOL, lambda e: e.tensor_copy(out=u3[:, :, 0:2], in_=hist_src), reads=(hist_buf, u.buf), writes=(u.buf,))
            if sample:
                dst = uo.ap.rearrange("p (j s r) -> p j s r", s=NSEQ_S, r=2)[:, j, :, :]
                k.op(POOL, lambda e: e.tensor_copy(out=dst, in_=u3[:, :, L:L + 2]), reads=(u.buf,), writes=(uo.buf,))
            else:
                dst = uh.ap.rearrange("p (j s r) -> p j s r", s=1, r=2)[:, j, :, :]
                k.op(POOL, lambda e: e.tensor_copy(out=dst, in_=u3[:, :, L:L + 2]), reads=(u.buf,), writes=(uh.buf,))
            cw3 = cw.ap.rearrange("p (j t) -> p j t", t=3)
            k.op(ACT, lambda e: e.activation(out=s3(t1.ap), in_=u3[:, :, 0:L], func=AF.Identity, scale=cw3[:, j, 0:1]),
                 reads=(u.buf, cw.buf), writes=(t1.buf,))
            k.op(DVE, lambda e: e.scalar_tensor_tensor(out=s3(t2.ap), in0=u3[:, :, 1:1 + L], scalar=cw3[:, j, 1:2],
                                                       in1=s3(t1.ap), op0=ALU.mult, op1=ALU.add),
                 reads=(u.buf, cw.buf, t1.buf), writes=(t2.buf,))
            k.op(DVE, lambda e: e.scalar_tensor_tensor(out=s3(t1.ap), in0=u3[:, :, 2:2 + L], scalar=cw3[:, j, 2:3],
                                                       in1=s3(t2.ap), op0=ALU.mult, op1=ALU.add),
                 reads=(u.buf, cw.buf, t2.buf), writes=(t1.buf,))
            k.op(POOL, lambda e: e.tensor_tensor(out=gated3[:, j, :], in0=sgz.ap[:, 0:NT], in1=t1.ap[:, 0:NT], op=ALU.mult),
                 reads=(t1.buf, sgz.buf), writes=(bigE.buf,))

        def brA_ew(j):
            brA_early(j)
            brA_late(j)

        stage(6)
        if pre["recv"] is not None:
            pre["recv"]()
            pre["recv"] = None
        def po_view(hh_bank, h, c0, n):
            return PO[hh_bank].ap[:, (h % 4) * 128 + c0:(h % 4) * 128 + c0 + n]

        for p in range(NB):
            for half in range(2):
                c = 2 * p + half
                if 8 // NCH == 1 and c >= 1:
                    brA_early(c - 1)
                if sample:
                    si = c % 2
                    state["S"] = si
                    state["B"] = si
                    k.dma(SP, lambda e: e.dma_start(out=Sf[si].ap.rearrange("p (h v) -> p h v", v=128),
                                                    in_=shgrn[c].rearrange("h k v -> k h v")),
                          s_ld_ch[si], writes=(Sf[si].buf,))
                    k.op(ACT, lambda e: e.copy(out=Sb[si].ap, in_=Sf[si].ap), reads=(Sf[si].buf,), writes=(Sb[si].buf,))
                si = state["S"]
                S_f, S_b = Sf[si], Sb[state["B"]]
                lo, hi = half * 64, (half + 1) * 64
                fns = []
                for h in range(8):
                    fns.append(lambda e, h=h: e.matmul(po_view(h // 4, h, lo, 64), lhsT=S_b.ap[:, h * 128:(h + 1) * 128],
                                                       rhs=qb3[:, h, c * 64:(c + 1) * 64], start=True, stop=False))
                    fns.append(lambda e, h=h: e.matmul(po_view(h // 4, h, lo, 64), lhsT=hn[p].ap[:, h * 128:(h + 1) * 128],
                                                       rhs=attm4[:, p, h, lo:hi], start=False, stop=True))
                k.pe_group(fns, reads=(S_b.buf, bigC.buf, hn[p].buf, bigD.buf), writes=(PO[0].buf, PO[1].buf))
                pb = [bank(), bank()]
                k.pe_group([(lambda e, h=h: e.matmul(pb[h // 4].ap[:, (h % 4) * 128:(h % 4 + 1) * 128],
                                                     lhsT=kd3[lo:hi, p, h * 128:(h + 1) * 128],
                                                     rhs=hn[p].ap[lo:hi, h * 128:(h + 1) * 128], start=True, stop=True))
                            for h in range(8)], reads=(kd_tm.buf, hn[p].buf), writes=(pb[0].buf, pb[1].buf))
                for h in range(8):
                    k.op(DVE, lambda e: e.scalar_tensor_tensor(out=S_f.ap[:, h * 128:(h + 1) * 128],
                                                               in0=S_f.ap[:, h * 128:(h + 1) * 128],
                                                               scalar=dcy3[:, h, c:c + 1],
                                                               in1=pb[h // 4].ap[:, (h % 4) * 128:(h % 4 + 1) * 128],
                                                               op0=ALU.mult, op1=ALU.add),
                         reads=(S_f.buf, dcy.buf, pb[h // 4].buf), writes=(S_f.buf,))
                if sample:
                    k.dma(SP, lambda e: e.dma_start(out=hgrn_s[c].rearrange("h k v -> k h v"),
                                                    in_=S_f.ap.rearrange("p (h v) -> p h v", v=128)),
                          s_st_ch[si], reads=(S_f.buf,))
                else:
                    nb2 = 1 - state["B"]
                    k.op(ACT, lambda e: e.copy(out=Sb[nb2].ap, in_=S_f.ap), reads=(S_f.buf,), writes=(Sb[nb2].buf,))
                    state["B"] = nb2
                cps = 8 // NCH
                if cps == 1:
                    if c >= 1:
                        brA_late(c - 1)
                    brA_pe(c)
                else:
                    for jx in range(cps):
                        brA_pe(c * cps + jx)
                        brA_ew(c * cps + jx)
            for hg in range(2):
                k.op(DVE, lambda e: e.tensor_copy(out=oT3[:, hg * 4:(hg + 1) * 4, p * 128:(p + 1) * 128],
                                                  in_=PO[hg].ap.rearrange("p (h t) -> p h t", t=128)),
                     reads=(PO[hg].buf,), writes=(oT.buf,))

        if 8 // NCH == 1:
            brA_ew(7)
        stage(7)
        items = [(p, hg) for p in range(NB) for hg in range(2)]

        def sl_of(it):
            p, hg = it
            return oT3[:, hg * 4:(hg + 1) * 4, p * 128:(p + 1) * 128]

        def sq_emit(it):
            sqh = sqb.ap[:, it[1] * 512:(it[1] + 1) * 512]
            k.op(DVE, lambda e: e.tensor_tensor(out=sqh.rearrange("p (h t) -> p h t", t=128), in0=sl_of(it), in1=sl_of(it),
                                                op=ALU.mult), reads=(oT.buf,), writes=(sqb_h[it[1]], sqb.buf))

        sqb_h = [Buf("sqb_h0"), Buf("sqb_h1")]
        zr = [tmps[i] for i in range(8)]
        lr = [tmps[8 + i] for i in range(4)]
        zw = {}
        hpi = 8 // len(items)

        def zb_head(h):
            if h % 4 == 0:
                zw["s"] = wfetch("zB%d" % (h // 4))
            b_ = proj_fm(zw["s"], h % 4, NT, hT, hT3)
            k.op(DVE, lambda e: e.tensor_copy(out=zr[h].ap[:, 0:NT], in_=b_.ap[:, 0:NT]), reads=(b_.buf,), writes=(zr[h].buf,))

        sq_emit(items[0])
        for ii, it in enumerate(items):
            if ii + 1 < len(items):
                sq_emit(items[ii + 1])
            sqh = sqb.ap[:, it[1] * 512:(it[1] + 1) * 512]
            b_ = bank()
            k.pe_group([lambda e: e.matmul(b_.ap, lhsT=ones_bf.ap, rhs=sqh, start=True, stop=True)],
                       reads=(ones_bf.buf, sqb_h[it[1]]), writes=(b_.buf,))
            for hx in range(hpi):
                zb_head(ii * hpi + hx)
            l_, r_ = lr[(2 * ii) % 4], lr[(2 * ii + 1) % 4]
            k.op(ACT, lambda e: e.activation(out=l_.ap[:, 0:512], in_=b_.ap, func=AF.Ln, bias=float(128 * EPS), scale=1.0),
                 reads=(b_.buf,), writes=(l_.buf,))
            k.op(ACT, lambda e: e.activation(out=r_.ap[:, 0:512], in_=l_.ap[:, 0:512], func=AF.Exp, scale=-0.5),
                 reads=(l_.buf,), writes=(r_.buf,))
            k.op(DVE, lambda e: e.tensor_tensor(out=sl_of(it), in0=sl_of(it),
                                                in1=r_.ap[:, 0:512].rearrange("p (h t) -> p h t", t=128), op=ALU.mult),
                 reads=(oT.buf, r_.buf), writes=(oT.buf,))

        if nxt is not None:
            s1_stats(nxt[0], nxt[1], nxt[2] // 128)
            pre["stats"] = True
        stage(8)
        for h in range(8):
            k.op(ACT, lambda e: e.activation(out=zr[h].ap[:, 0:NT], in_=zr[h].ap[:, 0:NT], func=AF.Silu),
                 reads=(zr[h].buf,), writes=(zr[h].buf,))
            k.op(DVE, lambda e: e.scalar_tensor_tensor(out=og3[:, h, :], in0=zr[h].ap[:, 0:NT], scalar=gon.ap[:, h:h + 1],
                                                       in1=oT3[:, h, :], op0=ALU.mult, op1=ALU.mult),
                 reads=(zr[h].buf, gon.buf, oT.buf), writes=(bigB.buf,))

        stage(9)
        for s in range(2):
            sga, swa, sgb, swb = wfetch("gA%d" % s), wfetch("wao%d" % s), wfetch("gB%d" % s), wfetch("wbo%d" % s)
            for ii in range(4):
                i = s * 4 + ii
                bga = proj_fm(sga, ii, NT, hT, hT3)
                sa = tmp()
                k.op(ACT, lambda e: e.activation(out=sa.ap[:, 0:NT], in_=bga.ap[:, 0:NT], func=AF.Sigmoid),
                     reads=(bga.buf,), writes=(sa.buf,))
                bya = proj_fm(swa, ii, NT, bigE, gated3)
                k.op(DVE, lambda e: e.tensor_tensor(out=sa.ap[:, 0:NT], in0=sa.ap[:, 0:NT], in1=bya.ap[:, 0:NT], op=ALU.mult),
                     reads=(sa.buf, bya.buf), writes=(sa.buf,))
                bgb = proj_fm(sgb, ii, NT, hT, hT3)
                sb2 = tmp()
                k.op(ACT, lambda e: e.activation(out=sb2.ap[:, 0:NT], in_=bgb.ap[:, 0:NT], func=AF.Sigmoid),
                     reads=(bgb.buf,), writes=(sb2.buf,))
                byb = proj_fm(swb, ii, NT, bigB, og3)
                k.op(DVE, lambda e: e.tensor_tensor(out=sb2.ap[:, 0:NT], in0=sb2.ap[:, 0:NT], in1=byb.ap[:, 0:NT], op=ALU.mult),
                     reads=(sb2.buf, byb.buf), writes=(sb2.buf,))
                k.op(POOL, lambda e: e.tensor_tensor(out=merged3[:, i, :], in0=sa.ap[:, 0:NT], in1=sb2.ap[:, 0:NT], op=ALU.add),
                     reads=(sa.buf, sb2.buf), writes=(bigC.buf,))

        if nxt is not None:
            s1_xpose(nxt[2] // 128, nxt[2])
            pre["xposed"] = True
        stage(10)
        ob = [Buf("outs%d" % i) for i in range(NB)]
        for ob_ in ob:
            ob_.w = oT.buf.w
            ob_.r = dict(oT.buf.r)
        wo = [wfetch("wo0"), wfetch("wo1")]
        ssq2p = small()
        for b in range(NB):
            for half in range(2):
                b_ = bank()
                k.pe_group([(lambda e, kc=kc: e.matmul(b_.ap, lhsT=merged3[:, kc, b * 128:(b + 1) * 128],
                                                       rhs=wo[half].ap[:, kc, :], start=(kc == 0), stop=(kc == 7)))
                            for kc in range(8)], reads=(bigC.buf, wo[half].buf), writes=(b_.buf,))
                k.op(DVE, lambda e: e.tensor_copy(out=outs3[:, b, half * 512:(half + 1) * 512], in_=b_.ap),
                     reads=(b_.buf,), writes=(ob[b],))
                k.op(ACT, lambda e: e.activation(out=junk.ap[:, 0:512], in_=outs3[:, b, half * 512:(half + 1) * 512],
                                                 func=AF.Square, accum_out=ssq2p.ap[:, 2 * b + half:2 * b + half + 1]),
                     reads=(ob[b],), writes=(junk.buf, ssq2p.buf))
        stage(100)
        ssq2 = small()
        sp3 = ssq2p.ap.rearrange("p (b t) -> p b t", t=2)
        k.op(DVE, lambda e: e.tensor_tensor(out=ssq2.ap[:, 0:NB], in0=sp3[:, 0:NB, 0], in1=sp3[:, 0:NB, 1], op=ALU.add),
             reads=(ssq2p.buf,), writes=(ssq2.buf,))
        rstd2 = rstd_from_ssq(ssq2, NB, D * EPS)
        stage(101)
        ssq3 = small()
        for b in range(NB):
            xb = xr[b % 2]
            k.dma(SP, lambda e: e.dma_start(out=xb.ap, in_=x_d[row0 + b * 128: row0 + (b + 1) * 128, :]),
                  xr_ch[b % 2], writes=(xb.buf,))
            k.op(DVE, lambda e: e.scalar_tensor_tensor(out=outs3[:, b, :], in0=outs3[:, b, :], scalar=rstd2.ap[:, b:b + 1],
                                                       in1=gpost32.ap, op0=ALU.mult, op1=ALU.mult),
                 reads=(ob[b], rstd2.buf, gpost32.buf), writes=(ob[b],))
            k.op(DVE, lambda e: e.tensor_tensor(out=outs3[:, b, :], in0=outs3[:, b, :], in1=xb.ap, op=ALU.add),
                 reads=(ob[b], xb.buf), writes=(ob[b],))
            k.op(ACT, lambda e: e.activation(out=junk.ap, in_=outs3[:, b, :], func=AF.Square, accum_out=ssq3.ap[:, b:b + 1]),
                 reads=(ob[b],), writes=(junk.buf, ssq3.buf))
        stage(102)
        rstd3 = rstd_from_ssq(ssq3, NB, D * EPS)
        stage(103)
        if nxt is not None and nxt[3]:
            s2_f(nxt[2])
            pre["f"] = True
        wg = [wfetch("wg0"), wfetch("wg1")]
        wpl = wfetch("ple")
        def tail_front(b):
            nb_, nT = n2[b % 2], n2T[b % 2]
            k.op(DVE, lambda e: e.tensor_scalar(out=nb_.ap, in0=outs3[:, b, :], scalar1=rstd3.ap[:, b:b + 1], scalar2=None,
                                                op0=ALU.mult), reads=(ob[b], rstd3.buf), writes=(nb_.buf,))
            bk = bank()
            pv = bk.ap.bitcast(BF16)
            k.pe_group([(lambda e, kc=kc: e.transpose(out=pv[:, kc * 128:(kc + 1) * 128],
                                                      in_=nb_.ap[:, kc * 128:(kc + 1) * 128], identity=ident_bf.ap))
                        for kc in range(8)], reads=(nb_.buf, ident_bf.buf), writes=(bk.buf,))
            nT3 = nT.ap.rearrange("p (k t) -> p k t", t=128)
            k.op(DVE, lambda e: e.tensor_tensor(out=nT3, in0=pv.rearrange("p (k t) -> p k t", t=128),
                                                in1=gple32.ap.unsqueeze(2).to_broadcast([128, 8, 128]), op=ALU.mult),
                 reads=(bk.buf, gple32.buf), writes=(nT.buf,))
            ptb_, pt_, pT_ = ptb[b % 2], pt[b % 2], pT[b % 2]
            k.dma(SP, lambda e: e.dma_start(out=pt_.ap, in_=p_d[row0 + b * 128: row0 + (b + 1) * 128, :]),
                  pt_ch[b % 2], writes=(pt_.buf,))
            k.op(POOL, lambda e: e.tensor_copy(out=ptb_.ap, in_=pt_.ap), reads=(pt_.buf,), writes=(ptb_.buf,))
            bk2 = bank()
            pv2 = bk2.ap.bitcast(BF16)
            k.pe_group([(lambda e, kc=kc: e.transpose(out=pv2[:, kc * 128:(kc + 1) * 128],
                                                      in_=ptb_.ap[:, kc * 128:(kc + 1) * 128], identity=ident_bf.ap))
                        for kc in range(2)], reads=(ptb_.buf, ident_bf.buf), writes=(bk2.buf,))
            k.op(ACT, lambda e: e.copy(out=pT_.ap, in_=pv2[:, 0:256]), reads=(bk2.buf,), writes=(pT_.buf,))
            pT3 = pT_.ap.rearrange("p (k t) -> p k t", t=128)
            return nT3, pT3, pT_

        def tail_back(b, nT3, pT3, pT_):
            nT = n2T[b % 2]
            for half in range(2):
                bg = bank()
                k.pe_group([(lambda e, kc=kc: e.matmul(bg.ap, lhsT=nT3[:, kc, :], rhs=wg[half].ap[:, kc, :],
                                                       start=(kc == 0), stop=(kc == 7))) for kc in range(8)],
                           reads=(nT.buf, wg[half].buf), writes=(bg.buf,))
                sgt = tmp()
                k.op(ACT, lambda e: e.activation(out=sgt.ap[:, 0:512], in_=bg.ap, func=AF.Sigmoid),
                     reads=(bg.buf,), writes=(sgt.buf,))
                bp = bank()
                k.pe_group([(lambda e, kc=kc: e.matmul(bp.ap, lhsT=pT3[:, kc, :],
                                                       rhs=wpl.ap[:, kc, half * 512:(half + 1) * 512],
                                                       start=(kc == 0), stop=(kc == 1))) for kc in range(2)],
                           reads=(pT_.buf, wpl.buf), writes=(bp.buf,))
                k.op(DVE, lambda e: e.tensor_tensor(out=sgt.ap[:, 0:512], in0=sgt.ap[:, 0:512], in1=bp.ap, op=ALU.mult),
                     reads=(sgt.buf, bp.buf), writes=(sgt.buf,))
                k.op(DVE, lambda e: e.tensor_tensor(out=outs3[:, b, half * 512:(half + 1) * 512],
                                                     in0=outs3[:, b, half * 512:(half + 1) * 512], in1=sgt.ap[:, 0:512],
                                                     op=ALU.add), reads=(ob[b], sgt.buf), writes=(ob[b],))
            k.dma(SP, lambda e: e.dma_start(out=y_d[row0 + b * 128: row0 + (b + 1) * 128, :], in_=outs3[:, b, :]),
                  y_ch[b], reads=(ob[b],))

        fr = {0: tail_front(0)}
        for b in range(NB):
            if b + 1 < NB:
                fr[b + 1] = tail_front(b + 1)
            tail_back(b, *fr.pop(b))
        oT.buf.w = None
        merged_r = {}
        for ob_ in ob:
            for st in ([ob_.w] if ob_.w else []) + list(ob_.r.values()):
                if st[0] not in merged_r or merged_r[st[0]][2] < st[2]:
                    merged_r[st[0]] = st
        oT.buf.r = merged_r
        stage(11)


    onec = k.sb("onec", [128, 8], F32)
    sel_t = k.sb("sel_t", [128, 1], F32)
    k.op(POOL, lambda e: e.memset(onec.ap, 1.0), writes=(onec.buf,))

    P1_NT, P1_NB = 512, 4
    p1_sig = [
        (lambda h: oT.ap[:, h * 512:(h + 1) * 512], lambda h: oT.buf),
        (lambda h: (bigA if h < 4 else bigB).ap.bitcast(F32)[:, (h % 4) * 512:(h % 4 + 1) * 512],
         lambda h: (bigA if h < 4 else bigB).buf),
    ]
    p1_v = [v_tm, bigC]

    p1_w = {}

    def p1_weights():
        if not p1_w:
            for nm in ("f0", "f1", "iv0", "iv1"):
                p1_w[nm] = wfetch(nm)
        return p1_w

    def p1_f_head(pp, h):
        NT = P1_NT
        hT3 = hT.ap[:, 0:8 * NT].rearrange("p (k t) -> p k t", t=NT)
        sig_ap, sig_buf = p1_sig[pp]
        slot = p1_weights()["f%d" % (h // 4)]
        b_ = proj_fm(slot, h % 4, NT, hT, hT3)
        k.op(ACT, lambda e: e.copy(out=sig_ap(h), in_=b_.ap[:, 0:NT]), reads=(b_.buf,), writes=(sig_buf(h),))

    def p1_iv_group(pp, g, blk):
        NT, NB = P1_NT, P1_NB
        hT3 = hT.ap[:, 0:8 * NT].rearrange("p (k t) -> p k t", t=NT)
        vt3 = p1_v[pp].ap[:, 0:NB * 1024].rearrange("p (b c) -> p b c", c=1024)
        w_ = p1_weights()["iv%d" % g]
        b_ = bank()
        k.pe_group([(lambda e, kc=kc: e.matmul(b_.ap[:, 0:512], lhsT=hT3[:, kc, blk * 128:(blk + 1) * 128],
                                               rhs=w_.ap[:, kc, :], start=(kc == 0), stop=(kc == 7)))
                    for kc in range(8)], reads=(w_.buf, hT.buf), writes=(b_.buf,))
        k.op(DVE, lambda e: e.tensor_copy(out=vt3[:, blk, g * 512:(g + 1) * 512], in_=b_.ap[:, 0:512]),
             reads=(b_.buf,), writes=(p1_v[pp].buf,))

    def p1_sigmoid_batch(pp):
        sig_ap, sig_buf = p1_sig[pp]
        for h in range(8):
            k.op(ACT, lambda e: e.activation(out=sig_ap(h), in_=sig_ap(h), func=AF.Sigmoid),
                 reads=(sig_buf(h),), writes=(sig_buf(h),))

    def p1_prep_head(pp, h):
        NT = P1_NT
        kdT3 = bigD.ap[:, 0:8 * NT].rearrange("p (k t) -> p k t", t=NT)
        dcy3 = dcy.ap.rearrange("p (h c) -> p h c", c=8)
        sig_ap, sig_buf = p1_sig[pp]
        if True:
            Lg, KK, B_, BL, E4 = tmp(), tmp(), tmp(), tmp(), tmp()
            k.op(ACT, lambda e: e.activation(out=Lg.ap[:, 0:NT], in_=sig_ap(h), func=AF.Ln,
                                             bias=lbt.ap[:, h:h + 1], scale=omlt.ap[:, h:h + 1]),
                 reads=(sig_buf(h), lbt.buf, omlt.buf), writes=(Lg.buf,))
            k.op(ACT, lambda e: e.activation(out=KK.ap[:, 0:NT], in_=sig_ap(h), func=AF.Identity,
                                             bias=omlt.ap[:, h:h + 1], scale=nomlt.ap[:, h:h + 1]),
                 reads=(sig_buf(h), nomlt.buf, omlt.buf), writes=(KK.buf,))
            k.op(DVE, lambda e: e.tensor_tensor_scan(out=B_.ap[:, 0:NT], data0=onec.ap[:, 0:1].to_broadcast([128, NT]),
                                                     data1=Lg.ap[:, 0:NT], initial=0.0, op0=ALU.mult, op1=ALU.add),
                 reads=(onec.buf, Lg.buf), writes=(B_.buf,))
            k.op(DVE, lambda e: e.tensor_scalar(out=BL.ap[:, 0:NT], in0=B_.ap[:, 0:NT], scalar1=-1.0,
                                                scalar2=B_.ap[:, NT - 1:NT], op0=ALU.mult, op1=ALU.add),
                 reads=(B_.buf,), writes=(BL.buf,))
            k.op(ACT, lambda e: e.activation(out=E4.ap[:, 0:NT], in_=BL.ap[:, 0:NT], func=AF.Exp), reads=(BL.buf,), writes=(E4.buf,))
            k.op(ACT, lambda e: e.activation(out=dcy3[:, h, 0:1], in_=B_.ap[:, NT - 1:NT], func=AF.Exp),
                 reads=(B_.buf,), writes=(dcy.buf,))
            k.op(POOL, lambda e: e.tensor_tensor(out=kdT3[:, h, :], in0=KK.ap[:, 0:NT], in1=E4.ap[:, 0:NT], op=ALU.mult),
                 reads=(KK.buf, E4.buf), writes=(bigD.buf,))

    def p1_state(pp):
        NT, NB = P1_NT, P1_NB
        kdT3 = bigD.ap[:, 0:8 * NT].rearrange("p (k t) -> p k t", t=NT)
        kd3 = kd_tm.ap[:, 0:NB * 1024].rearrange("p (b c) -> p b c", c=1024)
        vt3 = p1_v[pp].ap[:, 0:NB * 1024].rearrange("p (b c) -> p b c", c=1024)
        dcy3 = dcy.ap.rearrange("p (h c) -> p h c", c=8)
        for blk in range(NB):
            bk = bank()
            pv = bk.ap.bitcast(BF16)
            k.pe_group([(lambda e, h=h: e.transpose(out=pv[:, h * 128:(h + 1) * 128],
                                                    in_=kdT3[:, h, blk * 128:(blk + 1) * 128], identity=ident_bf.ap))
                        for h in range(8)], reads=(bigD.buf, ident_bf.buf), writes=(bk.buf,))
            k.op(DVE, lambda e: e.tensor_copy(out=kd3[:, blk, :], in_=pv), reads=(bk.buf,), writes=(kd_tm.buf,))
        pb = [bank(), bank()]
        fns = []
        for h in range(8):
            for blk in range(NB):
                fns.append(lambda e, h=h, blk=blk: e.matmul(pb[h // 4].ap[:, (h % 4) * 128:(h % 4 + 1) * 128],
                                                            lhsT=kd3[:, blk, h * 128:(h + 1) * 128],
                                                            rhs=vt3[:, blk, h * 128:(h + 1) * 128],
                                                            start=(blk == 0), stop=(blk == NB - 1)))
        k.pe_group(fns, reads=(kd_tm.buf, p1_v[pp].buf), writes=(pb[0].buf, pb[1].buf))
        S_f = Sf[0]
        for h in range(8):
            k.op(DVE, lambda e: e.scalar_tensor_tensor(out=S_f.ap[:, h * 128:(h + 1) * 128],
                                                       in0=S_f.ap[:, h * 128:(h + 1) * 128], scalar=dcy3[:, h, 0:1],
                                                       in1=pb[h // 4].ap[:, (h % 4) * 128:(h % 4 + 1) * 128],
                                                       op0=ALU.mult, op1=ALU.add),
                 reads=(S_f.buf, dcy.buf, pb[h // 4].buf), writes=(S_f.buf,))

    def phase1_all():
        ivg = [(g, blk) for g in range(2) for blk in range(P1_NB)]
        s1(xp, 0, P1_NB, P1_NT)
        for h in range(8):
            p1_f_head(0, h)
            p1_iv_group(0, *ivg[h])
        for t_i in range(NPT):
            pp = t_i % 2
            more = t_i + 1 < NPT
            if more:
                s1(xp, (t_i + 1) * 512, P1_NB, P1_NT)
            p1_sigmoid_batch(pp)
            for h in range(8):
                if more:
                    p1_f_head(1 - pp, h)
                    p1_iv_group(1 - pp, *ivg[h])
                p1_prep_head(pp, h)
            p1_state(pp)

    def exchange_and_halo():
        cc_in = nc.dram_tensor("cc_in", [128, 1024], F32).ap()
        cc_out = nc.dram_tensor("cc_out", [256, 1024], F32).ap()
        ccin_buf, ccout_buf = Buf("ccin"), Buf("ccout")
        ex_ch = k.chan("ex_ch")
        ex_ch2 = k.chan("ex_ch2")
        ex_ch3 = k.chan("ex_ch3")
        cc_sem = k.sem("cc_sem")
        k.dma(SP, lambda e: e.dma_start(out=sel_t.ap, in_=selm), ex_ch3, writes=(sel_t.buf,))
        k.dma(POOL, lambda e: e.dma_start(out=cc_in, in_=Sf[0].ap), ex_ch, reads=(Sf[0].buf,), writes=(ccin_buf,))
        POOL.wait_for(k._deps((ccin_buf,), (ccout_buf,)))
        ins = nc.gpsimd.collective_compute("AllGather", ALU.bypass, replica_groups=[[0, 4], [1, 5], [2, 6], [3, 7]],
                                           ins=[cc_in.opt()], outs=[cc_out.opt()])
        ins.then_inc(cc_sem)
        k._mark(("cc", cc_sem, 1), (ccin_buf,), (ccout_buf,))
        def receive():
            k.dma(POOL, lambda e: e.dma_start(out=Sf[1].ap, in_=cc_out[0:128, :]), ex_ch2, reads=(ccout_buf,), writes=(Sf[1].buf,))
            k.op(DVE, lambda e: e.tensor_scalar(out=Sf[0].ap, in0=Sf[1].ap, scalar1=sel_t.ap[:, 0:1], scalar2=None, op0=ALU.mult),
                 reads=(Sf[1].buf, sel_t.buf), writes=(Sf[0].buf,))
            k.op(ACT, lambda e: e.copy(out=Sb[0].ap, in_=Sf[0].ap), reads=(Sf[0].buf,), writes=(Sb[0].buf,))

        pre["recv"] = receive
        state["S"] = 0
        state["B"] = 0
        s1(xhalo, 0, 1, 128)
        hT3 = hT.ap[:, 0:8 * 128].rearrange("p (k t) -> p k t", t=128)
        uh3 = uh.ap.rearrange("p (j r) -> p j r", r=2)
        for s_ in range(2):
            sv, sc_ = wfetch("vA%d" % s_), wfetch("cA%d" % s_)
            for jj in range(4):
                j = s_ * 4 + jj
                bv = proj_fm(sv, jj, 128, hT, hT3)
                bc = proj_fm(sc_, jj, 128, hT, hT3)
                vs = small()
                k.op(ACT, lambda e: e.copy(out=vs.ap[:, 0:2], in_=bv.ap[:, 0:2]), reads=(bv.buf,), writes=(vs.buf,))
                k.op(DVE, lambda e: e.tensor_tensor(out=uh3[:, j, :], in0=vs.ap[:, 0:2], in1=bc.ap[:, 0:2], op=ALU.mult),
                     reads=(vs.buf, bc.buf), writes=(uh.buf,))


    def _program():
        stage(0)
        if EXCH:
            phase1_all()
            exchange_and_halo()
        for t_i in range(NPT):
            tile(xp, pp, yp, t_i * 512, 512, sample=False, nxt=((xp, (t_i + 1) * 512, 512, True) if t_i + 1 < NPT else (xs, 0, NSEQ_S * 64, False)))
        si = state["S"]
        k.dma(SP, lambda e: e.dma_start(out=hgrn_p.rearrange("h k v -> k h v"),
                                        in_=Sf[si].ap.rearrange("p (h v) -> p h v", v=128)),
              misc_ch, reads=(Sf[si].buf,))
        uh3 = uh.ap.rearrange("p (j r) -> p j r", r=2)
        for j in range(8):
            bj = bank()
            k.pe_group([lambda e: e.matmul(bj.ap[0:2, 0:128], lhsT=uh3[:, j, :], rhs=identf.ap, start=True, stop=True)],
                       reads=(uh.buf, identf.buf), writes=(bj.buf,))
            k.op(DVE, lambda e: e.tensor_copy(out=cv_out.ap[0:2, j * 128:(j + 1) * 128], in_=bj.ap[0:2, 0:128]),
                 reads=(bj.buf,), writes=(cv_out.buf,))
        k.dma(SP, lambda e: e.dma_start(out=conv_p, in_=cv_out.ap[0:2, :]), misc_ch, reads=(cv_out.buf,))
        stage(12)

        k.dma(SP, lambda e: e.dma_start(out=sc_in.ap, in_=sconv), misc_ch, writes=(sc_in.buf,))
        uhs3 = uhs.ap.rearrange("p (j m) -> p j m", m=8)
        for j in range(8):
            bj = bank()
            k.pe_group([lambda e: e.matmul(bj.ap[:, 0:8], lhsT=sc_in.ap[:, j * 128:(j + 1) * 128], rhs=identf.ap[0:8, 0:8],
                                           start=True, stop=True)], reads=(sc_in.buf, identf.buf), writes=(bj.buf,))
            k.op(DVE, lambda e: e.tensor_copy(out=uhs3[:, j, :], in_=bj.ap[:, 0:8]), reads=(bj.buf,), writes=(uhs.buf,))
        stage(13)
        tile(xs, ps_, ys, 0, NSEQ_S * 64, sample=True)
        uo3 = uo.ap.rearrange("p (j m) -> p j m", m=8)
        for j in range(8):
            bj = bank()
            k.pe_group([lambda e: e.matmul(bj.ap[0:8, 0:128], lhsT=uo3[:, j, :], rhs=identf.ap, start=True, stop=True)],
                       reads=(uo.buf, identf.buf, cv_out.buf), writes=(bj.buf,))
            k.op(DVE, lambda e: e.tensor_copy(out=cv_out.ap[0:8, j * 128:(j + 1) * 128], in_=bj.ap[0:8, 0:128]),
                 reads=(bj.buf,), writes=(cv_out.buf,))
        k.dma(SP, lambda e: e.dma_start(out=conv_s, in_=cv_out.ap[0:8, :]), misc_ch, reads=(cv_out.buf,))


    try:
        _program()
    except StopBuild:
        pass
    dump_names = [n for n in os.environ.get("KDUMP", "").split(",") if n]
    reg = dict(hT=hT, bigA=bigA, bigB=bigB, bigC=bigC, bigD=bigD, oT=oT, dcy=dcy, kd_tm=kd_tm, v_tm=v_tm,
               Sf0=Sf[0], uh=uh, gpre32=gpre32, lbt=lbt, cw=cw, mask2=mask2, gon=gon)
    dbg_ch = k.chan("dbg")
    for n in dump_names:
        t_ = reg[n]
        shp = list(t_.ap.shape)
        dd = nc.dram_tensor("dbg_" + n, shp, t_.ap.dtype, kind="ExternalOutput").ap()
        allb = [b for b in [t_.buf]]
        k.dma(SP, lambda e, dd=dd, t_=t_: e.dma_start(out=dd, in_=t_.ap), dbg_ch, reads=tuple(allb))
    if dbg_ch.val:
        nc.sync.wait_ge(dbg_ch.sem, dbg_ch.val)
    finals = [(c.sem, c.val) for c in y_ch + s_st_ch + [misc_ch] + conv_ch + wchan + xin_ch + pt_ch + s_ld_ch if c.val > 0]
    for sem, val in finals:
        nc.sync.wait_ge(sem, val)
    for E in (PE, ACT, DVE, POOL):
        if E.cnt:
            nc.sync.wait_ge(E.sem, E.cnt)
    es.close()
    return nc


_CACHE = {}
_LAST = None


def _get_nc(NPT):
    if NPT not in _CACHE:
        _CACHE[NPT] = build(NPT)
    return _CACHE[NPT]


def kernel(x_prompt, x_sample, state_conv, state_hgrn, p_prompt, p_sample, w_in, conv_w, lb_raw,
           g_pre, g_onorm, w_a_out, w_b_out, w_o, g_post, g_ple, w_ple_gate, w_ple_proj, _npt=None):
    f = lambda a: np.ascontiguousarray(np.asarray(a, dtype=np.float32))
    B, S = x_prompt.shape[0], x_prompt.shape[1]
    NPT = (S // 2) // 512 if _npt is None else _npt
    ntok = NPT * 512
    nc = _get_nc(NPT)
    common = dict(w_in=f(w_in[0]), conv_w=f(conv_w[0]), lb_raw=f(lb_raw), g_pre=f(g_pre[0]), g_onorm=f(g_onorm[0]),
                  w_a_out=f(w_a_out[0]), w_b_out=f(w_b_out[0]), w_o=f(w_o[0]), g_post=f(g_post[0]), g_ple=f(g_ple[0]),
                  w_ple_gate=f(w_ple_gate[0]), w_ple_proj=f(w_ple_proj[0]))
    in_maps = []
    for c in range(N_CORES):
        m = dict(common)
        b, half = c % B, c // B
        t0 = half * ntok
        m["xp"] = f(x_prompt[b, t0:t0 + ntok])
        m["pp"] = f(p_prompt[0, b, t0:t0 + ntok])
        xh = np.zeros((128, D), np.float32)
        if half == 1:
            xh[0:2] = x_prompt[b, t0 - 2:t0]
        m["xhalo"] = xh
        m["selm"] = np.full((128, 1), float(half), np.float32)
        s0 = c * NSEQ_S
        m["xs"] = f(x_sample[s0:s0 + NSEQ_S]).reshape(NSEQ_S * 64, D)
        m["ps"] = f(p_sample[0, s0:s0 + NSEQ_S]).reshape(NSEQ_S * 64, DPLE)
        m["sconv"] = f(state_conv[0, s0:s0 + NSEQ_S]).reshape(NSEQ_S * 2, D)
        m["shgrn"] = f(state_hgrn[0, s0:s0 + NSEQ_S])
        in_maps.append(m)
    res = run_bass_kernel_spmd(nc, in_maps, core_ids=list(range(N_CORES)))
    r = res.results
    global _LAST
    _LAST = r
    y_prompt = np.stack([np.concatenate([r[b]["yp"], r[b + B]["yp"]], 0) for b in range(B)], 0)
    y_sample = np.concatenate([r[c]["ys"].reshape(NSEQ_S, 64, D) for c in range(N_CORES)], 0)
    ncp = np.stack([r[b + B]["conv_p"] for b in range(B)], 0)[None]
    nhp = np.stack([r[b + B]["hgrn_p"] for b in range(B)], 0)[None]
    ncs = np.concatenate([r[c]["conv_s"].reshape(NSEQ_S, 2, D) for c in range(N_CORES)], 0)[None]
    nhs = np.concatenate([r[c]["hgrn_s"] for c in range(N_CORES)], 0)[None]
    return (y_prompt.astype(np.float32), y_sample.astype(np.float32), ncp.astype(np.float32),
            nhp.astype(np.float32), ncs.astype(np.float32), nhs.astype(np.float32))
```

```python
import numpy as np
from contextlib import ExitStack
import concourse.bass as bass
import concourse.mybir as mybir
from concourse.bass_utils import run_bass_kernel_spmd

F32, BF16 = mybir.dt.float32, mybir.dt.bfloat16
AF = mybir.ActivationFunctionType
ALU = mybir.AluOpType

D = 1024
DPLE = 256
NIN = 10240
H = 8
EPS = 1e-6
QK_SCALE = 128 ** -0.5
SEQ = 8192
NSEQ_S = 4
R_SLOTS = 6
NTMP = 12
N_CORES = 8

OFF = dict(vA=0, bA=1024, cA=2048, zA=3072, q=4096, f=5120, iv=6144, zB=7168, gA=8192, gB=9216)


class Buf:
    __slots__ = ("name", "w", "r")

    def __init__(self, name):
        self.name = name
        self.w = None
        self.r = {}


class Chan:
    def __init__(self, sem, key):
        self.sem = sem
        self.key = key
        self.val = 0


class Eng:
    def __init__(self, eng, sem, key, is_pe=False):
        self.eng = eng
        self.sem = sem
        self.key = key
        self.cnt = 0
        self.seen = {}
        self.is_pe = is_pe

    def wait_for(self, deps):
        need = {}
        for d in deps:
            if d is None:
                continue
            key, sem, val = d
            if self.is_pe and key == self.key:
                continue
            if val > self.seen.get(key, 0):
                if key not in need or need[key][1] < val:
                    need[key] = (sem, val)
        for key, (sem, val) in need.items():
            self.eng.wait_ge(sem, val)
            self.seen[key] = val


class T:
    def __init__(self, ap, buf):
        self.ap = ap
        self.buf = buf


class K:
    def __init__(self, nc, es):
        self.nc = nc
        self.es = es
        self.nsem = 0
        self.PE = Eng(nc.tensor, self.sem("pe"), "pe", True)
        self.ACT = Eng(nc.scalar, self.sem("act"), "act")
        self.DVE = Eng(nc.vector, self.sem("dve"), "dve")
        self.POOL = Eng(nc.gpsimd, self.sem("pool"), "pool")
        self.SP = Eng(nc.sync, None, "sp")
        self.tmp_i = 0
        self.bank_i = 0

    def sem(self, name):
        self.nsem += 1
        return self.es.enter_context(self.nc.semaphore(name))

    def chan(self, name):
        return Chan(self.sem(name), name)

    def sb(self, name, shape, dt):
        t = self.es.enter_context(self.nc.sbuf_tensor(name, shape, dt))
        return T(t[:], Buf(name))

    def ps(self, name, shape, dt):
        t = self.es.enter_context(self.nc.psum_tensor(name, shape, dt))
        return T(t[:], Buf(name))

    def _deps(self, reads, writes):
        deps = []
        for b in reads:
            deps.append(b.w)
        for b in writes:
            deps.append(b.w)
            deps.extend(b.r.values())
        return deps

    def _mark(self, stamp, reads, writes):
        for b in writes:
            b.w = stamp
            b.r = {}
        for b in reads:
            old = b.r.get(stamp[0])
            if old is None or old[2] < stamp[2]:
                b.r[stamp[0]] = stamp

    def op(self, E, fn, reads=(), writes=()):
        E.wait_for(self._deps(reads, writes))
        ins = fn(E.eng)
        E.cnt += 1
        ins.then_inc(E.sem, 1)
        self._mark((E.key, E.sem, E.cnt), reads, writes)

    def pe_group(self, fns, reads=(), writes=()):
        E = self.PE
        E.wait_for(self._deps(reads, writes))
        ins = None
        for fn in fns:
            ins = fn(E.eng)
        E.cnt += 1
        ins.then_inc(E.sem, 1)
        self._mark((E.key, E.sem, E.cnt), reads, writes)

    def dma(self, Q, fn, chan, reads=(), writes=()):
        Q.wait_for(self._deps(reads, writes))
        ins = fn(Q.eng)
        chan.val += 16
        ins.then_inc(chan.sem, 16)
        self._mark((chan.key, chan.sem, chan.val), reads, writes)


class StopBuild(Exception):
    pass


import os
_STOP = int(os.environ.get("KSTOP", "-1"))
_VAR = os.environ.get("KVAR", "")


_stage_ctr = [0]


def stage(n):
    c = _stage_ctr[0]
    _stage_ctr[0] += 1
    if _STOP == c:
        raise StopBuild()


def build(NPT, EXCH=True):
    _stage_ctr[0] = 0
    nc = bass.Bass("TRN2", target_bir_lowering=False)
    es = ExitStack()
    k = K(nc, es)
    PE, ACT, DVE, POOL, SP = k.PE, k.ACT, k.DVE, k.POOL, k.SP
    NTOK = NPT * 512

    def din(name, shape):
        return nc.dram_tensor(name, shape, F32, kind="ExternalInput").ap()

    def dout(name, shape):
        return nc.dram_tensor(name, shape, F32, kind="ExternalOutput").ap()

    xp = din("xp", [NTOK, D])
    pp = din("pp", [NTOK, DPLE])
    xs = din("xs", [NSEQ_S * 64, D])
    ps_ = din("ps", [NSEQ_S * 64, DPLE])
    sconv = din("sconv", [NSEQ_S * 2, D])
    shgrn = din("shgrn", [NSEQ_S, H, 128, 128])
    w_in = din("w_in", [D, NIN])
    conv_w = din("conv_w", [3, D])
    lb_raw = din("lb_raw", [2, D])
    g_pre = din("g_pre", [D])
    g_onorm = din("g_onorm", [D])
    w_a_out = din("w_a_out", [D, D])
    w_b_out = din("w_b_out", [D, D])
    w_o = din("w_o", [D, D])
    g_post = din("g_post", [D])
    g_ple = din("g_ple", [D])
    w_ple_gate = din("w_ple_gate", [D, D])
    w_ple_proj = din("w_ple_proj", [DPLE, D])

    xhalo = din("xhalo", [128, D])
    selm = din("selm", [128, 1])
    yp = dout("yp", [NTOK, D])
    ys = dout("ys", [NSEQ_S * 64, D])
    conv_p = dout("conv_p", [2, D])
    hgrn_p = dout("hgrn_p", [H, 128, 128])
    conv_s = dout("conv_s", [NSEQ_S * 2, D])
    hgrn_s = dout("hgrn_s", [NSEQ_S, H, 128, 128])

    def win(sec, half):
        return ("w_in", w_in, OFF[sec] * 1 + half * 512)

    stream = []
    stream += [("q%d" % i, w_in, OFF["q"] + i * 512) for i in range(2)]
    stream += [("f%d" % i, w_in, OFF["f"] + i * 512) for i in range(2)]
    stream += [("iv%d" % i, w_in, OFF["iv"] + i * 512) for i in range(2)]
    for s in range(2):
        for sec in ("vA", "cA", "zA", "bA"):
            stream.append(("%s%d" % (sec, s), w_in, OFF[sec] + s * 512))
    stream += [("zB%d" % i, w_in, OFF["zB"] + i * 512) for i in range(2)]
    for s in range(2):
        stream.append(("gA%d" % s, w_in, OFF["gA"] + s * 512))
        stream.append(("wao%d" % s, w_a_out, s * 512))
        stream.append(("gB%d" % s, w_in, OFF["gB"] + s * 512))
        stream.append(("wbo%d" % s, w_b_out, s * 512))
    stream += [("wo%d" % i, w_o, i * 512) for i in range(2)]
    stream += [("wg%d" % i, w_ple_gate, i * 512) for i in range(2)]
    stream.append(("ple", w_ple_proj, 0))
    NG = len(stream)
    wsc = nc.dram_tensor("wsc", [NG, 128, 4096], BF16, kind="Internal").ap()
    wsc_buf = [Buf("wsc%d" % g) for g in range(NG)]

    ident_bf = k.sb("ident_bf", [128, 128], BF16)
    identf = k.sb("identf", [128, 128], F32)
    ones_bf = k.sb("ones_bf", [128, 128], BF16)
    mask2 = k.sb("mask2", [128, 128], F32)
    rmask = k.sb("rmask", [128, 512], F32)
    gpre32 = k.sb("gpre32", [128, 8], F32)
    gon = k.sb("gon", [128, 8], F32)
    gple32 = k.sb("gple32", [128, 8], F32)
    lbt = k.sb("lbt", [128, 8], F32)
    omlt = k.sb("omlt", [128, 8], F32)
    nomlt = k.sb("nomlt", [128, 8], F32)
    lbr = k.sb("lbr", [128, 16], F32)
    cw = k.sb("cw", [128, 24], F32)
    gpost32 = k.sb("gpost32", [128, 1024], F32)
    dcy = k.sb("dcy", [128, 64], F32)
    uh = k.sb("uh", [128, 16], F32)
    uhs = k.sb("uhs", [128, 64], F32)
    uo = k.sb("uo", [128, 64], F32)
    smalls = [k.sb("small%d" % i, [128, 8], F32) for i in range(8)]
    sqb = k.sb("sqb", [128, 1024], BF16)
    junk = sqb

    wring = [k.sb("wring%d" % i, [128, 4096], BF16) for i in range(R_SLOTS)]
    wchan = [k.chan("wch%d" % i) for i in range(R_SLOTS)]
    hT = k.sb("hT", [128, 4096], BF16)
    bigA = k.sb("bigA", [128, 4096], BF16)
    bigB = k.sb("bigB", [128, 4096], BF16)
    bigC = k.sb("bigC", [128, 4096], BF16)
    bigD = k.sb("bigD", [128, 4096], BF16)
    bigE = k.sb("bigE", [128, 4096], BF16)
    kv = k.es.enter_context(nc.sbuf_tensor("kv", [128, 8192], BF16))
    kd_tm = T(kv[:, 0:4096], Buf("kd_tm"))
    v_tm = T(kv[:, 4096:8192], Buf("v_tm"))
    sigf_ap = kv[:].bitcast(F32)
    oT = k.sb("oT", [128, 4096], F32)
    tmps = [k.sb("tmp%d" % i, [128, 520], F32) for i in range(NTMP)]
    xin = [k.sb("xin%d" % i, [128, 1024], F32) for i in range(2)]
    xin_ch = [k.chan("xin_ch%d" % i) for i in range(2)]
    sc_in = T(xin[0].ap[0:8, :], xin[0].buf)
    cv_out = T(xin[1].ap[0:8, :], xin[1].buf)
    hn = [k.sb("hn%d" % i, [128, 1024], BF16) for i in range(4)]
    Sf = [k.sb("Sf%d" % i, [128, 1024], F32) for i in range(2)]
    Sb = [k.sb("Sb%d" % i, [128, 1024], BF16) for i in range(2)]
    s_ld_ch = [k.chan("sld%d" % i) for i in range(2)]
    s_st_ch = [k.chan("sst%d" % i) for i in range(2)]
    xr = xin
    xr_ch = xin_ch
    n2 = [k.sb("n2_%d" % i, [128, 1024], BF16) for i in range(2)]
    n2T = [k.sb("n2T%d" % i, [128, 1024], BF16) for i in range(2)]
    pt = [k.sb("pt%d" % i, [128, 256], F32) for i in range(2)]
    pt_ch = [k.chan("pt_ch%d" % i) for i in range(2)]
    ptb = [k.sb("ptb%d" % i, [128, 256], BF16) for i in range(2)]
    pT = [k.sb("pT%d" % i, [128, 256], BF16) for i in range(2)]
    y_ch = [k.chan("y_ch%d" % i) for i in range(4)]
    cst = k.chan("cst")
    misc_ch = k.chan("misc")
    conv_ch = [k.chan("cv%d" % g) for g in range(NG)]

    banks = [k.ps("bank%d" % i, [128, 512], F32) for i in range(8)]
    PO = banks[6:8]
    GB = banks[0:6]

    def bank():
        b = GB[k.bank_i % len(GB)]
        k.bank_i += 1
        return b

    def tmp():
        t = tmps[k.tmp_i % NTMP]
        k.tmp_i += 1
        return t

    small_i = [0]

    def small():
        t = smalls[small_i[0] % len(smalls)]
        small_i[0] += 1
        return t

    ws = {"issued": 0, "consumed": 0}
    gid = {nm: g for g, (nm, _, _) in enumerate(stream)}
    full_list = [nm for (nm, _, _) in stream]
    seq = []
    if EXCH:
        seq += ["f0", "f1", "iv0", "iv1"]
        seq += ["vA0", "cA0", "vA1", "cA1"]
    for _t in range(NPT + 1):
        pre_f = (_t >= 1) and (_t < NPT)
        nxt_pre = (_t + 1 < NPT)
        for nm in full_list:
            if nm in ("f0", "f1") and pre_f:
                continue
            seq.append(nm)
            if nm == "wo1" and nxt_pre:
                seq += ["f0", "f1"]

    def wissue_upto(limit):
        while ws["issued"] < min(limit, len(seq)):
            i = ws["issued"]
            g = gid[seq[i]]
            slot = i % R_SLOTS
            name = stream[g][0]
            if name == "ple":
                k.dma(SP, lambda e, slot=slot, g=g: e.dma_start(out=wring[slot].ap[:, 0:2048], in_=wsc[g][:, 0:2048]),
                      wchan[slot], reads=(wsc_buf[g],), writes=(wring[slot].buf,))
            else:
                k.dma(SP, lambda e, slot=slot, g=g: e.dma_start(out=wring[slot].ap, in_=wsc[g]),
                      wchan[slot], reads=(wsc_buf[g],), writes=(wring[slot].buf,))
            ws["issued"] += 1

    def wfetch(expect):
        i = ws["consumed"]
        assert seq[i] == expect, (seq[i], expect)
        ws["consumed"] += 1
        wissue_upto(i + R_SLOTS - 3)
        slot = wring[i % R_SLOTS]
        if expect == "ple":
            v = slot.ap[:, 0:2048].rearrange("p (kc c) -> p kc c", c=1024)
        else:
            v = slot.ap.rearrange("p (kc c) -> p kc c", c=512)
        return T(v, slot.buf)

    def fm(vec):
        return vec.rearrange("(c p) -> p c", p=128)

    raw_g = k.sb("raw_g", [128, 40], F32)
    cdmas = []

    def cdma(out_ap, in_ap):
        ins = nc.sync.dma_start(out=out_ap, in_=in_ap, allow_slow_non_contiguous=True)
        cst.val += 16
        ins.then_inc(cst.sem, 16)

    cdma(raw_g.ap[:, 0:8], fm(g_pre))
    cdma(raw_g.ap[:, 8:16], fm(g_onorm))
    cdma(raw_g.ap[:, 16:24], fm(g_ple))
    cdma(lbr.ap[:, 0:8], fm(lb_raw[0]))
    cdma(lbr.ap[:, 8:16], fm(lb_raw[1]))
    for t_ in range(3):
        cdma(cw.ap.rearrange("p (j t) -> p j t", t=3)[:, :, t_], fm(conv_w[t_]))
    cdma(gpost32.ap, g_post.partition_broadcast(128))
    cstamp = (cst.key, cst.sem, cst.val)
    for t_ in (raw_g, lbr, cw, gpost32):
        t_.buf.w = cstamp

    k.op(DVE, lambda e: e.memset(identf.ap, 1.0), writes=(identf.buf,))
    k.op(POOL, lambda e: e.affine_select(out=identf.ap, in_=identf.ap, pattern=[[-1, 128]],
                                         compare_op=ALU.is_equal, fill=0.0, base=0, channel_multiplier=1),
         reads=(identf.buf,), writes=(identf.buf,))
    k.op(DVE, lambda e: e.tensor_copy(out=ident_bf.ap, in_=identf.ap), reads=(identf.buf,), writes=(ident_bf.buf,))
    k.op(DVE, lambda e: e.memset(ones_bf.ap, 1.0), writes=(ones_bf.buf,))
    k.op(DVE, lambda e: e.memset(mask2.ap, 1.0), writes=(mask2.buf,))
    k.op(POOL, lambda e: e.affine_select(out=mask2.ap, in_=mask2.ap, pattern=[[1, 128]],
                                         compare_op=ALU.is_ge, fill=0.0, base=0, channel_multiplier=-1),
         reads=(mask2.buf,), writes=(mask2.buf,))
    k.op(DVE, lambda e: e.memset(mask2.ap[0:64, 64:128], 0.0), reads=(mask2.buf,), writes=(mask2.buf,))
    k.op(DVE, lambda e: e.memset(rmask.ap, 1.0), writes=(rmask.buf,))
    k.op(DVE, lambda e: e.memset(rmask.ap.rearrange("p (c t) -> p c t", t=64)[:, :, 0:1], 0.0),
         reads=(rmask.buf,), writes=(rmask.buf,))
    k.op(DVE, lambda e: e.tensor_scalar(out=gpre32.ap, in0=raw_g.ap[:, 0:8], scalar1=32.0, scalar2=None, op0=ALU.mult),
         reads=(raw_g.buf,), writes=(gpre32.buf,))
    k.op(DVE, lambda e: e.tensor_scalar(out=gon.ap, in0=raw_g.ap[:, 8:16], scalar1=float(128 ** 0.5), scalar2=None,
                                         op0=ALU.mult), reads=(raw_g.buf,), writes=(gon.buf,))
    k.op(DVE, lambda e: e.tensor_scalar(out=gple32.ap, in0=raw_g.ap[:, 16:24], scalar1=32.0, scalar2=None,
                                         op0=ALU.mult), reads=(raw_g.buf,), writes=(gple32.buf,))
    k.op(DVE, lambda e: e.tensor_scalar(out=gpost32.ap, in0=gpost32.ap, scalar1=32.0, scalar2=None, op0=ALU.mult),
         reads=(gpost32.buf,), writes=(gpost32.buf,))
    k.op(DVE, lambda e: e.tensor_tensor(out=lbr.ap[:, 0:8], in0=lbr.ap[:, 0:8], in1=lbr.ap[:, 8:16], op=ALU.subtract),
         reads=(lbr.buf,), writes=(lbr.buf,))
    k.op(ACT, lambda e: e.activation(out=lbt.ap, in_=lbr.ap[:, 0:8], func=AF.Sigmoid), reads=(lbr.buf,), writes=(lbt.buf,))
    k.op(DVE, lambda e: e.tensor_scalar(out=omlt.ap, in0=lbt.ap, scalar1=-1.0, scalar2=1.0, op0=ALU.mult, op1=ALU.add),
         reads=(lbt.buf,), writes=(omlt.buf,))
    k.op(DVE, lambda e: e.tensor_scalar(out=nomlt.ap, in0=omlt.ap, scalar1=-1.0, scalar2=None, op0=ALU.mult),
         reads=(omlt.buf,), writes=(nomlt.buf,))
    k.op(DVE, lambda e: e.memset(uh.ap, 0.0), writes=(uh.buf,))
    k.op(DVE, lambda e: e.memset(Sf[0].ap, 0.0), writes=(Sf[0].buf,))
    k.op(DVE, lambda e: e.memset(Sb[0].ap, 0.0), writes=(Sb[0].buf,))

    conv_order = [g for g, (nm, _, _) in enumerate(stream) if nm[:1] == "f" or nm[:2] == "iv"]
    conv_order += [g for g, (nm, _, _) in enumerate(stream) if nm[:2] in ("vA", "cA")]
    conv_order += [g for g in range(len(stream)) if g not in conv_order]
    conv_state = {"i": 0}

    def convert_next(n, gate_bufs=()):
        for _ in range(n):
            if conv_state["i"] >= len(conv_order):
                return
            g = conv_order[conv_state["i"]]
            conv_state["i"] += 1
            name, W, c0 = stream[g]
            if name == "ple":
                src = W.rearrange("(kc p) c -> p kc c", p=128)
                dst = wsc[g][:, 0:2048].rearrange("p (kc c) -> p kc c", c=1024)
            else:
                src = W[:, c0:c0 + 512].rearrange("(kc p) c -> p kc c", p=128)
                dst = wsc[g].rearrange("p (kc c) -> p kc c", c=512)
            k.dma(POOL, lambda e, dst=dst, src=src: e.dma_start(out=dst, in_=src), conv_ch[g],
                  reads=tuple(gate_bufs), writes=(wsc_buf[g],))

    convert_next(len(conv_order))

    def rstd_from_ssq(ssq_t, ncols, n_eps):
        l_ = small()
        r_ = small()
        k.op(ACT, lambda e: e.activation(out=l_.ap[:, 0:ncols], in_=ssq_t.ap[:, 0:ncols], func=AF.Ln, bias=float(n_eps), scale=1.0),
             reads=(ssq_t.buf,), writes=(l_.buf,))
        k.op(ACT, lambda e: e.activation(out=r_.ap[:, 0:ncols], in_=l_.ap[:, 0:ncols], func=AF.Exp, scale=-0.5),
             reads=(l_.buf,), writes=(r_.buf,))
        return r_

    def proj_fm(slot, cc, NT, rhs_t, rhs_view):
        b = bank()
        k.pe_group([(lambda e, kc=kc: e.matmul(b.ap[:, 0:NT], lhsT=slot.ap[:, kc, cc * 128:(cc + 1) * 128],
                                               rhs=rhs_view[:, kc, :], start=(kc == 0), stop=(kc == 7)))
                    for kc in range(8)],
                   reads=(slot.buf, rhs_t.buf), writes=(b.buf,))
        return b

    def s1_stats(x_d, row0, NB):
        for b in range(NB):
            xb = xin[b % 2]
            k.dma(SP, lambda e: e.dma_start(out=xb.ap, in_=x_d[row0 + b * 128: row0 + (b + 1) * 128, :]),
                  xin_ch[b % 2], writes=(xb.buf,))
            ssq = small()
            k.op(ACT, lambda e: e.activation(out=junk.ap, in_=xb.ap, func=AF.Square, accum_out=ssq.ap[:, 0:1]),
                 reads=(xb.buf,), writes=(junk.buf, ssq.buf))
            r_ = rstd_from_ssq(ssq, 1, D * EPS)
            hb = hn[b]
            k.op(DVE, lambda e: e.tensor_scalar(out=hb.ap, in0=xb.ap, scalar1=r_.ap[:, 0:1], scalar2=None, op0=ALU.mult),
                 reads=(xb.buf, r_.buf), writes=(hb.buf,))

    def s1_xpose(NB, NT):
        hT3 = hT.ap[:, 0:8 * NT].rearrange("p (k t) -> p k t", t=NT)
        for b in range(NB):
            hb = hn[b]
            bk = bank()
            pv = bk.ap.bitcast(BF16)
            k.pe_group([(lambda e, kc=kc: e.transpose(out=pv[:, kc * 128:(kc + 1) * 128],
                                                      in_=hb.ap[:, kc * 128:(kc + 1) * 128], identity=ident_bf.ap))
                        for kc in range(8)], reads=(hb.buf, ident_bf.buf), writes=(bk.buf,))
            k.op(DVE, lambda e: e.tensor_tensor(out=hT3[:, :, b * 128:(b + 1) * 128],
                                                in0=pv.rearrange("p (k t) -> p k t", t=128),
                                                in1=gpre32.ap.unsqueeze(2).to_broadcast([128, 8, 128]), op=ALU.mult),
                 reads=(bk.buf, gpre32.buf), writes=(hT.buf,))

    def s1(x_d, row0, NB, NT):
        s1_stats(x_d, row0, NB)
        s1_xpose(NB, NT)

    def s2_f(NT):
        hT3 = hT.ap[:, 0:8 * NT].rearrange("p (k t) -> p k t", t=NT)
        sigf3 = sigf_ap[:, 0:8 * NT].rearrange("p (k t) -> p k t", t=NT)
        for g in range(2):
            slot = wfetch("f%d" % g)
            for cc in range(4):
                h = g * 4 + cc
                b_ = proj_fm(slot, cc, NT, hT, hT3)
                k.op(ACT, lambda e: e.activation(out=sigf3[:, h, :], in_=b_.ap[:, 0:NT], func=AF.Sigmoid),
                     reads=(b_.buf,), writes=(kd_tm.buf, v_tm.buf))

    pre = {"stats": False, "xposed": False, "f": False, "recv": None}

    state = {"S": 0, "B": 0}

    def tile(x_d, p_d, y_d, row0, NT, sample, nxt=None):
        NB = NT // 128
        NCH = NT // 64
        nseq = NSEQ_S if sample else 1
        L = NT // nseq

        def v3(ap, n=NT):
            return ap[:, 0:8 * n].rearrange("p (k t) -> p k t", t=n)

        hT3 = v3(hT.ap)
        qe3, ke3, qb3, kdT3 = v3(bigA.ap), v3(bigB.ap), v3(bigC.ap), v3(bigD.ap)
        gated3, og3, merged3 = v3(bigE.ap), ke3, qb3
        attm4 = bigD.ap[:, 0:NB * 1024].rearrange("p (b h t) -> p b h t", h=8, t=128)
        kd3 = kd_tm.ap[:, 0:NB * 1024].rearrange("p (b c) -> p b c", c=1024)
        vt3 = v_tm.ap[:, 0:NB * 1024].rearrange("p (b c) -> p b c", c=1024)
        sigf3 = v3(sigf_ap)
        sq3 = v3(oT.ap)
        oT3 = v3(oT.ap)
        outs3 = oT.ap[:, 0:NB * 1024].rearrange("p (b c) -> p b c", c=1024)
        dcy3 = dcy.ap.rearrange("p (h c) -> p h c", c=8)

        if not pre["stats"]:
            s1_stats(x_d, row0, NB)
        if not pre["xposed"]:
            s1_xpose(NB, NT)
        pre["stats"] = pre["xposed"] = False
        stage(1)
        for g in range(2):
            slot = wfetch("q%d" % g)
            for cc in range(4):
                h = g * 4 + cc
                b_ = proj_fm(slot, cc, NT, hT, hT3)
                sg = tmp()
                k.op(ACT, lambda e: e.activation(out=sg.ap[:, 0:NT], in_=b_.ap[:, 0:NT], func=AF.Sigmoid),
                     reads=(b_.buf,), writes=(sg.buf,))
                k.op(DVE, lambda e: e.scalar_tensor_tensor(out=sq3[:, h, :], in0=b_.ap[:, 0:NT], scalar=float(QK_SCALE),
                                                           in1=sg.ap[:, 0:NT], op0=ALU.mult, op1=ALU.mult),
                     reads=(b_.buf, sg.buf), writes=(oT.buf,))
        if not pre["f"]:
            s2_f(NT)
        pre["f"] = False

        ivs = [wfetch("iv0"), wfetch("iv1")]
        iv_items = [(g, blk) for g in range(2) for blk in range(NB)]

        def iv_group(g, blk):
            b_ = bank()
            k.pe_group([(lambda e, kc=kc: e.matmul(b_.ap[:, 0:512], lhsT=hT3[:, kc, blk * 128:(blk + 1) * 128],
                                                   rhs=ivs[g].ap[:, kc, :], start=(kc == 0), stop=(kc == 7)))
                        for kc in range(8)], reads=(ivs[g].buf, hT.buf), writes=(b_.buf,))
            k.op(ACT, lambda e: e.copy(out=hn[blk].ap[:, g * 512:(g + 1) * 512], in_=b_.ap[:, 0:512]),
                 reads=(b_.buf,), writes=(hn[blk].buf,))

        stage(2)
        def c3(ap):
            return ap[:, 0:NT].rearrange("p (c t) -> p c t", t=64)

        s3t = {}

        def s3_A(h):
            for it_ in iv_items[h * len(iv_items) // 8:(h + 1) * len(iv_items) // 8]:
                iv_group(*it_)
            Lg, B_, BM, BL, KK = [tmp() for _ in range(5)]
            s3t[h] = (Lg, B_, BM, BL, KK)
            k.op(ACT, lambda e: e.activation(out=Lg.ap[:, 0:NT], in_=sigf3[:, h, :], func=AF.Ln,
                                             bias=lbt.ap[:, h:h + 1], scale=omlt.ap[:, h:h + 1]),
                 reads=(kd_tm.buf, v_tm.buf, lbt.buf, omlt.buf), writes=(Lg.buf,))
            k.op(DVE, lambda e: e.tensor_scalar(out=KK.ap[:, 0:NT], in0=sigf3[:, h, :], scalar1=nomlt.ap[:, h:h + 1],
                                                scalar2=omlt.ap[:, h:h + 1], op0=ALU.mult, op1=ALU.add),
                 reads=(kd_tm.buf, v_tm.buf, nomlt.buf, omlt.buf), writes=(KK.buf,))
            k.op(DVE, lambda e: e.tensor_tensor_scan(out=B_.ap[:, 0:NT], data0=rmask.ap[:, 0:NT], data1=Lg.ap[:, 0:NT],
                                                     initial=0.0, op0=ALU.mult, op1=ALU.add),
                 reads=(rmask.buf, Lg.buf), writes=(B_.buf,))
            k.op(DVE, lambda e: e.tensor_tensor(out=c3(BM.ap), in0=c3(B_.ap),
                                                in1=c3(B_.ap)[:, :, 31:32].to_broadcast([128, NCH, 64]), op=ALU.subtract),
                 reads=(B_.buf,), writes=(BM.buf,))
            k.op(DVE, lambda e: e.tensor_tensor(out=c3(BL.ap), in0=c3(B_.ap)[:, :, 63:64].to_broadcast([128, NCH, 64]),
                                                in1=c3(B_.ap), op=ALU.subtract),
                 reads=(B_.buf,), writes=(BL.buf,))

        def s3_B(h):
            Lg, B_, BM, BL, KK = s3t.pop(h)
            k.op(ACT, lambda e: e.activation(out=Lg.ap[:, 0:NT], in_=BM.ap[:, 0:NT], func=AF.Exp, scale=-1.0),
                 reads=(BM.buf,), writes=(Lg.buf,))
            k.op(ACT, lambda e: e.activation(out=BM.ap[:, 0:NT], in_=BM.ap[:, 0:NT], func=AF.Exp), reads=(BM.buf,), writes=(BM.buf,))
            k.op(ACT, lambda e: e.activation(out=dcy3[:, h, 0:NCH], in_=c3(B_.ap)[:, :, 63], func=AF.Exp),
                 reads=(B_.buf,), writes=(dcy.buf,))
            k.op(ACT, lambda e: e.activation(out=B_.ap[:, 0:NT], in_=B_.ap[:, 0:NT], func=AF.Exp), reads=(B_.buf,), writes=(B_.buf,))
            k.op(ACT, lambda e: e.activation(out=BL.ap[:, 0:NT], in_=BL.ap[:, 0:NT], func=AF.Exp), reads=(BL.buf,), writes=(BL.buf,))
            k.op(DVE, lambda e: e.tensor_tensor(out=qe3[:, h, :], in0=sq3[:, h, :], in1=BM.ap[:, 0:NT], op=ALU.mult),
                 reads=(oT.buf, BM.buf), writes=(bigA.buf,))
            k.op(DVE, lambda e: e.tensor_tensor(out=qb3[:, h, :], in0=sq3[:, h, :], in1=B_.ap[:, 0:NT], op=ALU.mult),
                 reads=(oT.buf, B_.buf), writes=(bigC.buf,))
            k.op(POOL, lambda e: e.tensor_tensor(out=ke3[:, h, :], in0=KK.ap[:, 0:NT], in1=Lg.ap[:, 0:NT], op=ALU.mult),
                 reads=(KK.buf, Lg.buf), writes=(bigB.buf,))
            k.op(POOL, lambda e: e.tensor_tensor(out=kdT3[:, h, :], in0=KK.ap[:, 0:NT], in1=BL.ap[:, 0:NT], op=ALU.mult),
                 reads=(KK.buf, BL.buf), writes=(bigD.buf,))

        s3_A(0)
        for h in range(8):
            if h + 1 < 8:
                s3_A(h + 1)
            s3_B(h)

        stage(4)
        for blk in range(NB):
            bk = bank()
            pv = bk.ap.bitcast(BF16)
            k.pe_group([(lambda e, h=h: e.transpose(out=pv[:, h * 128:(h + 1) * 128],
                                                    in_=kdT3[:, h, blk * 128:(blk + 1) * 128], identity=ident_bf.ap))
                        for h in range(8)], reads=(bigD.buf, ident_bf.buf), writes=(bk.buf,))
            k.op(DVE, lambda e: e.tensor_copy(out=kd3[:, blk, :], in_=pv), reads=(bk.buf,), writes=(kd_tm.buf,))

        stage(5)
        for p in range(NB):
            for hg in range(2):
                b_ = bank()
                k.pe_group([(lambda e, hh=hh: e.matmul(b_.ap[:, hh * 128:(hh + 1) * 128],
                                                       lhsT=ke3[:, hg * 4 + hh, p * 128:(p + 1) * 128],
                                                       rhs=qe3[:, hg * 4 + hh, p * 128:(p + 1) * 128], start=True, stop=True))
                            for hh in range(4)], reads=(bigA.buf, bigB.buf), writes=(b_.buf,))
                k.op(DVE, lambda e: e.tensor_tensor(out=attm4[:, p, hg * 4:(hg + 1) * 4, :],
                                                    in0=b_.ap.rearrange("p (h t) -> p h t", t=128),
                                                    in1=mask2.ap.unsqueeze(1).to_broadcast([128, 4, 128]), op=ALU.mult),
                     reads=(b_.buf, mask2.buf), writes=(bigD.buf,))

        aw = {}

        abk = {}

        def brA_pe(j):
            s_, jj = j // 4, j % 4
            if jj == 0:
                aw["s"] = (wfetch("vA%d" % s_), wfetch("cA%d" % s_), wfetch("zA%d" % s_), wfetch("bA%d" % s_))
            sv, sc_, sz, sb_ = aw["s"]
            abk[j] = (proj_fm(sv, jj, NT, hT, hT3), proj_fm(sc_, jj, NT, hT, hT3),
                      proj_fm(sz, jj, NT, hT, hT3), proj_fm(sb_, jj, NT, hT, hT3))

        atm = {}

        def brA_early(j):
            bv, bc, bz, bb = abk.pop(j)
            vAs, sgz, u, t1, t2 = tmp(), tmp(), tmp(), tmp(), tmp()
            atm[j] = (sgz, u, t1, t2)
            u3 = u.ap[:, 0:nseq * (L + 2)].rearrange("p (s l) -> p s l", l=L + 2)

            def s3(ap):
                return ap[:, 0:NT].rearrange("p (s l) -> p s l", l=L)

            k.op(ACT, lambda e: e.copy(out=vAs.ap[:, 0:NT], in_=bv.ap[:, 0:NT]), reads=(bv.buf,), writes=(vAs.buf,))
            k.op(ACT, lambda e: e.activation(out=sgz.ap[:, 0:NT], in_=bz.ap[:, 0:NT], func=AF.Sigmoid),
                 reads=(bz.buf,), writes=(sgz.buf,))
            k.op(DVE, lambda e: e.tensor_tensor(out=u3[:, :, 2:2 + L], in0=s3(vAs.ap), in1=s3(bc.ap), op=ALU.mult),
                 reads=(vAs.buf, bc.buf), writes=(u.buf,))
            k.op(DVE, lambda e: e.tensor_tensor(out=sgz.ap[:, 0:NT], in0=sgz.ap[:, 0:NT], in1=bz.ap[:, 0:NT], op=ALU.mult),
                 reads=(sgz.buf, bz.buf), writes=(sgz.buf,))
            k.op(DVE, lambda e: e.tensor_tensor(out=sgz.ap[:, 0:NT], in0=sgz.ap[:, 0:NT], in1=bb.ap[:, 0:NT], op=ALU.mult),
                 reads=(sgz.buf, bb.buf), writes=(sgz.buf,))

        def brA_late(j):
            sgz, u, t1, t2 = atm.pop(j)
            u3 = u.ap[:, 0:nseq * (L + 2)].rearrange("p (s l) -> p s l", l=L + 2)

            def s3(ap):
                return ap[:, 0:NT].rearrange("p (s l) -> p s l", l=L)

            if sample:
                hist_src = uhs.ap.rearrange("p (j s r) -> p j s r", s=NSEQ_S, r=2)[:, j, :, :]
                hist_buf = uhs.buf
            else:
                hist_src = uh.ap.rearrange("p (j s r) -> p j s r", s=1, r=2)[:, j, :, :]
                hist_buf = uh.buf
            k.op(POOL, lambda e: e.tensor_copy(out=u3[:, :, 0:2], in_=hist_src), reads=(hist_buf, u.buf), writes=(u.buf,))
            if sample:
                dst = uo.ap.rearrange("p (j s r) -> p j s r", s=NSEQ_S, r=2)[:, j, :, :]
                k.op(POOL, lambda e: e.tensor_copy(out=dst, in_=u3[:, :, L:L + 2]), reads=(u.buf,), writes=(uo.buf,))
            else:
                dst = uh.ap.rearrange("p (j s r) -> p j s r", s=1, r=2)[:, j, :, :]
                k.op(POOL, lambda e: e.tensor_copy(out=dst, in_=u3[:, :, L:L + 2]), reads=(u.buf,), writes=(uh.buf,))
            cw3 = cw.ap.rearrange("p (j t) -> p j t", t=3)
            k.op(ACT, lambda e: e.activation(out=s3(t1.ap), in_=u3[:, :, 0:L], func=AF.Identity, scale=cw3[:, j, 0:1]),
                 reads=(u.buf, cw.buf), writes=(t1.buf,))
            k.op(DVE, lambda e: e.scalar_tensor_tensor(out=s3(t2.ap), in0=u3[:, :, 1:1 + L], scalar=cw3[:, j, 1:2],
                                                       in1=s3(t1.ap), op0=ALU.mult, op1=ALU.add),
                 reads=(u.buf, cw.buf, t1.buf), writes=(t2.buf,))
            k.op(DVE, lambda e: e.scalar_tensor_tensor(out=s3(t1.ap), in0=u3[:, :, 2:2 + L], scalar=cw3[:, j, 2:3],
                                                       in1=s3(t2.ap), op0=ALU.mult, op1=ALU.add),
                 reads=(u.buf, cw.buf, t2.buf), writes=(t1.buf,))
            k.op(POOL, lambda e: e.tensor_tensor(out=gated3[:, j, :], in0=sgz.ap[:, 0:NT], in1=t1.ap[:, 0:NT], op=ALU.mult),
                 reads=(t1.buf, sgz.buf), writes=(bigE.buf,))

        def brA_ew(j):
            brA_early(j)
            brA_late(j)

        stage(6)
        if pre["recv"] is not None:
            pre["recv"]()
            pre["recv"] = None
        def po_view(hh_bank, h, c0, n):
            return PO[hh_bank].ap[:, (h % 4) * 128 + c0:(h % 4) * 128 + c0 + n]

        for p in range(NB):
            for half in range(2):
                c = 2 * p + half
                if 8 // NCH == 1 and c >= 1:
                    brA_early(c - 1)
                if sample:
                    si = c % 2
                    state["S"] = si
                    state["B"] = si
                    k.dma(SP, lambda e: e.dma_start(out=Sf[si].ap.rearrange("p (h v) -> p h v", v=128),
                                                    in_=shgrn[c].rearrange("h k v -> k h v")),
                          s_ld_ch[si], writes=(Sf[si].buf,))
                    k.op(ACT, lambda e: e.copy(out=Sb[si].ap, in_=Sf[si].ap), reads=(Sf[si].buf,), writes=(Sb[si].buf,))
                si = state["S"]
                S_f, S_b = Sf[si], Sb[state["B"]]
                lo, hi = half * 64, (half + 1) * 64
                fns = []
                for h in range(8):
                    fns.append(lambda e, h=h: e.matmul(po_view(h // 4, h, lo, 64), lhsT=S_b.ap[:, h * 128:(h + 1) * 128],
                                                       rhs=qb3[:, h, c * 64:(c + 1) * 64], start=True, stop=False))
                    fns.append(lambda e, h=h: e.matmul(po_view(h // 4, h, lo, 64), lhsT=hn[p].ap[:, h * 128:(h + 1) * 128],
                                                       rhs=attm4[:, p, h, lo:hi], start=False, stop=True))
                k.pe_group(fns, reads=(S_b.buf, bigC.buf, hn[p].buf, bigD.buf), writes=(PO[0].buf, PO[1].buf))
                pb = [bank(), bank()]
                k.pe_group([(lambda e, h=h: e.matmul(pb[h // 4].ap[:, (h % 4) * 128:(h % 4 + 1) * 128],
                                                     lhsT=kd3[lo:hi, p, h * 128:(h + 1) * 128],
                                                     rhs=hn[p].ap[lo:hi, h * 128:(h + 1) * 128], start=True, stop=True))
                            for h in range(8)], reads=(kd_tm.buf, hn[p].buf), writes=(pb[0].buf, pb[1].buf))
                for h in range(8):
                    k.op(DVE, lambda e: e.scalar_tensor_tensor(out=S_f.ap[:, h * 128:(h + 1) * 128],
                                                               in0=S_f.ap[:, h * 128:(h + 1) * 128],
                                                               scalar=dcy3[:, h, c:c + 1],
                                                               in1=pb[h // 4].ap[:, (h % 4) * 128:(h % 4 + 1) * 128],
                                                               op0=ALU.mult, op1=ALU.add),
                         reads=(S_f.buf, dcy.buf, pb[h // 4].buf), writes=(S_f.buf,))
                if sample:
                    k.dma(SP, lambda e: e.dma_start(out=hgrn_s[c].rearrange("h k v -> k h v"),
                                                    in_=S_f.ap.rearrange("p (h v) -> p h v", v=128)),
                          s_st_ch[si], reads=(S_f.buf,))
                else:
                    nb2 = 1 - state["B"]
                    k.op(ACT, lambda e: e.copy(out=Sb[nb2].ap, in_=S_f.ap), reads=(S_f.buf,), writes=(Sb[nb2].buf,))
                    state["B"] = nb2
                cps = 8 // NCH
                if cps == 1:
                    if c >= 1:
                        brA_late(c - 1)
                    brA_pe(c)
                else:
                    for jx in range(cps):
                        brA_pe(c * cps + jx)
                        brA_ew(c * cps + jx)
            for hg in range(2):
                k.op(DVE, lambda e: e.tensor_copy(out=oT3[:, hg * 4:(hg + 1) * 4, p * 128:(p + 1) * 128],
                                                  in_=PO[hg].ap.rearrange("p (h t) -> p h t", t=128)),
                     reads=(PO[hg].buf,), writes=(oT.buf,))

        if 8 // NCH == 1:
            brA_ew(7)
        stage(7)
        items = [(p, hg) for p in range(NB) for hg in range(2)]

        def sl_of(it):
            p, hg = it
            return oT3[:, hg * 4:(hg + 1) * 4, p * 128:(p + 1) * 128]

        def sq_emit(it):
            sqh = sqb.ap[:, it[1] * 512:(it[1] + 1) * 512]
            k.op(DVE, lambda e: e.tensor_tensor(out=sqh.rearrange("p (h t) -> p h t", t=128), in0=sl_of(it), in1=sl_of(it),
                                                op=ALU.mult), reads=(oT.buf,), writes=(sqb_h[it[1]], sqb.buf))

        sqb_h = [Buf("sqb_h0"), Buf("sqb_h1")]
        zr = [tmps[i] for i in range(8)]
        lr = [tmps[8 + i] for i in range(4)]
        zw = {}
        hpi = 8 // len(items)

        def zb_head(h):
            if h % 4 == 0:
                zw["s"] = wfetch("zB%d" % (h // 4))
            b_ = proj_fm(zw["s"], h % 4, NT, hT, hT3)
            k.op(DVE, lambda e: e.tensor_copy(out=zr[h].ap[:, 0:NT], in_=b_.ap[:, 0:NT]), reads=(b_.buf,), writes=(zr[h].buf,))

        sq_emit(items[0])
        for ii, it in enumerate(items):
            if ii + 1 < len(items):
                sq_emit(items[ii + 1])
            sqh = sqb.ap[:, it[1] * 512:(it[1] + 1) * 512]
            b_ = bank()
            k.pe_group([lambda e: e.matmul(b_.ap, lhsT=ones_bf.ap, rhs=sqh, start=True, stop=True)],
                       reads=(ones_bf.buf, sqb_h[it[1]]), writes=(b_.buf,))
            for hx in range(hpi):
                zb_head(ii * hpi + hx)
            l_, r_ = lr[(2 * ii) % 4], lr[(2 * ii + 1) % 4]
            k.op(ACT, lambda e: e.activation(out=l_.ap[:, 0:512], in_=b_.ap, func=AF.Ln, bias=float(128 * EPS), scale=1.0),
                 reads=(b_.buf,), writes=(l_.buf,))
            k.op(ACT, lambda e: e.activation(out=r_.ap[:, 0:512], in_=l_.ap[:, 0:512], func=AF.Exp, scale=-0.5),
                 reads=(l_.buf,), writes=(r_.buf,))
            k.op(DVE, lambda e: e.tensor_tensor(out=sl_of(it), in0=sl_of(it),
                                                in1=r_.ap[:, 0:512].rearrange("p (h t) -> p h t", t=128), op=ALU.mult),
                 reads=(oT.buf, r_.buf), writes=(oT.buf,))

        if nxt is not None:
            s1_stats(nxt[0], nxt[1], nxt[2] // 128)
            pre["stats"] = True
        stage(8)
        for h in range(8):
            k.op(ACT, lambda e: e.activation(out=zr[h].ap[:, 0:NT], in_=zr[h].ap[:, 0:NT], func=AF.Silu),
                 reads=(zr[h].buf,), writes=(zr[h].buf,))
            k.op(DVE, lambda e: e.scalar_tensor_tensor(out=og3[:, h, :], in0=zr[h].ap[:, 0:NT], scalar=gon.ap[:, h:h + 1],
                                                       in1=oT3[:, h, :], op0=ALU.mult, op1=ALU.mult),
                 reads=(zr[h].buf, gon.buf, oT.buf), writes=(bigB.buf,))

        stage(9)
        for s in range(2):
            sga, swa, sgb, swb = wfetch("gA%d" % s), wfetch("wao%d" % s), wfetch("gB%d" % s), wfetch("wbo%d" % s)
            for ii in range(4):
                i = s * 4 + ii
                bga = proj_fm(sga, ii, NT, hT, hT3)
                sa = tmp()
                k.op(ACT, lambda e: e.activation(out=sa.ap[:, 0:NT], in_=bga.ap[:, 0:NT], func=AF.Sigmoid),
                     reads=(bga.buf,), writes=(sa.buf,))
                bya = proj_fm(swa, ii, NT, bigE, gated3)
                k.op(DVE, lambda e: e.tensor_tensor(out=sa.ap[:, 0:NT], in0=sa.ap[:, 0:NT], in1=bya.ap[:, 0:NT], op=ALU.mult),
                     reads=(sa.buf, bya.buf), writes=(sa.buf,))
                bgb = proj_fm(sgb, ii, NT, hT, hT3)
                sb2 = tmp()
                k.op(ACT, lambda e: e.activation(out=sb2.ap[:, 0:NT], in_=bgb.ap[:, 0:NT], func=AF.Sigmoid),
                     reads=(bgb.buf,), writes=(sb2.buf,))
                byb = proj_fm(swb, ii, NT, bigB, og3)
                k.op(DVE, lambda e: e.tensor_tensor(out=sb2.ap[:, 0:NT], in0=sb2.ap[:, 0:NT], in1=byb.ap[:, 0:NT], op=ALU.mult),
                     reads=(sb2.buf, byb.buf), writes=(sb2.buf,))
                k.op(POOL, lambda e: e.tensor_tensor(out=merged3[:, i, :], in0=sa.ap[:, 0:NT], in1=sb2.ap[:, 0:NT], op=ALU.add),
                     reads=(sa.buf, sb2.buf), writes=(bigC.buf,))

        if nxt is not None:
            s1_xpose(nxt[2] // 128, nxt[2])
            pre["xposed"] = True
        stage(10)
        ob = [Buf("outs%d" % i) for i in range(NB)]
        for ob_ in ob:
            ob_.w = oT.buf.w
            ob_.r = dict(oT.buf.r)
        wo = [wfetch("wo0"), wfetch("wo1")]
        ssq2p = small()
        for b in range(NB):
            for half in range(2):
                b_ = bank()
                k.pe_group([(lambda e, kc=kc: e.matmul(b_.ap, lhsT=merged3[:, kc, b * 128:(b + 1) * 128],
                                                       rhs=wo[half].ap[:, kc, :], start=(kc == 0), stop=(kc == 7)))
                            for kc in range(8)], reads=(bigC.buf, wo[half].buf), writes=(b_.buf,))
                k.op(DVE, lambda e: e.tensor_copy(out=outs3[:, b, half * 512:(half + 1) * 512], in_=b_.ap),
                     reads=(b_.buf,), writes=(ob[b],))
                k.op(ACT, lambda e: e.activation(out=junk.ap[:, 0:512], in_=outs3[:, b, half * 512:(half + 1) * 512],
                                                 func=AF.Square, accum_out=ssq2p.ap[:, 2 * b + half:2 * b + half + 1]),
                     reads=(ob[b],), writes=(junk.buf, ssq2p.buf))
        stage(100)
        ssq2 = small()
        sp3 = ssq2p.ap.rearrange("p (b t) -> p b t", t=2)
        k.op(DVE, lambda e: e.tensor_tensor(out=ssq2.ap[:, 0:NB], in0=sp3[:, 0:NB, 0], in1=sp3[:, 0:NB, 1], op=ALU.add),
             reads=(ssq2p.buf,), writes=(ssq2.buf,))
        rstd2 = rstd_from_ssq(ssq2, NB, D * EPS)
        stage(101)
        ssq3 = small()
        for b in range(NB):
            xb = xr[b % 2]
            k.dma(SP, lambda e: e.dma_start(out=xb.ap, in_=x_d[row0 + b * 128: row0 + (b + 1) * 128, :]),
                  xr_ch[b % 2], writes=(xb.buf,))
            k.op(DVE, lambda e: e.scalar_tensor_tensor(out=outs3[:, b, :], in0=outs3[:, b, :], scalar=rstd2.ap[:, b:b + 1],
                                                       in1=gpost32.ap, op0=ALU.mult, op1=ALU.mult),
                 reads=(ob[b], rstd2.buf, gpost32.buf), writes=(ob[b],))
            k.op(DVE, lambda e: e.tensor_tensor(out=outs3[:, b, :], in0=outs3[:, b, :], in1=xb.ap, op=ALU.add),
                 reads=(ob[b], xb.buf), writes=(ob[b],))
            k.op(ACT, lambda e: e.activation(out=junk.ap, in_=outs3[:, b, :], func=AF.Square, accum_out=ssq3.ap[:, b:b + 1]),
                 reads=(ob[b],), writes=(junk.buf, ssq3.buf))
        stage(102)
        rstd3 = rstd_from_ssq(ssq3, NB, D * EPS)
        stage(103)
        if nxt is not None and nxt[3]:
            s2_f(nxt[2])
            pre["f"] = True
        wg = [wfetch("wg0"), wfetch("wg1")]
        wpl = wfetch("ple")
        def tail_front(b):
            nb_, nT = n2[b % 2], n2T[b % 2]
            k.op(DVE, lambda e: e.tensor_scalar(out=nb_.ap, in0=outs3[:, b, :], scalar1=rstd3.ap[:, b:b + 1], scalar2=None,
                                                op0=ALU.mult), reads=(ob[b], rstd3.buf), writes=(nb_.buf,))
            bk = bank()
            pv = bk.ap.bitcast(BF16)
            k.pe_group([(lambda e, kc=kc: e.transpose(out=pv[:, kc * 128:(kc + 1) * 128],
                                                      in_=nb_.ap[:, kc * 128:(kc + 1) * 128], identity=ident_bf.ap))
                        for kc in range(8)], reads=(nb_.buf, ident_bf.buf), writes=(bk.buf,))
            nT3 = nT.ap.rearrange("p (k t) -> p k t", t=128)
            k.op(DVE, lambda e: e.tensor_tensor(out=nT3, in0=pv.rearrange("p (k t) -> p k t", t=128),
                                                in1=gple32.ap.unsqueeze(2).to_broadcast([128, 8, 128]), op=ALU.mult),
                 reads=(bk.buf, gple32.buf), writes=(nT.buf,))
            ptb_, pt_, pT_ = ptb[b % 2], pt[b % 2], pT[b % 2]
            k.dma(SP, lambda e: e.dma_start(out=pt_.ap, in_=p_d[row0 + b * 128: row0 + (b + 1) * 128, :]),
                  pt_ch[b % 2], writes=(pt_.buf,))
            k.op(POOL, lambda e: e.tensor_copy(out=ptb_.ap, in_=pt_.ap), reads=(pt_.buf,), writes=(ptb_.buf,))
            bk2 = bank()
            pv2 = bk2.ap.bitcast(BF16)
            k.pe_group([(lambda e, kc=kc: e.transpose(out=pv2[:, kc * 128:(kc + 1) * 128],
                                                      in_=ptb_.ap[:, kc * 128:(kc + 1) * 128], identity=ident_bf.ap))
                        for kc in range(2)], reads=(ptb_.buf, ident_bf.buf), writes=(bk2.buf,))
            k.op(ACT, lambda e: e.copy(out=pT_.ap, in_=pv2[:, 0:256]), reads=(bk2.buf,), writes=(pT_.buf,))
            pT3 = pT_.ap.rearrange("p (k t) -> p k t", t=128)
            return nT3, pT3, pT_

        def tail_back(b, nT3, pT3, pT_):
            nT = n2T[b % 2]
            for half in range(2):
                bg = bank()
                k.pe_group([(lambda e, kc=kc: e.matmul(bg.ap, lhsT=nT3[:, kc, :], rhs=wg[half].ap[:, kc, :],
                                                       start=(kc == 0), stop=(kc == 7))) for kc in range(8)],
                           reads=(nT.buf, wg[half].buf), writes=(bg.buf,))
                sgt = tmp()
                k.op(ACT, lambda e: e.activation(out=sgt.ap[:, 0:512], in_=bg.ap, func=AF.Sigmoid),
                     reads=(bg.buf,), writes=(sgt.buf,))
                bp = bank()
                k.pe_group([(lambda e, kc=kc: e.matmul(bp.ap, lhsT=pT3[:, kc, :],
                                                       rhs=wpl.ap[:, kc, half * 512:(half + 1) * 512],
                                                       start=(kc == 0), stop=(kc == 1))) for kc in range(2)],
                           reads=(pT_.buf, wpl.buf), writes=(bp.buf,))
                k.op(DVE, lambda e: e.tensor_tensor(out=sgt.ap[:, 0:512], in0=sgt.ap[:, 0:512], in1=bp.ap, op=ALU.mult),
                     reads=(sgt.buf, bp.buf), writes=(sgt.buf,))
                k.op(DVE, lambda e: e.tensor_tensor(out=outs3[:, b, half * 512:(half + 1) * 512],
                                                     in0=outs3[:, b, half * 512:(half + 1) * 512], in1=sgt.ap[:, 0:512],
                                                     op=ALU.add), reads=(ob[b], sgt.buf), writes=(ob[b],))
            k.dma(SP, lambda e: e.dma_start(out=y_d[row0 + b * 128: row0 + (b + 1) * 128, :], in_=outs3[:, b, :]),
                  y_ch[b], reads=(ob[b],))

        fr = {0: tail_front(0)}
        for b in range(NB):
            if b + 1 < NB:
                fr[b + 1] = tail_front(b + 1)
            tail_back(b, *fr.pop(b))
        oT.buf.w = None
        merged_r = {}
        for ob_ in ob:
            for st in ([ob_.w] if ob_.w else []) + list(ob_.r.values()):
                if st[0] not in merged_r or merged_r[st[0]][2] < st[2]:
                    merged_r[st[0]] = st
        oT.buf.r = merged_r
        stage(11)


    onec = k.sb("onec", [128, 8], F32)
    sel_t = k.sb("sel_t", [128, 1], F32)
    k.op(POOL, lambda e: e.memset(onec.ap, 1.0), writes=(onec.buf,))

    P1_NT, P1_NB = 512, 4
    p1_sig = [
        (lambda h: oT.ap[:, h * 512:(h + 1) * 512], lambda h: oT.buf),
        (lambda h: (bigA if h < 4 else bigB).ap.bitcast(F32)[:, (h % 4) * 512:(h % 4 + 1) * 512],
         lambda h: (bigA if h < 4 else bigB).buf),
    ]
    p1_v = [v_tm, bigC]

    p1_w = {}

    def p1_weights():
        if not p1_w:
            for nm in ("f0", "f1", "iv0", "iv1"):
                p1_w[nm] = wfetch(nm)
        return p1_w

    def p1_f_head(pp, h):
        NT = P1_NT
        hT3 = hT.ap[:, 0:8 * NT].rearrange("p (k t) -> p k t", t=NT)
        sig_ap, sig_buf = p1_sig[pp]
        slot = p1_weights()["f%d" % (h // 4)]
        b_ = proj_fm(slot, h % 4, NT, hT, hT3)
        k.op(ACT, lambda e: e.copy(out=sig_ap(h), in_=b_.ap[:, 0:NT]), reads=(b_.buf,), writes=(sig_buf(h),))

    def p1_iv_group(pp, g, blk):
        NT, NB = P1_NT, P1_NB
        hT3 = hT.ap[:, 0:8 * NT].rearrange("p (k t) -> p k t", t=NT)
        vt3 = p1_v[pp].ap[:, 0:NB * 1024].rearrange("p (b c) -> p b c", c=1024)
        w_ = p1_weights()["iv%d" % g]
        b_ = bank()
        k.pe_group([(lambda e, kc=kc: e.matmul(b_.ap[:, 0:512], lhsT=hT3[:, kc, blk * 128:(blk + 1) * 128],
                                               rhs=w_.ap[:, kc, :], start=(kc == 0), stop=(kc == 7)))
                    for kc in range(8)], reads=(w_.buf, hT.buf), writes=(b_.buf,))
        k.op(DVE, lambda e: e.tensor_copy(out=vt3[:, blk, g * 512:(g + 1) * 512], in_=b_.ap[:, 0:512]),
             reads=(b_.buf,), writes=(p1_v[pp].buf,))

    def p1_sigmoid_batch(pp):
        sig_ap, sig_buf = p1_sig[pp]
        for h in range(8):
            k.op(ACT, lambda e: e.activation(out=sig_ap(h), in_=sig_ap(h), func=AF.Sigmoid),
                 reads=(sig_buf(h),), writes=(sig_buf(h),))

    def p1_prep_head(pp, h):
        NT = P1_NT
        kdT3 = bigD.ap[:, 0:8 * NT].rearrange("p (k t) -> p k t", t=NT)
        dcy3 = dcy.ap.rearrange("p (h c) -> p h c", c=8)
        sig_ap, sig_buf = p1_sig[pp]
        if True:
            Lg, KK, B_, BL, E4 = tmp(), tmp(), tmp(), tmp(), tmp()
            k.op(ACT, lambda e: e.activation(out=Lg.ap[:, 0:NT], in_=sig_ap(h), func=AF.Ln,
                                             bias=lbt.ap[:, h:h + 1], scale=omlt.ap[:, h:h + 1]),
                 reads=(sig_buf(h), lbt.buf, omlt.buf), writes=(Lg.buf,))
            k.op(ACT, lambda e: e.activation(out=KK.ap[:, 0:NT], in_=sig_ap(h), func=AF.Identity,
                                             bias=omlt.ap[:, h:h + 1], scale=nomlt.ap[:, h:h + 1]),
                 reads=(sig_buf(h), nomlt.buf, omlt.buf), writes=(KK.buf,))
            k.op(DVE, lambda e: e.tensor_tensor_scan(out=B_.ap[:, 0:NT], data0=onec.ap[:, 0:1].to_broadcast([128, NT]),
                                                     data1=Lg.ap[:, 0:NT], initial=0.0, op0=ALU.mult, op1=ALU.add),
                 reads=(onec.buf, Lg.buf), writes=(B_.buf,))
            k.op(DVE, lambda e: e.tensor_scalar(out=BL.ap[:, 0:NT], in0=B_.ap[:, 0:NT], scalar1=-1.0,
                                                scalar2=B_.ap[:, NT - 1:NT], op0=ALU.mult, op1=ALU.add),
                 reads=(B_.buf,), writes=(BL.buf,))
            k.op(ACT, lambda e: e.activation(out=E4.ap[:, 0:NT], in_=BL.ap[:, 0:NT], func=AF.Exp), reads=(BL.buf,), writes=(E4.buf,))
            k.op(ACT, lambda e: e.activation(out=dcy3[:, h, 0:1], in_=B_.ap[:, NT - 1:NT], func=AF.Exp),
                 reads=(B_.buf,), writes=(dcy.buf,))
            k.op(DVE, lambda e: e.tensor_tensor(out=kdT3[:, h, :], in0=KK.ap[:, 0:NT], in1=E4.ap[:, 0:NT], op=ALU.mult),
                 reads=(KK.buf, E4.buf), writes=(bigD.buf,))

    def p1_state(pp):
        NT, NB = P1_NT, P1_NB
        kdT3 = bigD.ap[:, 0:8 * NT].rearrange("p (k t) -> p k t", t=NT)
        kd3 = kd_tm.ap[:, 0:NB * 1024].rearrange("p (b c) -> p b c", c=1024)
        vt3 = p1_v[pp].ap[:, 0:NB * 1024].rearrange("p (b c) -> p b c", c=1024)
        dcy3 = dcy.ap.rearrange("p (h c) -> p h c", c=8)
        for blk in range(NB):
            bk = bank()
            pv = bk.ap.bitcast(BF16)
            k.pe_group([(lambda e, h=h: e.transpose(out=pv[:, h * 128:(h + 1) * 128],
                                                    in_=kdT3[:, h, blk * 128:(blk + 1) * 128], identity=ident_bf.ap))
                        for h in range(8)], reads=(bigD.buf, ident_bf.buf), writes=(bk.buf,))
            k.op(DVE, lambda e: e.tensor_copy(out=kd3[:, blk, :], in_=pv), reads=(bk.buf,), writes=(kd_tm.buf,))
        pb = [bank(), bank()]
        fns = []
        for h in range(8):
            for blk in range(NB):
                fns.append(lambda e, h=h, blk=blk: e.matmul(pb[h // 4].ap[:, (h % 4) * 128:(h % 4 + 1) * 128],
                                                            lhsT=kd3[:, blk, h * 128:(h + 1) * 128],
                                                            rhs=vt3[:, blk, h * 128:(h + 1) * 128],
                                                            start=(blk == 0), stop=(blk == NB - 1)))
        k.pe_group(fns, reads=(kd_tm.buf, p1_v[pp].buf), writes=(pb[0].buf, pb[1].buf))
        S_f = Sf[0]
        for h in range(8):
            k.op(DVE, lambda e: e.scalar_tensor_tensor(out=S_f.ap[:, h * 128:(h + 1) * 128],
                                                       in0=S_f.ap[:, h * 128:(h + 1) * 128], scalar=dcy3[:, h, 0:1],
                                                       in1=pb[h // 4].ap[:, (h % 4) * 128:(h % 4 + 1) * 128],
                                                       op0=ALU.mult, op1=ALU.add),
                 reads=(S_f.buf, dcy.buf, pb[h // 4].buf), writes=(S_f.buf,))

    def phase1_all():
        ivg = [(g, blk) for g in range(2) for blk in range(P1_NB)]
        s1(xp, 0, P1_NB, P1_NT)
        for h in range(8):
            p1_f_head(0, h)
            p1_iv_group(0, *ivg[h])
        for t_i in range(NPT):
            pp = t_i % 2
            more = t_i + 1 < NPT
            if more:
                s1(xp, (t_i + 1) * 512, P1_NB, P1_NT)
            p1_sigmoid_batch(pp)
            for h in range(8):
                if more:
                    p1_f_head(1 - pp, h)
                    p1_iv_group(1 - pp, *ivg[h])
                p1_prep_head(pp, h)
            p1_state(pp)

    def exchange_and_halo():
        cc_in = nc.dram_tensor("cc_in", [128, 1024], F32).ap()
        cc_out = nc.dram_tensor("cc_out", [256, 1024], F32).ap()
        ccin_buf, ccout_buf = Buf("ccin"), Buf("ccout")
        ex_ch = k.chan("ex_ch")
        ex_ch2 = k.chan("ex_ch2")
        ex_ch3 = k.chan("ex_ch3")
        cc_sem = k.sem("cc_sem")
        k.dma(SP, lambda e: e.dma_start(out=sel_t.ap, in_=selm), ex_ch3, writes=(sel_t.buf,))
        k.dma(POOL, lambda e: e.dma_start(out=cc_in, in_=Sf[0].ap), ex_ch, reads=(Sf[0].buf,), writes=(ccin_buf,))
        POOL.wait_for(k._deps((ccin_buf,), (ccout_buf,)))
        ins = nc.gpsimd.collective_compute("AllGather", ALU.bypass, replica_groups=[[0, 4], [1, 5], [2, 6], [3, 7]],
                                           ins=[cc_in.opt()], outs=[cc_out.opt()])
        ins.then_inc(cc_sem)
        k._mark(("cc", cc_sem, 1), (ccin_buf,), (ccout_buf,))
        def receive():
            k.dma(POOL, lambda e: e.dma_start(out=Sf[1].ap, in_=cc_out[0:128, :]), ex_ch2, reads=(ccout_buf,), writes=(Sf[1].buf,))
            k.op(DVE, lambda e: e.tensor_scalar(out=Sf[0].ap, in0=Sf[1].ap, scalar1=sel_t.ap[:, 0:1], scalar2=None, op0=ALU.mult),
                 reads=(Sf[1].buf, sel_t.buf), writes=(Sf[0].buf,))
            k.op(ACT, lambda e: e.copy(out=Sb[0].ap, in_=Sf[0].ap), reads=(Sf[0].buf,), writes=(Sb[0].buf,))

        pre["recv"] = receive
        state["S"] = 0
        state["B"] = 0
        s1(xhalo, 0, 1, 128)
        hT3 = hT.ap[:, 0:8 * 128].rearrange("p (k t) -> p k t", t=128)
        uh3 = uh.ap.rearrange("p (j r) -> p j r", r=2)
        for s_ in range(2):
            sv, sc_ = wfetch("vA%d" % s_), wfetch("cA%d" % s_)
            for jj in range(4):
                j = s_ * 4 + jj
                bv = proj_fm(sv, jj, 128, hT, hT3)
                bc = proj_fm(sc_, jj, 128, hT, hT3)
                vs = small()
                k.op(ACT, lambda e: e.copy(out=vs.ap[:, 0:2], in_=bv.ap[:, 0:2]), reads=(bv.buf,), writes=(vs.buf,))
                k.op(DVE, lambda e: e.tensor_tensor(out=uh3[:, j, :], in0=vs.ap[:, 0:2], in1=bc.ap[:, 0:2], op=ALU.mult),
                     reads=(vs.buf, bc.buf), writes=(uh.buf,))


    def _program():
        stage(0)
        if EXCH:
            phase1_all()
            exchange_and_halo()
        for t_i in range(NPT):
            tile(xp, pp, yp, t_i * 512, 512, sample=False, nxt=((xp, (t_i + 1) * 512, 512, True) if t_i + 1 < NPT else (xs, 0, NSEQ_S * 64, False)))
        si = state["S"]
        k.dma(SP, lambda e: e.dma_start(out=hgrn_p.rearrange("h k v -> k h v"),
                                        in_=Sf[si].ap.rearrange("p (h v) -> p h v", v=128)),
              misc_ch, reads=(Sf[si].buf,))
        uh3 = uh.ap.rearrange("p (j r) -> p j r", r=2)
        for j in range(8):
            bj = bank()
            k.pe_group([lambda e: e.matmul(bj.ap[0:2, 0:128], lhsT=uh3[:, j, :], rhs=identf.ap, start=True, stop=True)],
                       reads=(uh.buf, identf.buf), writes=(bj.buf,))
            k.op(DVE, lambda e: e.tensor_copy(out=cv_out.ap[0:2, j * 128:(j + 1) * 128], in_=bj.ap[0:2, 0:128]),
                 reads=(bj.buf,), writes=(cv_out.buf,))
        k.dma(SP, lambda e: e.dma_start(out=conv_p, in_=cv_out.ap[0:2, :]), misc_ch, reads=(cv_out.buf,))
        stage(12)

        k.dma(SP, lambda e: e.dma_start(out=sc_in.ap, in_=sconv), misc_ch, writes=(sc_in.buf,))
        uhs3 = uhs.ap.rearrange("p (j m) -> p j m", m=8)
        for j in range(8):
            bj = bank()
            k.pe_group([lambda e: e.matmul(bj.ap[:, 0:8], lhsT=sc_in.ap[:, j * 128:(j + 1) * 128], rhs=identf.ap[0:8, 0:8],
                                           start=True, stop=True)], reads=(sc_in.buf, identf.buf), writes=(bj.buf,))
            k.op(DVE, lambda e: e.tensor_copy(out=uhs3[:, j, :], in_=bj.ap[:, 0:8]), reads=(bj.buf,), writes=(uhs.buf,))
        stage(13)
        tile(xs, ps_, ys, 0, NSEQ_S * 64, sample=True)
        uo3 = uo.ap.rearrange("p (j m) -> p j m", m=8)
        for j in range(8):
            bj = bank()
            k.pe_group([lambda e: e.matmul(bj.ap[0:8, 0:128], lhsT=uo3[:, j, :], rhs=identf.ap, start=True, stop=True)],
                       reads=(uo.buf, identf.buf, cv_out.buf), writes=(bj.buf,))
            k.op(DVE, lambda e: e.tensor_copy(out=cv_out.ap[0:8, j * 128:(j + 1) * 128], in_=bj.ap[0:8, 0:128]),
                 reads=(bj.buf,), writes=(cv_out.buf,))
        k.dma(SP, lambda e: e.dma_start(out=conv_s, in_=cv_out.ap[0:8, :]), misc_ch, reads=(cv_out.buf,))


    try:
        _program()
    except StopBuild:
        pass
    dump_names = [n for n in os.environ.get("KDUMP", "").split(",") if n]
    reg = dict(hT=hT, bigA=bigA, bigB=bigB, bigC=bigC, bigD=bigD, oT=oT, dcy=dcy, kd_tm=kd_tm, v_tm=v_tm,
               Sf0=Sf[0], uh=uh, gpre32=gpre32, lbt=lbt, cw=cw, mask2=mask2, gon=gon)
    dbg_ch = k.chan("dbg")
    for n in dump_names:
        t_ = reg[n]
        shp = list(t_.ap.shape)
        dd = nc.dram_tensor("dbg_" + n, shp, t_.ap.dtype, kind="ExternalOutput").ap()
        allb = [b for b in [t_.buf]]
        k.dma(SP, lambda e, dd=dd, t_=t_: e.dma_start(out=dd, in_=t_.ap), dbg_ch, reads=tuple(allb))
    if dbg_ch.val:
        nc.sync.wait_ge(dbg_ch.sem, dbg_ch.val)
    finals = [(c.sem, c.val) for c in y_ch + s_st_ch + [misc_ch] + conv_ch + wchan + xin_ch + pt_ch + s_ld_ch if c.val > 0]
    for sem, val in finals:
        nc.sync.wait_ge(sem, val)
    for E in (PE, ACT, DVE, POOL):
        if E.cnt:
            nc.sync.wait_ge(E.sem, E.cnt)
    es.close()
    return nc


_CACHE = {}
_LAST = None


def _get_nc(NPT):
    if NPT not in _CACHE:
        _CACHE[NPT] = build(NPT)
    return _CACHE[NPT]


def kernel(x_prompt, x_sample, state_conv, state_hgrn, p_prompt, p_sample, w_in, conv_w, lb_raw,
           g_pre, g_onorm, w_a_out, w_b_out, w_o, g_post, g_ple, w_ple_gate, w_ple_proj, _npt=None):
    f = lambda a: np.ascontiguousarray(np.asarray(a, dtype=np.float32))
    B, S = x_prompt.shape[0], x_prompt.shape[1]
    NPT = (S // 2) // 512 if _npt is None else _npt
    ntok = NPT * 512
    nc = _get_nc(NPT)
    common = dict(w_in=f(w_in[0]), conv_w=f(conv_w[0]), lb_raw=f(lb_raw), g_pre=f(g_pre[0]), g_onorm=f(g_onorm[0]),
                  w_a_out=f(w_a_out[0]), w_b_out=f(w_b_out[0]), w_o=f(w_o[0]), g_post=f(g_post[0]), g_ple=f(g_ple[0]),
                  w_ple_gate=f(w_ple_gate[0]), w_ple_proj=f(w_ple_proj[0]))
    in_maps = []
    for c in range(N_CORES):
        m = dict(common)
        b, half = c % B, c // B
        t0 = half * ntok
        m["xp"] = f(x_prompt[b, t0:t0 + ntok])
        m["pp"] = f(p_prompt[0, b, t0:t0 + ntok])
        xh = np.zeros((128, D), np.float32)
        if half == 1:
            xh[0:2] = x_prompt[b, t0 - 2:t0]
        m["xhalo"] = xh
        m["selm"] = np.full((128, 1), float(half), np.float32)
        s0 = c * NSEQ_S
        m["xs"] = f(x_sample[s0:s0 + NSEQ_S]).reshape(NSEQ_S * 64, D)
        m["ps"] = f(p_sample[0, s0:s0 + NSEQ_S]).reshape(NSEQ_S * 64, DPLE)
        m["sconv"] = f(state_conv[0, s0:s0 + NSEQ_S]).reshape(NSEQ_S * 2, D)
        m["shgrn"] = f(state_hgrn[0, s0:s0 + NSEQ_S])
        in_maps.append(m)
    res = run_bass_kernel_spmd(nc, in_maps, core_ids=list(range(N_CORES)))
    r = res.results
    global _LAST
    _LAST = r
    y_prompt = np.stack([np.concatenate([r[b]["yp"], r[b + B]["yp"]], 0) for b in range(B)], 0)
    y_sample = np.concatenate([r[c]["ys"].reshape(NSEQ_S, 64, D) for c in range(N_CORES)], 0)
    ncp = np.stack([r[b + B]["conv_p"] for b in range(B)], 0)[None]
    nhp = np.stack([r[b + B]["hgrn_p"] for b in range(B)], 0)[None]
    ncs = np.concatenate([r[c]["conv_s"].reshape(NSEQ_S, 2, D) for c in range(N_CORES)], 0)[None]
    nhs = np.concatenate([r[c]["hgrn_s"] for c in range(N_CORES)], 0)[None]
    return (y_prompt.astype(np.float32), y_sample.astype(np.float32), ncp.astype(np.float32),
            nhp.astype(np.float32), ncs.astype(np.float32), nhs.astype(np.float32))
```

```python
import numpy as np
from contextlib import ExitStack
import concourse.bass as bass
import concourse.mybir as mybir
from concourse.bass_utils import run_bass_kernel_spmd

F32, BF16 = mybir.dt.float32, mybir.dt.bfloat16
AF = mybir.ActivationFunctionType
ALU = mybir.AluOpType

D = 1024
DPLE = 256
NIN = 10240
H = 8
EPS = 1e-6
QK_SCALE = 128 ** -0.5
SEQ = 8192
NSEQ_S = 4
R_SLOTS = 6
NTMP = 12
N_CORES = 8

OFF = dict(vA=0, bA=1024, cA=2048, zA=3072, q=4096, f=5120, iv=6144, zB=7168, gA=8192, gB=9216)


class Buf:
    __slots__ = ("name", "w", "r")

    def __init__(self, name):
        self.name = name
        self.w = None
        self.r = {}


class Chan:
    def __init__(self, sem, key):
        self.sem = sem
        self.key = key
        self.val = 0


class Eng:
    def __init__(self, eng, sem, key, is_pe=False):
        self.eng = eng
        self.sem = sem
        self.key = key
        self.cnt = 0
        self.seen = {}
        self.is_pe = is_pe

    def wait_for(self, deps):
        need = {}
        for d in deps:
            if d is None:
                continue
            key, sem, val = d
            if self.is_pe and key == self.key:
                continue
            if val > self.seen.get(key, 0):
                if key not in need or need[key][1] < val:
                    need[key] = (sem, val)
        for key, (sem, val) in need.items():
            self.eng.wait_ge(sem, val)
            self.seen[key] = val


class T:
    def __init__(self, ap, buf):
        self.ap = ap
        self.buf = buf


class K:
    def __init__(self, nc, es):
        self.nc = nc
        self.es = es
        self.nsem = 0
        self.PE = Eng(nc.tensor, self.sem("pe"), "pe", True)
        self.ACT = Eng(nc.scalar, self.sem("act"), "act")
        self.DVE = Eng(nc.vector, self.sem("dve"), "dve")
        self.POOL = Eng(nc.gpsimd, self.sem("pool"), "pool")
        self.SP = Eng(nc.sync, None, "sp")
        self.tmp_i = 0
        self.bank_i = 0

    def sem(self, name):
        self.nsem += 1
        return self.es.enter_context(self.nc.semaphore(name))

    def chan(self, name):
        return Chan(self.sem(name), name)

    def sb(self, name, shape, dt):
        t = self.es.enter_context(self.nc.sbuf_tensor(name, shape, dt))
        return T(t[:], Buf(name))

    def ps(self, name, shape, dt):
        t = self.es.enter_context(self.nc.psum_tensor(name, shape, dt))
        return T(t[:], Buf(name))

    def _deps(self, reads, writes):
        deps = []
        for b in reads:
            deps.append(b.w)
        for b in writes:
            deps.append(b.w)
            deps.extend(b.r.values())
        return deps

    def _mark(self, stamp, reads, writes):
        for b in writes:
            b.w = stamp
            b.r = {}
        for b in reads:
            old = b.r.get(stamp[0])
            if old is None or old[2] < stamp[2]:
                b.r[stamp[0]] = stamp

    def op(self, E, fn, reads=(), writes=()):
        E.wait_for(self._deps(reads, writes))
        ins = fn(E.eng)
        E.cnt += 1
        ins.then_inc(E.sem, 1)
        self._mark((E.key, E.sem, E.cnt), reads, writes)

    def pe_group(self, fns, reads=(), writes=()):
        E = self.PE
        E.wait_for(self._deps(reads, writes))
        ins = None
        for fn in fns:
            ins = fn(E.eng)
        E.cnt += 1
        ins.then_inc(E.sem, 1)
        self._mark((E.key, E.sem, E.cnt), reads, writes)

    def dma(self, Q, fn, chan, reads=(), writes=()):
        Q.wait_for(self._deps(reads, writes))
        ins = fn(Q.eng)
        chan.val += 16
        ins.then_inc(chan.sem, 16)
        self._mark((chan.key, chan.sem, chan.val), reads, writes)


class StopBuild(Exception):
    pass


import os
_STOP = int(os.environ.get("KSTOP", "-1"))
_VAR = os.environ.get("KVAR", "")


_stage_ctr = [0]


def stage(n):
    c = _stage_ctr[0]
    _stage_ctr[0] += 1
    if _STOP == c:
        raise StopBuild()


def build(NPT, EXCH=True):
    _stage_ctr[0] = 0
    nc = bass.Bass("TRN2", target_bir_lowering=False)
    es = ExitStack()
    k = K(nc, es)
    PE, ACT, DVE, POOL, SP = k.PE, k.ACT, k.DVE, k.POOL, k.SP
    NTOK = NPT * 512

    def din(name, shape):
        return nc.dram_tensor(name, shape, F32, kind="ExternalInput").ap()

    def dout(name, shape):
        return nc.dram_tensor(name, shape, F32, kind="ExternalOutput").ap()

    xp = din("xp", [NTOK, D])
    pp = din("pp", [NTOK, DPLE])
    xs = din("xs", [NSEQ_S * 64, D])
    ps_ = din("ps", [NSEQ_S * 64, DPLE])
    sconv = din("sconv", [NSEQ_S * 2, D])
    shgrn = din("shgrn", [NSEQ_S, H, 128, 128])
    w_in = din("w_in", [D, NIN])
    conv_w = din("conv_w", [3, D])
    lb_raw = din("lb_raw", [2, D])
    g_pre = din("g_pre", [D])
    g_onorm = din("g_onorm", [D])
    w_a_out = din("w_a_out", [D, D])
    w_b_out = din("w_b_out", [D, D])
    w_o = din("w_o", [D, D])
    g_post = din("g_post", [D])
    g_ple = din("g_ple", [D])
    w_ple_gate = din("w_ple_gate", [D, D])
    w_ple_proj = din("w_ple_proj", [DPLE, D])

    xhalo = din("xhalo", [128, D])
    selm = din("selm", [128, 1])
    yp = dout("yp", [NTOK, D])
    ys = dout("ys", [NSEQ_S * 64, D])
    conv_p = dout("conv_p", [2, D])
    hgrn_p = dout("hgrn_p", [H, 128, 128])
    conv_s = dout("conv_s", [NSEQ_S * 2, D])
    hgrn_s = dout("hgrn_s", [NSEQ_S, H, 128, 128])

    def win(sec, half):
        return ("w_in", w_in, OFF[sec] * 1 + half * 512)

    stream = []
    stream += [("q%d" % i, w_in, OFF["q"] + i * 512) for i in range(2)]
    stream += [("f%d" % i, w_in, OFF["f"] + i * 512) for i in range(2)]
    stream += [("iv%d" % i, w_in, OFF["iv"] + i * 512) for i in range(2)]
    for s in range(2):
        for sec in ("vA", "cA", "zA", "bA"):
            stream.append(("%s%d" % (sec, s), w_in, OFF[sec] + s * 512))
    stream += [("zB%d" % i, w_in, OFF["zB"] + i * 512) for i in range(2)]
    for s in range(2):
        stream.append(("gA%d" % s, w_in, OFF["gA"] + s * 512))
        stream.append(("wao%d" % s, w_a_out, s * 512))
        stream.append(("gB%d" % s, w_in, OFF["gB"] + s * 512))
        stream.append(("wbo%d" % s, w_b_out, s * 512))
    stream += [("wo%d" % i, w_o, i * 512) for i in range(2)]
    stream += [("wg%d" % i, w_ple_gate, i * 512) for i in range(2)]
    stream.append(("ple", w_ple_proj, 0))
    NG = len(stream)
    wsc = nc.dram_tensor("wsc", [NG, 128, 4096], BF16, kind="Internal").ap()
    wsc_buf = [Buf("wsc%d" % g) for g in range(NG)]

    ident_bf = k.sb("ident_bf", [128, 128], BF16)
    identf = k.sb("identf", [128, 128], F32)
    ones_bf = k.sb("ones_bf", [128, 128], BF16)
    mask2 = k.sb("mask2", [128, 128], F32)
    rmask = k.sb("rmask", [128, 512], F32)
    gpre32 = k.sb("gpre32", [128, 8], F32)
    gon = k.sb("gon", [128, 8], F32)
    gple32 = k.sb("gple32", [128, 8], F32)
    lbt = k.sb("lbt", [128, 8], F32)
    omlt = k.sb("omlt", [128, 8], F32)
    nomlt = k.sb("nomlt", [128, 8], F32)
    lbr = k.sb("lbr", [128, 16], F32)
    cw = k.sb("cw", [128, 24], F32)
    gpost32 = k.sb("gpost32", [128, 1024], F32)
    dcy = k.sb("dcy", [128, 64], F32)
    uh = k.sb("uh", [128, 16], F32)
    uhs = k.sb("uhs", [128, 64], F32)
    uo = k.sb("uo", [128, 64], F32)
    smalls = [k.sb("small%d" % i, [128, 8], F32) for i in range(8)]
    sqb = k.sb("sqb", [128, 1024], BF16)
    junk = sqb

    wring = [k.sb("wring%d" % i, [128, 4096], BF16) for i in range(R_SLOTS)]
    wchan = [k.chan("wch%d" % i) for i in range(R_SLOTS)]
    hT = k.sb("hT", [128, 4096], BF16)
    bigA = k.sb("bigA", [128, 4096], BF16)
    bigB = k.sb("bigB", [128, 4096], BF16)
    bigC = k.sb("bigC", [128, 4096], BF16)
    bigD = k.sb("bigD", [128, 4096], BF16)
    bigE = k.sb("bigE", [128, 4096], BF16)
    kv = k.es.enter_context(nc.sbuf_tensor("kv", [128, 8192], BF16))
    kd_tm = T(kv[:, 0:4096], Buf("kd_tm"))
    v_tm = T(kv[:, 4096:8192], Buf("v_tm"))
    sigf_ap = kv[:].bitcast(F32)
    oT = k.sb("oT", [128, 4096], F32)
    tmps = [k.sb("tmp%d" % i, [128, 520], F32) for i in range(NTMP)]
    xin = [k.sb("xin%d" % i, [128, 1024], F32) for i in range(2)]
    xin_ch = [k.chan("xin_ch%d" % i) for i in range(2)]
    sc_in = T(xin[0].ap[0:8, :], xin[0].buf)
    cv_out = T(xin[1].ap[0:8, :], xin[1].buf)
    hn = [k.sb("hn%d" % i, [128, 1024], BF16) for i in range(4)]
    Sf = [k.sb("Sf%d" % i, [128, 1024], F32) for i in range(2)]
    Sb = [k.sb("Sb%d" % i, [128, 1024], BF16) for i in range(2)]
    s_ld_ch = [k.chan("sld%d" % i) for i in range(2)]
    s_st_ch = [k.chan("sst%d" % i) for i in range(2)]
    xr = xin
    xr_ch = xin_ch
    n2 = [k.sb("n2_%d" % i, [128, 1024], BF16) for i in range(2)]
    n2T = [k.sb("n2T%d" % i, [128, 1024], BF16) for i in range(2)]
    pt = [k.sb("pt%d" % i, [128, 256], F32) for i in range(2)]
    pt_ch = [k.chan("pt_ch%d" % i) for i in range(2)]
    ptb = [k.sb("ptb%d" % i, [128, 256], BF16) for i in range(2)]
    pT = [k.sb("pT%d" % i, [128, 256], BF16) for i in range(2)]
    y_ch = [k.chan("y_ch%d" % i) for i in range(4)]
    cst = k.chan("cst")
    misc_ch = k.chan("misc")
    conv_ch = [k.chan("cv%d" % g) for g in range(NG)]

    banks = [k.ps("bank%d" % i, [128, 512], F32) for i in range(8)]
    PO = banks[6:8]
    GB = banks[0:6]

    def bank():
        b = GB[k.bank_i % len(GB)]
        k.bank_i += 1
        return b

    def tmp():
        t = tmps[k.tmp_i % NTMP]
        k.tmp_i += 1
        return t

    small_i = [0]

    def small():
        t = smalls[small_i[0] % len(smalls)]
        small_i[0] += 1
        return t

    ws = {"issued": 0, "consumed": 0}
    gid = {nm: g for g, (nm, _, _) in enumerate(stream)}
    full_list = [nm for (nm, _, _) in stream]
    seq = []
    if EXCH:
        seq += ["f0", "f1", "iv0", "iv1"]
        seq += ["vA0", "cA0", "vA1", "cA1"]
    for _t in range(NPT + 1):
        pre_f = (_t >= 1) and (_t < NPT)
        nxt_pre = (_t + 1 < NPT)
        for nm in full_list:
            if nm in ("f0", "f1") and pre_f:
                continue
            seq.append(nm)
            if nm == "wo1" and nxt_pre:
                seq += ["f0", "f1"]

    def wissue_upto(limit):
        while ws["issued"] < min(limit, len(seq)):
            i = ws["issued"]
            g = gid[seq[i]]
            slot = i % R_SLOTS
            name = stream[g][0]
            if name == "ple":
                k.dma(SP, lambda e, slot=slot, g=g: e.dma_start(out=wring[slot].ap[:, 0:2048], in_=wsc[g][:, 0:2048]),
                      wchan[slot], reads=(wsc_buf[g],), writes=(wring[slot].buf,))
            else:
                k.dma(SP, lambda e, slot=slot, g=g: e.dma_start(out=wring[slot].ap, in_=wsc[g]),
                      wchan[slot], reads=(wsc_buf[g],), writes=(wring[slot].buf,))
            ws["issued"] += 1

    def wfetch(expect):
        i = ws["consumed"]
        assert seq[i] == expect, (seq[i], expect)
        ws["consumed"] += 1
        wissue_upto(i + R_SLOTS - 3)
        slot = wring[i % R_SLOTS]
        if expect == "ple":
            v = slot.ap[:, 0:2048].rearrange("p (kc c) -> p kc c", c=1024)
        else:
            v = slot.ap.rearrange("p (kc c) -> p kc c", c=512)
        return T(v, slot.buf)

    def fm(vec):
        return vec.rearrange("(c p) -> p c", p=128)

    raw_g = k.sb("raw_g", [128, 40], F32)
    cdmas = []

    def cdma(out_ap, in_ap):
        ins = nc.sync.dma_start(out=out_ap, in_=in_ap, allow_slow_non_contiguous=True)
        cst.val += 16
        ins.then_inc(cst.sem, 16)

    cdma(raw_g.ap[:, 0:8], fm(g_pre))
    cdma(raw_g.ap[:, 8:16], fm(g_onorm))
    cdma(raw_g.ap[:, 16:24], fm(g_ple))
    cdma(lbr.ap[:, 0:8], fm(lb_raw[0]))
    cdma(lbr.ap[:, 8:16], fm(lb_raw[1]))
    for t_ in range(3):
        cdma(cw.ap.rearrange("p (j t) -> p j t", t=3)[:, :, t_], fm(conv_w[t_]))
    cdma(gpost32.ap, g_post.partition_broadcast(128))
    cstamp = (cst.key, cst.sem, cst.val)
    for t_ in (raw_g, lbr, cw, gpost32):
        t_.buf.w = cstamp

    k.op(DVE, lambda e: e.memset(identf.ap, 1.0), writes=(identf.buf,))
    k.op(POOL, lambda e: e.affine_select(out=identf.ap, in_=identf.ap, pattern=[[-1, 128]],
                                         compare_op=ALU.is_equal, fill=0.0, base=0, channel_multiplier=1),
         reads=(identf.buf,), writes=(identf.buf,))
    k.op(DVE, lambda e: e.tensor_copy(out=ident_bf.ap, in_=identf.ap), reads=(identf.buf,), writes=(ident_bf.buf,))
    k.op(DVE, lambda e: e.memset(ones_bf.ap, 1.0), writes=(ones_bf.buf,))
    k.op(DVE, lambda e: e.memset(mask2.ap, 1.0), writes=(mask2.buf,))
    k.op(POOL, lambda e: e.affine_select(out=mask2.ap, in_=mask2.ap, pattern=[[1, 128]],
                                         compare_op=ALU.is_ge, fill=0.0, base=0, channel_multiplier=-1),
         reads=(mask2.buf,), writes=(mask2.buf,))
    k.op(DVE, lambda e: e.memset(mask2.ap[0:64, 64:128], 0.0), reads=(mask2.buf,), writes=(mask2.buf,))
    k.op(DVE, lambda e: e.memset(rmask.ap, 1.0), writes=(rmask.buf,))
    k.op(DVE, lambda e: e.memset(rmask.ap.rearrange("p (c t) -> p c t", t=64)[:, :, 0:1], 0.0),
         reads=(rmask.buf,), writes=(rmask.buf,))
    k.op(DVE, lambda e: e.tensor_scalar(out=gpre32.ap, in0=raw_g.ap[:, 0:8], scalar1=32.0, scalar2=None, op0=ALU.mult),
         reads=(raw_g.buf,), writes=(gpre32.buf,))
    k.op(DVE, lambda e: e.tensor_scalar(out=gon.ap, in0=raw_g.ap[:, 8:16], scalar1=float(128 ** 0.5), scalar2=None,
                                         op0=ALU.mult), reads=(raw_g.buf,), writes=(gon.buf,))
    k.op(DVE, lambda e: e.tensor_scalar(out=gple32.ap, in0=raw_g.ap[:, 16:24], scalar1=32.0, scalar2=None,
                                         op0=ALU.mult), reads=(raw_g.buf,), writes=(gple32.buf,))
    k.op(DVE, lambda e: e.tensor_scalar(out=gpost32.ap, in0=gpost32.ap, scalar1=32.0, scalar2=None, op0=ALU.mult),
         reads=(gpost32.buf,), writes=(gpost32.buf,))
    k.op(DVE, lambda e: e.tensor_tensor(out=lbr.ap[:, 0:8], in0=lbr.ap[:, 0:8], in1=lbr.ap[:, 8:16], op=ALU.subtract),
         reads=(lbr.buf,), writes=(lbr.buf,))
    k.op(ACT, lambda e: e.activation(out=lbt.ap, in_=lbr.ap[:, 0:8], func=AF.Sigmoid), reads=(lbr.buf,), writes=(lbt.buf,))
    k.op(DVE, lambda e: e.tensor_scalar(out=omlt.ap, in0=lbt.ap, scalar1=-1.0, scalar2=1.0, op0=ALU.mult, op1=ALU.add),
         reads=(lbt.buf,), writes=(omlt.buf,))
    k.op(DVE, lambda e: e.tensor_scalar(out=nomlt.ap, in0=omlt.ap, scalar1=-1.0, scalar2=None, op0=ALU.mult),
         reads=(omlt.buf,), writes=(nomlt.buf,))
    k.op(DVE, lambda e: e.memset(uh.ap, 0.0), writes=(uh.buf,))
    k.op(DVE, lambda e: e.memset(Sf[0].ap, 0.0), writes=(Sf[0].buf,))
    k.op(DVE, lambda e: e.memset(Sb[0].ap, 0.0), writes=(Sb[0].buf,))

    conv_order = [g for g, (nm, _, _) in enumerate(stream) if nm[:1] == "f" or nm[:2] == "iv"]
    conv_order += [g for g, (nm, _, _) in enumerate(stream) if nm[:2] in ("vA", "cA")]
    conv_order += [g for g in range(len(stream)) if g not in conv_order]
    conv_state = {"i": 0}

    def convert_next(n, gate_bufs=()):
        for _ in range(n):
            if conv_state["i"] >= len(conv_order):
                return
            g = conv_order[conv_state["i"]]
            conv_state["i"] += 1
            name, W, c0 = stream[g]
            if name == "ple":
                src = W.rearrange("(kc p) c -> p kc c", p=128)
                dst = wsc[g][:, 0:2048].rearrange("p (kc c) -> p kc c", c=1024)
            else:
                src = W[:, c0:c0 + 512].rearrange("(kc p) c -> p kc c", p=128)
                dst = wsc[g].rearrange("p (kc c) -> p kc c", c=512)
            k.dma(POOL, lambda e, dst=dst, src=src: e.dma_start(out=dst, in_=src), conv_ch[g],
                  reads=tuple(gate_bufs), writes=(wsc_buf[g],))

    convert_next(len(conv_order))

    def rstd_from_ssq(ssq_t, ncols, n_eps):
        l_ = small()
        r_ = small()
        k.op(ACT, lambda e: e.activation(out=l_.ap[:, 0:ncols], in_=ssq_t.ap[:, 0:ncols], func=AF.Ln, bias=float(n_eps), scale=1.0),
             reads=(ssq_t.buf,), writes=(l_.buf,))
        k.op(ACT, lambda e: e.activation(out=r_.ap[:, 0:ncols], in_=l_.ap[:, 0:ncols], func=AF.Exp, scale=-0.5),
             reads=(l_.buf,), writes=(r_.buf,))
        return r_

    def proj_fm(slot, cc, NT, rhs_t, rhs_view):
        b = bank()
        k.pe_group([(lambda e, kc=kc: e.matmul(b.ap[:, 0:NT], lhsT=slot.ap[:, kc, cc * 128:(cc + 1) * 128],
                                               rhs=rhs_view[:, kc, :], start=(kc == 0), stop=(kc == 7)))
                    for kc in range(8)],
                   reads=(slot.buf, rhs_t.buf), writes=(b.buf,))
        return b

    def s1_stats(x_d, row0, NB):
        for b in range(NB):
            xb = xin[b % 2]
            k.dma(SP, lambda e: e.dma_start(out=xb.ap, in_=x_d[row0 + b * 128: row0 + (b + 1) * 128, :]),
                  xin_ch[b % 2], writes=(xb.buf,))
            ssq = small()
            k.op(ACT, lambda e: e.activation(out=junk.ap, in_=xb.ap, func=AF.Square, accum_out=ssq.ap[:, 0:1]),
                 reads=(xb.buf,), writes=(junk.buf, ssq.buf))
            r_ = rstd_from_ssq(ssq, 1, D * EPS)
            hb = hn[b]
            k.op(DVE, lambda e: e.tensor_scalar(out=hb.ap, in0=xb.ap, scalar1=r_.ap[:, 0:1], scalar2=None, op0=ALU.mult),
                 reads=(xb.buf, r_.buf), writes=(hb.buf,))

    def s1_xpose(NB, NT):
        hT3 = hT.ap[:, 0:8 * NT].rearrange("p (k t) -> p k t", t=NT)
        for b in range(NB):
            hb = hn[b]
            bk = bank()
            pv = bk.ap.bitcast(BF16)
            k.pe_group([(lambda e, kc=kc: e.transpose(out=pv[:, kc * 128:(kc + 1) * 128],
                                                      in_=hb.ap[:, kc * 128:(kc + 1) * 128], identity=ident_bf.ap))
                        for kc in range(8)], reads=(hb.buf, ident_bf.buf), writes=(bk.buf,))
            k.op(DVE, lambda e: e.tensor_tensor(out=hT3[:, :, b * 128:(b + 1) * 128],
                                                in0=pv.rearrange("p (k t) -> p k t", t=128),
                                                in1=gpre32.ap.unsqueeze(2).to_broadcast([128, 8, 128]), op=ALU.mult),
                 reads=(bk.buf, gpre32.buf), writes=(hT.buf,))

    def s1(x_d, row0, NB, NT):
        s1_stats(x_d, row0, NB)
        s1_xpose(NB, NT)

    def s2_f(NT):
        hT3 = hT.ap[:, 0:8 * NT].rearrange("p (k t) -> p k t", t=NT)
        sigf3 = sigf_ap[:, 0:8 * NT].rearrange("p (k t) -> p k t", t=NT)
        for g in range(2):
            slot = wfetch("f%d" % g)
            for cc in range(4):
                h = g * 4 + cc
                b_ = proj_fm(slot, cc, NT, hT, hT3)
                k.op(ACT, lambda e: e.activation(out=sigf3[:, h, :], in_=b_.ap[:, 0:NT], func=AF.Sigmoid),
                     reads=(b_.buf,), writes=(kd_tm.buf, v_tm.buf))

    pre = {"stats": False, "xposed": False, "f": False, "recv": None}

    state = {"S": 0, "B": 0}

    def tile(x_d, p_d, y_d, row0, NT, sample, nxt=None):
        NB = NT // 128
        NCH = NT // 64
        nseq = NSEQ_S if sample else 1
        L = NT // nseq

        def v3(ap, n=NT):
            return ap[:, 0:8 * n].rearrange("p (k t) -> p k t", t=n)

        hT3 = v3(hT.ap)
        qe3, ke3, qb3, kdT3 = v3(bigA.ap), v3(bigB.ap), v3(bigC.ap), v3(bigD.ap)
        gated3, og3, merged3 = v3(bigE.ap), ke3, qb3
        attm4 = bigD.ap[:, 0:NB * 1024].rearrange("p (b h t) -> p b h t", h=8, t=128)
        kd3 = kd_tm.ap[:, 0:NB * 1024].rearrange("p (b c) -> p b c", c=1024)
        vt3 = v_tm.ap[:, 0:NB * 1024].rearrange("p (b c) -> p b c", c=1024)
        sigf3 = v3(sigf_ap)
        sq3 = v3(oT.ap)
        oT3 = v3(oT.ap)
        outs3 = oT.ap[:, 0:NB * 1024].rearrange("p (b c) -> p b c", c=1024)
        dcy3 = dcy.ap.rearrange("p (h c) -> p h c", c=8)

        if not pre["stats"]:
            s1_stats(x_d, row0, NB)
        if not pre["xposed"]:
            s1_xpose(NB, NT)
        pre["stats"] = pre["xposed"] = False
        stage(1)
        for g in range(2):
            slot = wfetch("q%d" % g)
            for cc in range(4):
                h = g * 4 + cc
                b_ = proj_fm(slot, cc, NT, hT, hT3)
                sg = tmp()
                k.op(ACT, lambda e: e.activation(out=sg.ap[:, 0:NT], in_=b_.ap[:, 0:NT], func=AF.Sigmoid),
                     reads=(b_.buf,), writes=(sg.buf,))
                k.op(DVE, lambda e: e.scalar_tensor_tensor(out=sq3[:, h, :], in0=b_.ap[:, 0:NT], scalar=float(QK_SCALE),
                                                           in1=sg.ap[:, 0:NT], op0=ALU.mult, op1=ALU.mult),
                     reads=(b_.buf, sg.buf), writes=(oT.buf,))
        if not pre["f"]:
            s2_f(NT)
        pre["f"] = False

        ivs = [wfetch("iv0"), wfetch("iv1")]
        iv_items = [(g, blk) for g in range(2) for blk in range(NB)]

        def iv_group(g, blk):
            b_ = bank()
            k.pe_group([(lambda e, kc=kc: e.matmul(b_.ap[:, 0:512], lhsT=hT3[:, kc, blk * 128:(blk + 1) * 128],
                                                   rhs=ivs[g].ap[:, kc, :], start=(kc == 0), stop=(kc == 7)))
                        for kc in range(8)], reads=(ivs[g].buf, hT.buf), writes=(b_.buf,))
            k.op(ACT, lambda e: e.copy(out=hn[blk].ap[:, g * 512:(g + 1) * 512], in_=b_.ap[:, 0:512]),
                 reads=(b_.buf,), writes=(hn[blk].buf,))

        stage(2)
        def c3(ap):
            return ap[:, 0:NT].rearrange("p (c t) -> p c t", t=64)

        s3t = {}

        def s3_A(h):
            for it_ in iv_items[h * len(iv_items) // 8:(h + 1) * len(iv_items) // 8]:
                iv_group(*it_)
            Lg, B_, BM, BL, KK = [tmp() for _ in range(5)]
            s3t[h] = (Lg, B_, BM, BL, KK)
            k.op(ACT, lambda e: e.activation(out=Lg.ap[:, 0:NT], in_=sigf3[:, h, :], func=AF.Ln,
                                             bias=lbt.ap[:, h:h + 1], scale=omlt.ap[:, h:h + 1]),
                 reads=(kd_tm.buf, v_tm.buf, lbt.buf, omlt.buf), writes=(Lg.buf,))
            k.op(ACT, lambda e: e.activation(out=KK.ap[:, 0:NT], in_=sigf3[:, h, :], func=AF.Identity,
                                             bias=omlt.ap[:, h:h + 1], scale=nomlt.ap[:, h:h + 1]),
                 reads=(kd_tm.buf, v_tm.buf, nomlt.buf, omlt.buf), writes=(KK.buf,))
            k.op(DVE, lambda e: e.tensor_tensor_scan(out=B_.ap[:, 0:NT], data0=rmask.ap[:, 0:NT], data1=Lg.ap[:, 0:NT],
                                                     initial=0.0, op0=ALU.mult, op1=ALU.add),
                 reads=(rmask.buf, Lg.buf), writes=(B_.buf,))
            k.op(DVE, lambda e: e.tensor_tensor(out=c3(BM.ap), in0=c3(B_.ap),
                                                in1=c3(B_.ap)[:, :, 31:32].to_broadcast([128, NCH, 64]), op=ALU.subtract),
                 reads=(B_.buf,), writes=(BM.buf,))
            k.op(DVE, lambda e: e.tensor_tensor(out=c3(BL.ap), in0=c3(B_.ap)[:, :, 63:64].to_broadcast([128, NCH, 64]),
                                                in1=c3(B_.ap), op=ALU.subtract),
                 reads=(B_.buf,), writes=(BL.buf,))

        def s3_B(h):
            Lg, B_, BM, BL, KK = s3t.pop(h)
            k.op(ACT, lambda e: e.activation(out=Lg.ap[:, 0:NT], in_=BM.ap[:, 0:NT], func=AF.Exp, scale=-1.0),
                 reads=(BM.buf,), writes=(Lg.buf,))
            k.op(ACT, lambda e: e.activation(out=BM.ap[:, 0:NT], in_=BM.ap[:, 0:NT], func=AF.Exp), reads=(BM.buf,), writes=(BM.buf,))
            k.op(ACT, lambda e: e.activation(out=dcy3[:, h, 0:NCH], in_=c3(B_.ap)[:, :, 63], func=AF.Exp),
                 reads=(B_.buf,), writes=(dcy.buf,))
            k.op(ACT, lambda e: e.activation(out=B_.ap[:, 0:NT], in_=B_.ap[:, 0:NT], func=AF.Exp), reads=(B_.buf,), writes=(B_.buf,))
            k.op(ACT, lambda e: e.activation(out=BL.ap[:, 0:NT], in_=BL.ap[:, 0:NT], func=AF.Exp), reads=(BL.buf,), writes=(BL.buf,))
            k.op(DVE, lambda e: e.tensor_tensor(out=qe3[:, h, :], in0=sq3[:, h, :], in1=BM.ap[:, 0:NT], op=ALU.mult),
                 reads=(oT.buf, BM.buf), writes=(bigA.buf,))
            k.op(DVE, lambda e: e.tensor_tensor(out=qb3[:, h, :], in0=sq3[:, h, :], in1=B_.ap[:, 0:NT], op=ALU.mult),
                 reads=(oT.buf, B_.buf), writes=(bigC.buf,))
            k.op(POOL, lambda e: e.tensor_tensor(out=ke3[:, h, :], in0=KK.ap[:, 0:NT], in1=Lg.ap[:, 0:NT], op=ALU.mult),
                 reads=(KK.buf, Lg.buf), writes=(bigB.buf,))
            k.op(POOL, lambda e: e.tensor_tensor(out=kdT3[:, h, :], in0=KK.ap[:, 0:NT], in1=BL.ap[:, 0:NT], op=ALU.mult),
                 reads=(KK.buf, BL.buf), writes=(bigD.buf,))

        s3_A(0)
        for h in range(8):
            if h + 1 < 8:
                s3_A(h + 1)
            s3_B(h)

        stage(4)
        for blk in range(NB):
            bk = bank()
            pv = bk.ap.bitcast(BF16)
            k.pe_group([(lambda e, h=h: e.transpose(out=pv[:, h * 128:(h + 1) * 128],
                                                    in_=kdT3[:, h, blk * 128:(blk + 1) * 128], identity=ident_bf.ap))
                        for h in range(8)], reads=(bigD.buf, ident_bf.buf), writes=(bk.buf,))
            k.op(DVE, lambda e: e.tensor_copy(out=kd3[:, blk, :], in_=pv), reads=(bk.buf,), writes=(kd_tm.buf,))

        stage(5)
        for p in range(NB):
            for hg in range(2):
                b_ = bank()
                k.pe_group([(lambda e, hh=hh: e.matmul(b_.ap[:, hh * 128:(hh + 1) * 128],
                                                       lhsT=ke3[:, hg * 4 + hh, p * 128:(p + 1) * 128],
                                                       rhs=qe3[:, hg * 4 + hh, p * 128:(p + 1) * 128], start=True, stop=True))
                            for hh in range(4)], reads=(bigA.buf, bigB.buf), writes=(b_.buf,))
                k.op(DVE, lambda e: e.tensor_tensor(out=attm4[:, p, hg * 4:(hg + 1) * 4, :],
                                                    in0=b_.ap.rearrange("p (h t) -> p h t", t=128),
                                                    in1=mask2.ap.unsqueeze(1).to_broadcast([128, 4, 128]), op=ALU.mult),
                     reads=(b_.buf, mask2.buf), writes=(bigD.buf,))

        aw = {}

        abk = {}

        def brA_pe(j):
            s_, jj = j // 4, j % 4
            if jj == 0:
                aw["s"] = (wfetch("vA%d" % s_), wfetch("cA%d" % s_), wfetch("zA%d" % s_), wfetch("bA%d" % s_))
            sv, sc_, sz, sb_ = aw["s"]
            abk[j] = (proj_fm(sv, jj, NT, hT, hT3), proj_fm(sc_, jj, NT, hT, hT3),
                      proj_fm(sz, jj, NT, hT, hT3), proj_fm(sb_, jj, NT, hT, hT3))

        atm = {}

        def brA_early(j):
            bv, bc, bz, bb = abk.pop(j)
            vAs, sgz, u, t1, t2 = tmp(), tmp(), tmp(), tmp(), tmp()
            atm[j] = (sgz, u, t1, t2)
            u3 = u.ap[:, 0:nseq * (L + 2)].rearrange("p (s l) -> p s l", l=L + 2)

            def s3(ap):
                return ap[:, 0:NT].rearrange("p (s l) -> p s l", l=L)

            k.op(ACT, lambda e: e.copy(out=vAs.ap[:, 0:NT], in_=bv.ap[:, 0:NT]), reads=(bv.buf,), writes=(vAs.buf,))
            k.op(ACT, lambda e: e.activation(out=sgz.ap[:, 0:NT], in_=bz.ap[:, 0:NT], func=AF.Sigmoid),
                 reads=(bz.buf,), writes=(sgz.buf,))
            k.op(DVE, lambda e: e.tensor_tensor(out=u3[:, :, 2:2 + L], in0=s3(vAs.ap), in1=s3(bc.ap), op=ALU.mult),
                 reads=(vAs.buf, bc.buf), writes=(u.buf,))
            k.op(DVE, lambda e: e.tensor_tensor(out=sgz.ap[:, 0:NT], in0=sgz.ap[:, 0:NT], in1=bz.ap[:, 0:NT], op=ALU.mult),
                 reads=(sgz.buf, bz.buf), writes=(sgz.buf,))
            k.op(DVE, lambda e: e.tensor_tensor(out=sgz.ap[:, 0:NT], in0=sgz.ap[:, 0:NT], in1=bb.ap[:, 0:NT], op=ALU.mult),
                 reads=(sgz.buf, bb.buf), writes=(sgz.buf,))

        def brA_late(j):
            sgz, u, t1, t2 = atm.pop(j)
            u3 = u.ap[:, 0:nseq * (L + 2)].rearrange("p (s l) -> p s l", l=L + 2)

            def s3(ap):
                return ap[:, 0:NT].rearrange("p (s l) -> p s l", l=L)

            if sample:
                hist_src = uhs.ap.rearrange("p (j s r) -> p j s r", s=NSEQ_S, r=2)[:, j, :, :]
                hist_buf = uhs.buf
            else:
                hist_src = uh.ap.rearrange("p (j s r) -> p j s r", s=1, r=2)[:, j, :, :]
                hist_buf = uh.buf
            k.op(POOL, lambda e: e.tensor_copy(out=u3[:, :, 0:2], in_=hist_src), reads=(hist_buf, u.buf), writes=(u.buf,))
            if sample:
                dst = uo.ap.rearrange("p (j s r) -> p j s r", s=NSEQ_S, r=2)[:, j, :, :]
                k.op(POOL, lambda e: e.tensor_copy(out=dst, in_=u3[:, :, L:L + 2]), reads=(u.buf,), writes=(uo.buf,))
            else:
                dst = uh.ap.rearrange("p (j s r) -> p j s r", s=1, r=2)[:, j, :, :]
                k.op(POOL, lambda e: e.tensor_copy(out=dst, in_=u3[:, :, L:L + 2]), reads=(u.buf,), writes=(uh.buf,))
            cw3 = cw.ap.rearrange("p (j t) -> p j t", t=3)
            k.op(ACT, lambda e: e.activation(out=s3(t1.ap), in_=u3[:, :, 0:L], func=AF.Identity, scale=cw3[:, j, 0:1]),
                 reads=(u.buf, cw.buf), writes=(t1.buf,))
            k.op(DVE, lambda e: e.scalar_tensor_tensor(out=s3(t2.ap), in0=u3[:, :, 1:1 + L], scalar=cw3[:, j, 1:2],
                                                       in1=s3(t1.ap), op0=ALU.mult, op1=ALU.add),
                 reads=(u.buf, cw.buf, t1.buf), writes=(t2.buf,))
            k.op(DVE, lambda e: e.scalar_tensor_tensor(out=s3(t1.ap), in0=u3[:, :, 2:2 + L], scalar=cw3[:, j, 2:3],
                                                       in1=s3(t2.ap), op0=ALU.mult, op1=ALU.add),
                 reads=(u.buf, cw.buf, t2.buf), writes=(t1.buf,))
            k.op(POOL, lambda e: e.tensor_tensor(out=gated3[:, j, :], in0=sgz.ap[:, 0:NT], in1=t1.ap[:, 0:NT], op=ALU.mult),
                 reads=(t1.buf, sgz.buf), writes=(bigE.buf,))

        def brA_ew(j):
            brA_early(j)
            brA_late(j)

        stage(6)
        if pre["recv"] is not None:
            pre["recv"]()
            pre["recv"] = None
        def po_view(hh_bank, h, c0, n):
            return PO[hh_bank].ap[:, (h % 4) * 128 + c0:(h % 4) * 128 + c0 + n]

        for p in range(NB):
            for half in range(2):
                c = 2 * p + half
                if 8 // NCH == 1 and c >= 1:
                    brA_early(c - 1)
                if sample:
                    si = c % 2
                    state["S"] = si
                    state["B"] = si
                    k.dma(SP, lambda e: e.dma_start(out=Sf[si].ap.rearrange("p (h v) -> p h v", v=128),
                                                    in_=shgrn[c].rearrange("h k v -> k h v")),
                          s_ld_ch[si], writes=(Sf[si].buf,))
                    k.op(ACT, lambda e: e.copy(out=Sb[si].ap, in_=Sf[si].ap), reads=(Sf[si].buf,), writes=(Sb[si].buf,))
                si = state["S"]
                S_f, S_b = Sf[si], Sb[state["B"]]
                lo, hi = half * 64, (half + 1) * 64
                fns = []
                for h in range(8):
                    fns.append(lambda e, h=h: e.matmul(po_view(h // 4, h, lo, 64), lhsT=S_b.ap[:, h * 128:(h + 1) * 128],
                                                       rhs=qb3[:, h, c * 64:(c + 1) * 64], start=True, stop=False))
                    fns.append(lambda e, h=h: e.matmul(po_view(h // 4, h, lo, 64), lhsT=hn[p].ap[:, h * 128:(h + 1) * 128],
                                                       rhs=attm4[:, p, h, lo:hi], start=False, stop=True))
                k.pe_group(fns, reads=(S_b.buf, bigC.buf, hn[p].buf, bigD.buf), writes=(PO[0].buf, PO[1].buf))
                pb = [bank(), bank()]
                k.pe_group([(lambda e, h=h: e.matmul(pb[h // 4].ap[:, (h % 4) * 128:(h % 4 + 1) * 128],
                                                     lhsT=kd3[lo:hi, p, h * 128:(h + 1) * 128],
                                                     rhs=hn[p].ap[lo:hi, h * 128:(h + 1) * 128], start=True, stop=True))
                            for h in range(8)], reads=(kd_tm.buf, hn[p].buf), writes=(pb[0].buf, pb[1].buf))
                for h in range(8):
                    k.op(DVE, lambda e: e.scalar_tensor_tensor(out=S_f.ap[:, h * 128:(h + 1) * 128],
                                                               in0=S_f.ap[:, h * 128:(h + 1) * 128],
                                                               scalar=dcy3[:, h, c:c + 1],
                                                               in1=pb[h // 4].ap[:, (h % 4) * 128:(h % 4 + 1) * 128],
                                                               op0=ALU.mult, op1=ALU.add),
                         reads=(S_f.buf, dcy.buf, pb[h // 4].buf), writes=(S_f.buf,))
                if sample:
                    k.dma(SP, lambda e: e.dma_start(out=hgrn_s[c].rearrange("h k v -> k h v"),
                                                    in_=S_f.ap.rearrange("p (h v) -> p h v", v=128)),
                          s_st_ch[si], reads=(S_f.buf,))
                else:
                    nb2 = 1 - state["B"]
                    k.op(ACT, lambda e: e.copy(out=Sb[nb2].ap, in_=S_f.ap), reads=(S_f.buf,), writes=(Sb[nb2].buf,))
                    state["B"] = nb2
                cps = 8 // NCH
                if cps == 1:
                    if c >= 1:
                        brA_late(c - 1)
                    brA_pe(c)
                else:
                    for jx in range(cps):
                        brA_pe(c * cps + jx)
                        brA_ew(c * cps + jx)
            for hg in range(2):
                k.op(DVE, lambda e: e.tensor_copy(out=oT3[:, hg * 4:(hg + 1) * 4, p * 128:(p + 1) * 128],
                                                  in_=PO[hg].ap.rearrange("p (h t) -> p h t", t=128)),
                     reads=(PO[hg].buf,), writes=(oT.buf,))

        if 8 // NCH == 1:
            brA_ew(7)
        stage(7)
        items = [(p, hg) for p in range(NB) for hg in range(2)]

        def sl_of(it):
            p, hg = it
            return oT3[:, hg * 4:(hg + 1) * 4, p * 128:(p + 1) * 128]

        def sq_emit(it):
            sqh = sqb.ap[:, it[1] * 512:(it[1] + 1) * 512]
            k.op(DVE, lambda e: e.tensor_tensor(out=sqh.rearrange("p (h t) -> p h t", t=128), in0=sl_of(it), in1=sl_of(it),
                                                op=ALU.mult), reads=(oT.buf,), writes=(sqb_h[it[1]], sqb.buf))

        sqb_h = [Buf("sqb_h0"), Buf("sqb_h1")]
        zr = [tmps[i] for i in range(8)]
        lr = [tmps[8 + i] for i in range(4)]
        zw = {}
        hpi = 8 // len(items)

        def zb_head(h):
            if h % 4 == 0:
                zw["s"] = wfetch("zB%d" % (h // 4))
            b_ = proj_fm(zw["s"], h % 4, NT, hT, hT3)
            k.op(DVE, lambda e: e.tensor_copy(out=zr[h].ap[:, 0:NT], in_=b_.ap[:, 0:NT]), reads=(b_.buf,), writes=(zr[h].buf,))

        sq_emit(items[0])
        for ii, it in enumerate(items):
            if ii + 1 < len(items):
                sq_emit(items[ii + 1])
            sqh = sqb.ap[:, it[1] * 512:(it[1] + 1) * 512]
            b_ = bank()
            k.pe_group([lambda e: e.matmul(b_.ap, lhsT=ones_bf.ap, rhs=sqh, start=True, stop=True)],
                       reads=(ones_bf.buf, sqb_h[it[1]]), writes=(b_.buf,))
            for hx in range(hpi):
                zb_head(ii * hpi + hx)
            l_, r_ = lr[(2 * ii) % 4], lr[(2 * ii + 1) % 4]
            k.op(ACT, lambda e: e.activation(out=l_.ap[:, 0:512], in_=b_.ap, func=AF.Ln, bias=float(128 * EPS), scale=1.0),
                 reads=(b_.buf,), writes=(l_.buf,))
            k.op(ACT, lambda e: e.activation(out=r_.ap[:, 0:512], in_=l_.ap[:, 0:512], func=AF.Exp, scale=-0.5),
                 reads=(l_.buf,), writes=(r_.buf,))
            k.op(DVE, lambda e: e.tensor_tensor(out=sl_of(it), in0=sl_of(it),
                                                in1=r_.ap[:, 0:512].rearrange("p (h t) -> p h t", t=128), op=ALU.mult),
                 reads=(oT.buf, r_.buf), writes=(oT.buf,))

        if nxt is not None:
            s1_stats(nxt[0], nxt[1], nxt[2] // 128)
            pre["stats"] = True
        stage(8)
        for h in range(8):
            k.op(ACT, lambda e: e.activation(out=zr[h].ap[:, 0:NT], in_=zr[h].ap[:, 0:NT], func=AF.Silu),
                 reads=(zr[h].buf,), writes=(zr[h].buf,))
            k.op(DVE, lambda e: e.scalar_tensor_tensor(out=og3[:, h, :], in0=zr[h].ap[:, 0:NT], scalar=gon.ap[:, h:h + 1],
                                                       in1=oT3[:, h, :], op0=ALU.mult, op1=ALU.mult),
                 reads=(zr[h].buf, gon.buf, oT.buf), writes=(bigB.buf,))

        stage(9)
        for s in range(2):
            sga, swa, sgb, swb = wfetch("gA%d" % s), wfetch("wao%d" % s), wfetch("gB%d" % s), wfetch("wbo%d" % s)
            for ii in range(4):
                i = s * 4 + ii
                bga = proj_fm(sga, ii, NT, hT, hT3)
                sa = tmp()
                k.op(ACT, lambda e: e.activation(out=sa.ap[:, 0:NT], in_=bga.ap[:, 0:NT], func=AF.Sigmoid),
                     reads=(bga.buf,), writes=(sa.buf,))
                bya = proj_fm(swa, ii, NT, bigE, gated3)
                k.op(DVE, lambda e: e.tensor_tensor(out=sa.ap[:, 0:NT], in0=sa.ap[:, 0:NT], in1=bya.ap[:, 0:NT], op=ALU.mult),
                     reads=(sa.buf, bya.buf), writes=(sa.buf,))
                bgb = proj_fm(sgb, ii, NT, hT, hT3)
                sb2 = tmp()
                k.op(ACT, lambda e: e.activation(out=sb2.ap[:, 0:NT], in_=bgb.ap[:, 0:NT], func=AF.Sigmoid),
                     reads=(bgb.buf,), writes=(sb2.buf,))
                byb = proj_fm(swb, ii, NT, bigB, og3)
                k.op(DVE, lambda e: e.tensor_tensor(out=sb2.ap[:, 0:NT], in0=sb2.ap[:, 0:NT], in1=byb.ap[:, 0:NT], op=ALU.mult),
                     reads=(sb2.buf, byb.buf), writes=(sb2.buf,))
                k.op(POOL, lambda e: e.tensor_tensor(out=merged3[:, i, :], in0=sa.ap[:, 0:NT], in1=sb2.ap[:, 0:NT], op=ALU.add),
                     reads=(sa.buf, sb2.buf), writes=(bigC.buf,))

        if nxt is not None:
            s1_xpose(nxt[2] // 128, nxt[2])
            pre["xposed"] = True
        stage(10)
        ob = [Buf("outs%d" % i) for i in range(NB)]
        for ob_ in ob:
            ob_.w = oT.buf.w
            ob_.r = dict(oT.buf.r)
        wo = [wfetch("wo0"), wfetch("wo1")]
        ssq2p = small()
        for b in range(NB):
            for half in range(2):
                b_ = bank()
                k.pe_group([(lambda e, kc=kc: e.matmul(b_.ap, lhsT=merged3[:, kc, b * 128:(b + 1) * 128],
                                                       rhs=wo[half].ap[:, kc, :], start=(kc == 0), stop=(kc == 7)))
                            for kc in range(8)], reads=(bigC.buf, wo[half].buf), writes=(b_.buf,))
                k.op(DVE, lambda e: e.tensor_copy(out=outs3[:, b, half * 512:(half + 1) * 512], in_=b_.ap),
                     reads=(b_.buf,), writes=(ob[b],))
                k.op(ACT, lambda e: e.activation(out=junk.ap[:, 0:512], in_=outs3[:, b, half * 512:(half + 1) * 512],
                                                 func=AF.Square, accum_out=ssq2p.ap[:, 2 * b + half:2 * b + half + 1]),
                     reads=(ob[b],), writes=(junk.buf, ssq2p.buf))
        stage(100)
        ssq2 = small()
        sp3 = ssq2p.ap.rearrange("p (b t) -> p b t", t=2)
        k.op(DVE, lambda e: e.tensor_tensor(out=ssq2.ap[:, 0:NB], in0=sp3[:, 0:NB, 0], in1=sp3[:, 0:NB, 1], op=ALU.add),
             reads=(ssq2p.buf,), writes=(ssq2.buf,))
        rstd2 = rstd_from_ssq(ssq2, NB, D * EPS)
        stage(101)
        ssq3 = small()
        for b in range(NB):
            xb = xr[b % 2]
            k.dma(SP, lambda e: e.dma_start(out=xb.ap, in_=x_d[row0 + b * 128: row0 + (b + 1) * 128, :]),
                  xr_ch[b % 2], writes=(xb.buf,))
            k.op(DVE, lambda e: e.scalar_tensor_tensor(out=outs3[:, b, :], in0=outs3[:, b, :], scalar=rstd2.ap[:, b:b + 1],
                                                       in1=gpost32.ap, op0=ALU.mult, op1=ALU.mult),
                 reads=(ob[b], rstd2.buf, gpost32.buf), writes=(ob[b],))
            k.op(DVE, lambda e: e.tensor_tensor(out=outs3[:, b, :], in0=outs3[:, b, :], in1=xb.ap, op=ALU.add),
                 reads=(ob[b], xb.buf), writes=(ob[b],))
            k.op(ACT, lambda e: e.activation(out=junk.ap, in_=outs3[:, b, :], func=AF.Square, accum_out=ssq3.ap[:, b:b + 1]),
                 reads=(ob[b],), writes=(junk.buf, ssq3.buf))
        stage(102)
        rstd3 = rstd_from_ssq(ssq3, NB, D * EPS)
        stage(103)
        if nxt is not None and nxt[3]:
            s2_f(nxt[2])
            pre["f"] = True
        wg = [wfetch("wg0"), wfetch("wg1")]
        wpl = wfetch("ple")
        def tail_front(b):
            nb_, nT = n2[b % 2], n2T[b % 2]
            k.op(DVE, lambda e: e.tensor_scalar(out=nb_.ap, in0=outs3[:, b, :], scalar1=rstd3.ap[:, b:b + 1], scalar2=None,
                                                op0=ALU.mult), reads=(ob[b], rstd3.buf), writes=(nb_.buf,))
            bk = bank()
            pv = bk.ap.bitcast(BF16)
            k.pe_group([(lambda e, kc=kc: e.transpose(out=pv[:, kc * 128:(kc + 1) * 128],
                                                      in_=nb_.ap[:, kc * 128:(kc + 1) * 128], identity=ident_bf.ap))
                        for kc in range(8)], reads=(nb_.buf, ident_bf.buf), writes=(bk.buf,))
            nT3 = nT.ap.rearrange("p (k t) -> p k t", t=128)
            k.op(DVE, lambda e: e.tensor_tensor(out=nT3, in0=pv.rearrange("p (k t) -> p k t", t=128),
                                                in1=gple32.ap.unsqueeze(2).to_broadcast([128, 8, 128]), op=ALU.mult),
                 reads=(bk.buf, gple32.buf), writes=(nT.buf,))
            ptb_, pt_, pT_ = ptb[b % 2], pt[b % 2], pT[b % 2]
            k.dma(SP, lambda e: e.dma_start(out=pt_.ap, in_=p_d[row0 + b * 128: row0 + (b + 1) * 128, :]),
                  pt_ch[b % 2], writes=(pt_.buf,))
            k.op(POOL, lambda e: e.tensor_copy(out=ptb_.ap, in_=pt_.ap), reads=(pt_.buf,), writes=(ptb_.buf,))
            bk2 = bank()
            pv2 = bk2.ap.bitcast(BF16)
            k.pe_group([(lambda e, kc=kc: e.transpose(out=pv2[:, kc * 128:(kc + 1) * 128],
                                                      in_=ptb_.ap[:, kc * 128:(kc + 1) * 128], identity=ident_bf.ap))
                        for kc in range(2)], reads=(ptb_.buf, ident_bf.buf), writes=(bk2.buf,))
            k.op(ACT, lambda e: e.copy(out=pT_.ap, in_=pv2[:, 0:256]), reads=(bk2.buf,), writes=(pT_.buf,))
            pT3 = pT_.ap.rearrange("p (k t) -> p k t", t=128)
            return nT3, pT3, pT_

        def tail_back(b, nT3, pT3, pT_):
            nT = n2T[b % 2]
            for half in range(2):
                bg = bank()
                k.pe_group([(lambda e, kc=kc: e.matmul(bg.ap, lhsT=nT3[:, kc, :], rhs=wg[half].ap[:, kc, :],
                                                       start=(kc == 0), stop=(kc == 7))) for kc in range(8)],
                           reads=(nT.buf, wg[half].buf), writes=(bg.buf,))
                sgt = tmp()
                k.op(ACT, lambda e: e.activation(out=sgt.ap[:, 0:512], in_=bg.ap, func=AF.Sigmoid),
                     reads=(bg.buf,), writes=(sgt.buf,))
                bp = bank()
                k.pe_group([(lambda e, kc=kc: e.matmul(bp.ap, lhsT=pT3[:, kc, :],
                                                       rhs=wpl.ap[:, kc, half * 512:(half + 1) * 512],
                                                       start=(kc == 0), stop=(kc == 1))) for kc in range(2)],
                           reads=(pT_.buf, wpl.buf), writes=(bp.buf,))
                k.op(DVE, lambda e: e.tensor_tensor(out=sgt.ap[:, 0:512], in0=sgt.ap[:, 0:512], in1=bp.ap, op=ALU.mult),
                     reads=(sgt.buf, bp.buf), writes=(sgt.buf,))
                k.op(DVE, lambda e: e.tensor_tensor(out=outs3[:, b, half * 512:(half + 1) * 512],
                                                     in0=outs3[:, b, half * 512:(half + 1) * 512], in1=sgt.ap[:, 0:512],
                                                     op=ALU.add), reads=(ob[b], sgt.buf), writes=(ob[b],))
            k.dma(SP, lambda e: e.dma_start(out=y_d[row0 + b * 128: row0 + (b + 1) * 128, :], in_=outs3[:, b, :]),
                  y_ch[b], reads=(ob[b],))

        fr = {0: tail_front(0)}
        for b in range(NB):
            if b + 1 < NB:
                fr[b + 1] = tail_front(b + 1)
            tail_back(b, *fr.pop(b))
        oT.buf.w = None
        merged_r = {}
        for ob_ in ob:
            for st in ([ob_.w] if ob_.w else []) + list(ob_.r.values()):
                if st[0] not in merged_r or merged_r[st[0]][2] < st[2]:
                    merged_r[st[0]] = st
        oT.buf.r = merged_r
        stage(11)


    onec = k.sb("onec", [128, 8], F32)
    sel_t = k.sb("sel_t", [128, 1], F32)
    k.op(POOL, lambda e: e.memset(onec.ap, 1.0), writes=(onec.buf,))

    P1_NT, P1_NB = 512, 4
    p1_sig = [
        (lambda h: oT.ap[:, h * 512:(h + 1) * 512], lambda h: oT.buf),
        (lambda h: (bigA if h < 4 else bigB).ap.bitcast(F32)[:, (h % 4) * 512:(h % 4 + 1) * 512],
         lambda h: (bigA if h < 4 else bigB).buf),
    ]
    p1_v = [v_tm, bigC]

    p1_w = {}

    def p1_weights():
        if not p1_w:
            for nm in ("f0", "f1", "iv0", "iv1"):
                p1_w[nm] = wfetch(nm)
        return p1_w

    def p1_f_head(pp, h):
        NT = P1_NT
        hT3 = hT.ap[:, 0:8 * NT].rearrange("p (k t) -> p k t", t=NT)
        sig_ap, sig_buf = p1_sig[pp]
        slot = p1_weights()["f%d" % (h // 4)]
        b_ = proj_fm(slot, h % 4, NT, hT, hT3)
        k.op(ACT, lambda e: e.copy(out=sig_ap(h), in_=b_.ap[:, 0:NT]), reads=(b_.buf,), writes=(sig_buf(h),))

    def p1_iv_group(pp, g, blk):
        NT, NB = P1_NT, P1_NB
        hT3 = hT.ap[:, 0:8 * NT].rearrange("p (k t) -> p k t", t=NT)
        vt3 = p1_v[pp].ap[:, 0:NB * 1024].rearrange("p (b c) -> p b c", c=1024)
        w_ = p1_weights()["iv%d" % g]
        b_ = bank()
        k.pe_group([(lambda e, kc=kc: e.matmul(b_.ap[:, 0:512], lhsT=hT3[:, kc, blk * 128:(blk + 1) * 128],
                                               rhs=w_.ap[:, kc, :], start=(kc == 0), stop=(kc == 7)))
                    for kc in range(8)], reads=(w_.buf, hT.buf), writes=(b_.buf,))
        k.op(DVE, lambda e: e.tensor_copy(out=vt3[:, blk, g * 512:(g + 1) * 512], in_=b_.ap[:, 0:512]),
             reads=(b_.buf,), writes=(p1_v[pp].buf,))

    def p1_sigmoid_batch(pp):
        sig_ap, sig_buf = p1_sig[pp]
        for h in range(8):
            k.op(ACT, lambda e: e.activation(out=sig_ap(h), in_=sig_ap(h), func=AF.Sigmoid),
                 reads=(sig_buf(h),), writes=(sig_buf(h),))

    p1t = {}

    def p1_prep_A(pp, h):
        NT = P1_NT
        sig_ap, sig_buf = p1_sig[pp]
        Lg, KK, B_, BL = [tmp() for _ in range(4)]
        p1t[h] = (KK, B_, BL)
        k.op(ACT, lambda e: e.activation(out=Lg.ap[:, 0:NT], in_=sig_ap(h), func=AF.Ln,
                                         bias=lbt.ap[:, h:h + 1], scale=omlt.ap[:, h:h + 1]),
             reads=(sig_buf(h), lbt.buf, omlt.buf), writes=(Lg.buf,))
        k.op(ACT, lambda e: e.activation(out=KK.ap[:, 0:NT], in_=sig_ap(h), func=AF.Identity,
                                         bias=omlt.ap[:, h:h + 1], scale=nomlt.ap[:, h:h + 1]),
             reads=(sig_buf(h), nomlt.buf, omlt.buf), writes=(KK.buf,))
        k.op(DVE, lambda e: e.tensor_tensor_scan(out=B_.ap[:, 0:NT], data0=onec.ap[:, 0:1].to_broadcast([128, NT]),
                                                 data1=Lg.ap[:, 0:NT], initial=0.0, op0=ALU.mult, op1=ALU.add),
             reads=(onec.buf, Lg.buf), writes=(B_.buf,))
        k.op(DVE, lambda e: e.tensor_scalar(out=BL.ap[:, 0:NT], in0=B_.ap[:, 0:NT], scalar1=-1.0,
                                            scalar2=B_.ap[:, NT - 1:NT], op0=ALU.mult, op1=ALU.add),
             reads=(B_.buf,), writes=(BL.buf,))

    def p1_prep_B(pp, h):
        NT = P1_NT
        kdT3 = bigD.ap[:, 0:8 * NT].rearrange("p (k t) -> p k t", t=NT)
        dcy3 = dcy.ap.rearrange("p (h c) -> p h c", c=8)
        KK, B_, BL = p1t.pop(h)
        k.op(ACT, lambda e: e.activation(out=BL.ap[:, 0:NT], in_=BL.ap[:, 0:NT], func=AF.Exp), reads=(BL.buf,), writes=(BL.buf,))
        k.op(ACT, lambda e: e.activation(out=dcy3[:, h, 0:1], in_=B_.ap[:, NT - 1:NT], func=AF.Exp),
             reads=(B_.buf,), writes=(dcy.buf,))
        k.op(DVE, lambda e: e.tensor_tensor(out=kdT3[:, h, :], in0=KK.ap[:, 0:NT], in1=BL.ap[:, 0:NT], op=ALU.mult),
             reads=(KK.buf, BL.buf), writes=(bigD.buf,))

    def p1_state(pp):
        NT, NB = P1_NT, P1_NB
        kdT3 = bigD.ap[:, 0:8 * NT].rearrange("p (k t) -> p k t", t=NT)
        kd3 = kd_tm.ap[:, 0:NB * 1024].rearrange("p (b c) -> p b c", c=1024)
        vt3 = p1_v[pp].ap[:, 0:NB * 1024].rearrange("p (b c) -> p b c", c=1024)
        dcy3 = dcy.ap.rearrange("p (h c) -> p h c", c=8)
        for blk in range(NB):
            bk = bank()
            pv = bk.ap.bitcast(BF16)
            k.pe_group([(lambda e, h=h: e.transpose(out=pv[:, h * 128:(h + 1) * 128],
                                                    in_=kdT3[:, h, blk * 128:(blk + 1) * 128], identity=ident_bf.ap))
                        for h in range(8)], reads=(bigD.buf, ident_bf.buf), writes=(bk.buf,))
            k.op(DVE, lambda e: e.tensor_copy(out=kd3[:, blk, :], in_=pv), reads=(bk.buf,), writes=(kd_tm.buf,))
        pb = [bank(), bank()]
        fns = []
        for h in range(8):
            for blk in range(NB):
                fns.append(lambda e, h=h, blk=blk: e.matmul(pb[h // 4].ap[:, (h % 4) * 128:(h % 4 + 1) * 128],
                                                            lhsT=kd3[:, blk, h * 128:(h + 1) * 128],
                                                            rhs=vt3[:, blk, h * 128:(h + 1) * 128],
                                                            start=(blk == 0), stop=(blk == NB - 1)))
        k.pe_group(fns, reads=(kd_tm.buf, p1_v[pp].buf), writes=(pb[0].buf, pb[1].buf))
        S_f = Sf[0]
        for h in range(8):
            k.op(DVE, lambda e: e.scalar_tensor_tensor(out=S_f.ap[:, h * 128:(h + 1) * 128],
                                                       in0=S_f.ap[:, h * 128:(h + 1) * 128], scalar=dcy3[:, h, 0:1],
                                                       in1=pb[h // 4].ap[:, (h % 4) * 128:(h % 4 + 1) * 128],
                                                       op0=ALU.mult, op1=ALU.add),
                 reads=(S_f.buf, dcy.buf, pb[h // 4].buf), writes=(S_f.buf,))

    def phase1_all():
        ivg = [(g, blk) for g in range(2) for blk in range(P1_NB)]
        s1(xp, 0, P1_NB, P1_NT)
        for h in range(8):
            p1_f_head(0, h)
            p1_iv_group(0, *ivg[h])
        for t_i in range(NPT):
            pp = t_i % 2
            more = t_i + 1 < NPT
            if more:
                s1(xp, (t_i + 1) * 512, P1_NB, P1_NT)
            p1_sigmoid_batch(pp)
            p1_prep_A(pp, 0)
            for h in range(8):
                if more:
                    p1_f_head(1 - pp, h)
                    p1_iv_group(1 - pp, *ivg[h])
                if h + 1 < 8:
                    p1_prep_A(pp, h + 1)
                p1_prep_B(pp, h)
            p1_state(pp)

    def exchange_and_halo():
        cc_in = nc.dram_tensor("cc_in", [128, 1024], F32).ap()
        cc_out = nc.dram_tensor("cc_out", [256, 1024], F32).ap()
        ccin_buf, ccout_buf = Buf("ccin"), Buf("ccout")
        ex_ch = k.chan("ex_ch")
        ex_ch2 = k.chan("ex_ch2")
        ex_ch3 = k.chan("ex_ch3")
        cc_sem = k.sem("cc_sem")
        k.dma(SP, lambda e: e.dma_start(out=sel_t.ap, in_=selm), ex_ch3, writes=(sel_t.buf,))
        k.dma(POOL, lambda e: e.dma_start(out=cc_in, in_=Sf[0].ap), ex_ch, reads=(Sf[0].buf,), writes=(ccin_buf,))
        POOL.wait_for(k._deps((ccin_buf,), (ccout_buf,)))
        ins = nc.gpsimd.collective_compute("AllGather", ALU.bypass, replica_groups=[[0, 4], [1, 5], [2, 6], [3, 7]],
                                           ins=[cc_in.opt()], outs=[cc_out.opt()])
        ins.then_inc(cc_sem)
        k._mark(("cc", cc_sem, 1), (ccin_buf,), (ccout_buf,))
        def receive():
            k.dma(POOL, lambda e: e.dma_start(out=Sf[1].ap, in_=cc_out[0:128, :]), ex_ch2, reads=(ccout_buf,), writes=(Sf[1].buf,))
            k.op(DVE, lambda e: e.tensor_scalar(out=Sf[0].ap, in0=Sf[1].ap, scalar1=sel_t.ap[:, 0:1], scalar2=None, op0=ALU.mult),
                 reads=(Sf[1].buf, sel_t.buf), writes=(Sf[0].buf,))
            k.op(ACT, lambda e: e.copy(out=Sb[0].ap, in_=Sf[0].ap), reads=(Sf[0].buf,), writes=(Sb[0].buf,))

        pre["recv"] = receive
        state["S"] = 0
        state["B"] = 0
        s1(xhalo, 0, 1, 128)
        hT3 = hT.ap[:, 0:8 * 128].rearrange("p (k t) -> p k t", t=128)
        uh3 = uh.ap.rearrange("p (j r) -> p j r", r=2)
        for s_ in range(2):
            sv, sc_ = wfetch("vA%d" % s_), wfetch("cA%d" % s_)
            for jj in range(4):
                j = s_ * 4 + jj
                bv = proj_fm(sv, jj, 128, hT, hT3)
                bc = proj_fm(sc_, jj, 128, hT, hT3)
                vs = small()
                k.op(ACT, lambda e: e.copy(out=vs.ap[:, 0:2], in_=bv.ap[:, 0:2]), reads=(bv.buf,), writes=(vs.buf,))
                k.op(DVE, lambda e: e.tensor_tensor(out=uh3[:, j, :], in0=vs.ap[:, 0:2], in1=bc.ap[:, 0:2], op=ALU.mult),
                     reads=(vs.buf, bc.buf), writes=(uh.buf,))


    def _program():
        stage(0)
        if EXCH:
            phase1_all()
            exchange_and_halo()
        for t_i in range(NPT):
            tile(xp, pp, yp, t_i * 512, 512, sample=False, nxt=((xp, (t_i + 1) * 512, 512, True) if t_i + 1 < NPT else (xs, 0, NSEQ_S * 64, False)))
        si = state["S"]
        k.dma(SP, lambda e: e.dma_start(out=hgrn_p.rearrange("h k v -> k h v"),
                                        in_=Sf[si].ap.rearrange("p (h v) -> p h v", v=128)),
              misc_ch, reads=(Sf[si].buf,))
        uh3 = uh.ap.rearrange("p (j r) -> p j r", r=2)
        for j in range(8):
            bj = bank()
            k.pe_group([lambda e: e.matmul(bj.ap[0:2, 0:128], lhsT=uh3[:, j, :], rhs=identf.ap, start=True, stop=True)],
                       reads=(uh.buf, identf.buf), writes=(bj.buf,))
            k.op(DVE, lambda e: e.tensor_copy(out=cv_out.ap[0:2, j * 128:(j + 1) * 128], in_=bj.ap[0:2, 0:128]),
                 reads=(bj.buf,), writes=(cv_out.buf,))
        k.dma(SP, lambda e: e.dma_start(out=conv_p, in_=cv_out.ap[0:2, :]), misc_ch, reads=(cv_out.buf,))
        stage(12)

        k.dma(SP, lambda e: e.dma_start(out=sc_in.ap, in_=sconv), misc_ch, writes=(sc_in.buf,))
        uhs3 = uhs.ap.rearrange("p (j m) -> p j m", m=8)
        for j in range(8):
            bj = bank()
            k.pe_group([lambda e: e.matmul(bj.ap[:, 0:8], lhsT=sc_in.ap[:, j * 128:(j + 1) * 128], rhs=identf.ap[0:8, 0:8],
                                           start=True, stop=True)], reads=(sc_in.buf, identf.buf), writes=(bj.buf,))
            k.op(DVE, lambda e: e.tensor_copy(out=uhs3[:, j, :], in_=bj.ap[:, 0:8]), reads=(bj.buf,), writes=(uhs.buf,))
        stage(13)
        tile(xs, ps_, ys, 0, NSEQ_S * 64, sample=True)
        uo3 = uo.ap.rearrange("p (j m) -> p j m", m=8)
        for j in range(8):
            bj = bank()
            k.pe_group([lambda e: e.matmul(bj.ap[0:8, 0:128], lhsT=uo3[:, j, :], rhs=identf.ap, start=True, stop=True)],
                       reads=(uo.buf, identf.buf, cv_out.buf), writes=(bj.buf,))
            k.op(DVE, lambda e: e.tensor_copy(out=cv_out.ap[0:8, j * 128:(j + 1) * 128], in_=bj.ap[0:8, 0:128]),
                 reads=(bj.buf,), writes=(cv_out.buf,))
        k.dma(SP, lambda e: e.dma_start(out=conv_s, in_=cv_out.ap[0:8, :]), misc_ch, reads=(cv_out.buf,))


    try:
        _program()
    except StopBuild:
        pass
    dump_names = [n for n in os.environ.get("KDUMP", "").split(",") if n]
    reg = dict(hT=hT, bigA=bigA, bigB=bigB, bigC=bigC, bigD=bigD, oT=oT, dcy=dcy, kd_tm=kd_tm, v_tm=v_tm,
               Sf0=Sf[0], uh=uh, gpre32=gpre32, lbt=lbt, cw=cw, mask2=mask2, gon=gon)
    dbg_ch = k.chan("dbg")
    for n in dump_names:
        t_ = reg[n]
        shp = list(t_.ap.shape)
        dd = nc.dram_tensor("dbg_" + n, shp, t_.ap.dtype, kind="ExternalOutput").ap()
        allb = [b for b in [t_.buf]]
        k.dma(SP, lambda e, dd=dd, t_=t_: e.dma_start(out=dd, in_=t_.ap), dbg_ch, reads=tuple(allb))
    if dbg_ch.val:
        nc.sync.wait_ge(dbg_ch.sem, dbg_ch.val)
    finals = [(c.sem, c.val) for c in y_ch + s_st_ch + [misc_ch] + conv_ch + wchan + xin_ch + pt_ch + s_ld_ch if c.val > 0]
    for sem, val in finals:
        nc.sync.wait_ge(sem, val)
    for E in (PE, ACT, DVE, POOL):
        if E.cnt:
            nc.sync.wait_ge(E.sem, E.cnt)
    es.close()
    return nc


_CACHE = {}
_LAST = None


def _get_nc(NPT):
    if NPT not in _CACHE:
        _CACHE[NPT] = build(NPT)
    return _CACHE[NPT]


def kernel(x_prompt, x_sample, state_conv, state_hgrn, p_prompt, p_sample, w_in, conv_w, lb_raw,
           g_pre, g_onorm, w_a_out, w_b_out, w_o, g_post, g_ple, w_ple_gate, w_ple_proj, _npt=None):
    f = lambda a: np.ascontiguousarray(np.asarray(a, dtype=np.float32))
    B, S = x_prompt.shape[0], x_prompt.shape[1]
    NPT = (S // 2) // 512 if _npt is None else _npt
    ntok = NPT * 512
    nc = _get_nc(NPT)
    common = dict(w_in=f(w_in[0]), conv_w=f(conv_w[0]), lb_raw=f(lb_raw), g_pre=f(g_pre[0]), g_onorm=f(g_onorm[0]),
                  w_a_out=f(w_a_out[0]), w_b_out=f(w_b_out[0]), w_o=f(w_o[0]), g_post=f(g_post[0]), g_ple=f(g_ple[0]),
                  w_ple_gate=f(w_ple_gate[0]), w_ple_proj=f(w_ple_proj[0]))
    in_maps = []
    for c in range(N_CORES):
        m = dict(common)
        b, half = c % B, c // B
        t0 = half * ntok
        m["xp"] = f(x_prompt[b, t0:t0 + ntok])
        m["pp"] = f(p_prompt[0, b, t0:t0 + ntok])
        xh = np.zeros((128, D), np.float32)
        if half == 1:
            xh[0:2] = x_prompt[b, t0 - 2:t0]
        m["xhalo"] = xh
        m["selm"] = np.full((128, 1), float(half), np.float32)
        s0 = c * NSEQ_S
        m["xs"] = f(x_sample[s0:s0 + NSEQ_S]).reshape(NSEQ_S * 64, D)
        m["ps"] = f(p_sample[0, s0:s0 + NSEQ_S]).reshape(NSEQ_S * 64, DPLE)
        m["sconv"] = f(state_conv[0, s0:s0 + NSEQ_S]).reshape(NSEQ_S * 2, D)
        m["shgrn"] = f(state_hgrn[0, s0:s0 + NSEQ_S])
        in_maps.append(m)
    res = run_bass_kernel_spmd(nc, in_maps, core_ids=list(range(N_CORES)))
    r = res.results
    global _LAST
    _LAST = r
    y_prompt = np.stack([np.concatenate([r[b]["yp"], r[b + B]["yp"]], 0) for b in range(B)], 0)
    y_sample = np.concatenate([r[c]["ys"].reshape(NSEQ_S, 64, D) for c in range(N_CORES)], 0)
    ncp = np.stack([r[b + B]["conv_p"] for b in range(B)], 0)[None]
    nhp = np.stack([r[b + B]["hgrn_p"] for b in range(B)], 0)[None]
    ncs = np.concatenate([r[c]["conv_s"].reshape(NSEQ_S, 2, D) for c in range(N_CORES)], 0)[None]
    nhs = np.concatenate([r[c]["hgrn_s"] for c in range(N_CORES)], 0)[None]
    return (y_prompt.astype(np.float32), y_sample.astype(np.float32), ncp.astype(np.float32),
            nhp.astype(np.float32), ncs.astype(np.float32), nhs.astype(np.float32))
```
